# Optimizing a Trainium2 kernel written in Bass

```python
import math
import jax, jax.numpy as jnp
from jax import lax
import numpy as np

D_MODEL = 1024
BATCH = 2
SEQ = 8192
DEPTH = 1
DEC_BATCH = 8
DEC_SEQ = 8192
PAST_LEN = 128

MIX_WIDTH = D_MODEL
HG_WIDTH = MIX_WIDTH // 2
HG_EXPAND = 128
HG_HEADS = HG_WIDTH // HG_EXPAND
HG_DV = HG_WIDTH // HG_HEADS
HG_CHUNK = 64
AT_WIDTH = MIX_WIDTH - HG_WIDTH
AT_HEAD_DIM = 64
AT_HEADS = AT_WIDTH // AT_HEAD_DIM
AT_KV_HEADS = 2
WINDOW = 128
BLOCK = 128
ROPE_THETA = 10000.0
D_FF = 2816
EPS = 1e-6
N_IN = 5 * HG_WIDTH + AT_WIDTH + 2 * AT_KV_HEADS * AT_HEAD_DIM

kernel_name = "hymba_hgrn2_window_gqa_macaron_encoder"


def rmsnorm(x, g):
    xf = x.astype(jnp.float32)
    y = xf * lax.rsqrt(jnp.mean(xf * xf, axis=-1, keepdims=True) + EPS)
    return (y * g.astype(jnp.float32)).astype(x.dtype)


def swiglu(x, wi, wo):
    gu = x @ wi.astype(x.dtype)
    gate, up = jnp.split(gu, 2, axis=-1)
    return (jax.nn.silu(gate) * up) @ wo.astype(x.dtype)


def gla_scan(q, k, v, logf):
    B, S, H, DK = q.shape
    DV = v.shape[-1]
    C = HG_CHUNK
    N = S // C

    def to_chunks(a):
        return a.reshape(B, N, C, H, a.shape[-1]).transpose(1, 0, 3, 2, 4)

    qc, kc, vc, fc = to_chunks(q), to_chunks(k), to_chunks(v), to_chunks(logf)
    causal = jnp.tril(jnp.ones((C, C), dtype=bool))[:, :, None]

    def step(state, inp):
        qi, ki, vi, fi = inp
        b = jnp.cumsum(fi, axis=2)
        o_inter = jnp.einsum('bhtk,bhkv->bhtv', qi * jnp.exp(b), state)
        diff = b[:, :, :, None, :] - b[:, :, None, :, :]
        decay = jnp.exp(jnp.where(causal, diff, -jnp.inf))
        A = jnp.einsum('bhtk,bhsk,bhtsk->bhts', qi, ki, decay)
        o_intra = jnp.einsum('bhts,bhsv->bhtv', A, vi)
        b_last = b[:, :, -1:, :]
        new_state = jnp.exp(b_last[:, :, 0, :])[..., None] * state + jnp.einsum(
            'bhsk,bhsv->bhkv', ki * jnp.exp(b_last - b), vi)
        return new_state, o_inter + o_intra

    s0 = jnp.zeros((B, H, DK, DV), jnp.float32)
    _, o = lax.scan(step, s0, (qc, kc, vc, fc))
    return o.transpose(1, 0, 3, 2, 4).reshape(B, S, H, DV)


def hgrn2_bidir(hq, hf_fwd, hf_bwd, hi, hg, lb_fwd, lb_bwd, out_norm):
    B, S, _ = hq.shape
    heads = lambda a: a.reshape(B, S, HG_HEADS, -1)
    q = heads(jax.nn.silu(hq.astype(jnp.float32)))
    v = heads(hi.astype(jnp.float32))

    def gates(fpre, lb):
        f = lb + (1.0 - lb) * jax.nn.sigmoid(fpre.astype(jnp.float32))
        return heads(jnp.log(f)), heads(1.0 - f)

    logf_f, k_f = gates(hf_fwd, lb_fwd)
    logf_b, k_b = gates(hf_bwd, lb_bwd)
    o_f = gla_scan(q, k_f, v, logf_f)
    flip = lambda a: jnp.flip(a, axis=1)
    o_b = flip(gla_scan(flip(q), flip(k_b), flip(v), flip(logf_b)))
    o = o_f + o_b
    o = o * lax.rsqrt(jnp.mean(o * o, axis=-1, keepdims=True) + EPS) * out_norm.astype(jnp.float32)
    o = o.reshape(B, S, HG_WIDTH) * jax.nn.silu(hg.astype(jnp.float32))
    return o.astype(hq.dtype)


def rope(x, cos, sin):
    x1, x2 = jnp.split(x, 2, axis=-1)
    return jnp.concatenate([x1 * cos - x2 * sin, x2 * cos + x1 * sin], axis=-1)


def window_gqa(aq, ak, av, q_norm, k_norm, sink):
    B, S, _ = aq.shape
    dt = aq.dtype
    NB = S // BLOCK
    G = AT_HEADS // AT_KV_HEADS
    q = aq.reshape(B, S, AT_HEADS, AT_HEAD_DIM)
    k = ak.reshape(B, S, AT_KV_HEADS, AT_HEAD_DIM)
    v = av.reshape(B, S, AT_KV_HEADS, AT_HEAD_DIM)
    pos = jnp.arange(S, dtype=jnp.float32)
    inv_freq = ROPE_THETA ** (-jnp.arange(0, AT_HEAD_DIM, 2, dtype=jnp.float32) / AT_HEAD_DIM)
    ang = pos[:, None] * inv_freq[None, :]
    cos, sin = jnp.cos(ang)[:, None, :], jnp.sin(ang)[:, None, :]
    q = rope(rmsnorm(q, q_norm).astype(jnp.float32), cos, sin).astype(dt)
    k = rope(rmsnorm(k, k_norm).astype(jnp.float32), cos, sin).astype(dt)

    qb = q.reshape(B, NB, BLOCK, AT_KV_HEADS, G, AT_HEAD_DIM)
    padw = ((0, 0), (BLOCK, BLOCK), (0, 0), (0, 0))

    def band(a):
        ap = jnp.pad(a, padw).reshape(B, NB + 2, BLOCK, AT_KV_HEADS, AT_HEAD_DIM)
        return jnp.concatenate([ap[:, :-2], ap[:, 1:-1], ap[:, 2:]], axis=2)

    kb, vb = band(k), band(v)
    scale = 1.0 / math.sqrt(AT_HEAD_DIM)
    s = jnp.einsum('bnqhgd,bnkhd->bnhgqk', qb, kb, preferred_element_type=jnp.float32) * scale
    kpos = jnp.arange(3 * BLOCK) - BLOCK
    rel = kpos[None, :] - jnp.arange(BLOCK)[:, None]
    abs_k = jnp.arange(NB)[:, None, None] * BLOCK + kpos[None, None, :]
    mask = (jnp.abs(rel) <= WINDOW)[None] & (abs_k >= 0) & (abs_k < S)
    s = jnp.where(mask[None, :, None, None], s, -jnp.inf)
    sink_l = sink.astype(jnp.float32).reshape(AT_KV_HEADS, G)[None, None, :, :, None, None]
    m = jnp.maximum(jnp.max(s, axis=-1, keepdims=True), sink_l)
    p = jnp.exp(s - m)
    p = p / (jnp.sum(p, axis=-1, keepdims=True) + jnp.exp(sink_l - m))
    o = jnp.einsum('bnhgqk,bnkhd->bnqhgd', p.astype(dt), vb)
    return o.reshape(B, S, AT_WIDTH)


def setup_inputs(seed: int = 0) -> dict:
    key = jax.random.key(seed)
    ks = jax.random.split(key, 17)
    nrm = lambda k, shape, s: jax.random.normal(k, shape, jnp.float32) * s
    gain = lambda k, shape: 1.0 + nrm(k, shape, 0.05)
    return {
        "x_prompt": nrm(ks[0], (BATCH, SEQ, D_MODEL), 1.0),
        "x_sample": nrm(ks[1], (DEC_BATCH, DEC_SEQ, D_MODEL), 1.0),
        "ffn1_norm": gain(ks[2], (DEPTH, D_MODEL)),
        "ffn1_wi": nrm(ks[3], (DEPTH, D_MODEL, 2 * D_FF), D_MODEL ** -0.5),
        "ffn1_wo": nrm(ks[4], (DEPTH, D_FF, D_MODEL), D_FF ** -0.5),
        "mix_norm": gain(ks[5], (DEPTH, D_MODEL)),
        "w_in": nrm(ks[6], (DEPTH, D_MODEL, N_IN), D_MODEL ** -0.5),
        "hg_lb_fwd": nrm(ks[7], (DEPTH + 1, HG_WIDTH), 0.5),
        "hg_lb_bwd": nrm(ks[8], (DEPTH + 1, HG_WIDTH), 0.5),
        "hg_out_norm": gain(ks[9], (DEPTH, HG_DV)),
        "q_norm": gain(ks[10], (DEPTH, AT_HEAD_DIM)),
        "k_norm": gain(ks[11], (DEPTH, AT_HEAD_DIM)),
        "attn_sink": nrm(ks[12], (DEPTH, AT_HEADS), 0.5),
        "w_out": nrm(ks[13], (DEPTH, MIX_WIDTH, D_MODEL), MIX_WIDTH ** -0.5),
        "ffn2_norm": gain(ks[14], (DEPTH, D_MODEL)),
        "ffn2_wi": nrm(ks[15], (DEPTH, D_MODEL, 2 * D_FF), D_MODEL ** -0.5),
        "ffn2_wo": nrm(ks[16], (DEPTH, D_FF, D_MODEL), D_FF ** -0.5),
    }


def reference(x_prompt, x_sample, ffn1_norm, ffn1_wi, ffn1_wo, mix_norm, w_in, hg_lb_fwd, hg_lb_bwd,
              hg_out_norm, q_norm, k_norm, attn_sink, w_out, ffn2_norm, ffn2_wi, ffn2_wo):
    lb_f_all = jnp.cumsum(jax.nn.softmax(hg_lb_fwd.astype(jnp.float32), axis=0), axis=0)
    lb_b_all = jnp.cumsum(jax.nn.softmax(hg_lb_bwd.astype(jnp.float32), axis=0), axis=0)
    splits = np.cumsum([HG_WIDTH] * 5 + [AT_WIDTH, AT_KV_HEADS * AT_HEAD_DIM])

    def trunk(x):
        for l in range(DEPTH):
            x = x + 0.5 * swiglu(rmsnorm(x, ffn1_norm[l]), ffn1_wi[l], ffn1_wo[l])
            h = rmsnorm(x, mix_norm[l])
            proj = h @ w_in[l].astype(h.dtype)
            hq, hf_f, hf_b, hi, hg, aq, ak, av = jnp.split(proj, splits, axis=-1)
            o_hg = hgrn2_bidir(hq, hf_f, hf_b, hi, hg, lb_f_all[l], lb_b_all[l], hg_out_norm[l])
            o_at = window_gqa(aq, ak, av, q_norm[l], k_norm[l], attn_sink[l])
            mixed = jnp.concatenate([o_hg, o_at], axis=-1)
            x = x + mixed @ w_out[l].astype(x.dtype)
            x = x + 0.5 * swiglu(rmsnorm(x, ffn2_norm[l]), ffn2_wi[l], ffn2_wo[l])
        return x

    y_prompt = trunk(x_prompt)
    y_sample = trunk(x_sample)
    return (y_prompt, y_sample)
```

```python
import math
from contextlib import ExitStack

import numpy as np
import ml_dtypes
import concourse.bass as bass
import concourse.mybir as mybir
from concourse.bass_utils import run_bass_kernel_spmd

F32 = mybir.dt.float32
BF16 = mybir.dt.bfloat16
AF = mybir.ActivationFunctionType
ALU = mybir.AluOpType
AX = mybir.AxisListType

D = 1024
FF = 2816
KC = D // 128
HC = FF // 128
NIN = 3328
EPS = 1e-6
N_CORES = 8
DEBUG = False
BARRIER_TEST = False
WIDE_MIX = True


class Sem:
    def __init__(self, h, step):
        self.h = h
        self.n = 0
        self.step = step


class Prog:
    ENG = ["sync", "scalar", "vector", "gpsimd", "tensor"]

    def __init__(self, nc):
        self.nc = nc
        self.gstack = ExitStack()
        self.q = {e: [] for e in self.ENG}
        self.waited = {e: {} for e in self.ENG}
        self.esem = {}
        self.all_sems = []
        for e in ["scalar", "vector", "gpsimd", "tensor"]:
            self.esem[e] = self.sem("c_" + e, 1)
        self.final = []
        self.nsem = 0

    def sem(self, name, step):
        sm = Sem(self.gstack.enter_context(self.nc.semaphore(name)), step)
        self.all_sems.append(sm)
        return sm

    def _waits(self, eng, waits):
        out = []
        w = self.waited[eng]
        for t in waits:
            if t is None:
                continue
            s, v = t
            if v <= 0:
                continue
            if w.get(id(s), 0) >= v:
                continue
            w[id(s)] = v
            out.append((s, v))
        return out

    def op(self, eng, name, *args, waits=(), sig=True, **kw):
        fn = (lambda e, name=name, args=args, kw=kw: getattr(e, name)(*args, **kw))
        ws = self._waits(eng, waits)
        s = None
        tk = None
        if sig:
            s = self.esem[eng]
            s.n += 1
            tk = (s, s.n)
        self.q[eng].append((fn, ws, s))
        return tk

    def dma(self, eng, out, in_, waits=(), sem=None):
        ws = self._waits(eng, waits)
        sem.n += 16
        self.q[eng].append((lambda e, o=out, i=in_: e.dma_start(out=o, in_=i), ws, sem))
        return (sem, sem.n)

    def barrier(self, extra=()):
        tks = [(s, s.n) for s in self.esem.values()] + list(extra)
        for e in self.ENG:
            ws = self._waits(e, tks)
            if ws:
                self.q[e].append((None, ws, None))

    def flush(self):
        nc = self.nc
        with nc.Block() as block:
            def run(eng_name):
                def body(e):
                    for fn, ws, s in self.q[eng_name]:
                        for (ws_s, v) in ws:
                            e.wait_ge(ws_s.h, v)
                        if fn is None:
                            continue
                        ins = fn(e)
                        if s is not None:
                            ins.then_inc(s.h, s.step)
                return body
            block.sync(run("sync"))
            block.scalar(run("scalar"))
            block.vector(run("vector"))
            block.gpsimd(run("gpsimd"))
            block.tensor(run("tensor"))
        self.q = {e: [] for e in self.ENG}


def bc(ap, shape_dims):
    return ap


def ffn_phase(P, src, dst, g_dram, wi_dram, wo_dram, NTOK, ident_dram, tag):
    nc = P.nc
    NT = NTOK // 512
    with ExitStack() as st:
        sb = lambda name, shape, dt: st.enter_context(nc.sbuf_tensor(tag + name, shape, dt))
        ps = lambda name, shape, dt: st.enter_context(nc.psum_tensor(tag + name, shape, dt))
        wi = sb("wi", [128, KC, 2 * FF], BF16)
        wo = sb("wo", [128, HC, D], BF16)
        gt = sb("g", [128, D], F32)
        ident = sb("ident", [128, 128], BF16)
        xin = [sb(f"xin{i}", [128, D], F32) for i in range(2)]
        xres = [sb(f"xres{i}", [128, D], F32) for i in range(2)]
        xn = [sb(f"xn{i}", [128, D], BF16) for i in range(4)]
        junk = sb("junk", [128, D], BF16)
        xnT = sb("xnT", [128, KC, 512], BF16)
        hT = sb("hT", [128, HC, 512], BF16)
        sg = [sb(f"sg{i}", [128, 512], BF16) for i in range(2)]
        ss = sb("ss", [128, NT * 4], F32)
        ms = sb("ms", [128, NT * 4], F32)
        rstd = sb("rstd", [128, NT * 4], F32)
        nhalf = sb("nhalf", [128, 1], F32)
        tp = [ps(f"tp{i}", [128, D], BF16) for i in range(2)]
        Gp = [ps(f"G{i}", [128, 512], F32) for i in range(2)]
        Up = [ps(f"U{i}", [128, 512], F32) for i in range(2)]
        Yp = [ps(f"Y{i}", [128, 512], F32) for i in range(2)]

        s_w = P.sem(tag + "w", 16)
        s_xin = [P.sem(tag + f"xin{i}", 16) for i in range(2)]
        s_xres = [P.sem(tag + f"xres{i}", 16) for i in range(2)]
        s_st = [P.sem(tag + f"st{i}", 16) for i in range(2)]

        wtk = []
        wi_v = wi_dram.rearrange("(kc p) n -> p kc n", p=128)
        for kc in range(KC):
            wtk.append(P.dma("gpsimd", wi[:, kc, :], wi_v[:, kc, :], sem=s_w))
        wo_v = wo_dram.rearrange("(j p) n -> p j n", p=128)
        for j in range(0, HC, 2):
            wtk.append(P.dma("gpsimd", wo[:, j:j + 2, :], wo_v[:, j:j + 2, :], sem=s_w))
        wtk.append(P.dma("gpsimd", gt[:], g_dram.partition_broadcast(128), sem=s_w))
        wtk.append(P.dma("gpsimd", ident[:], ident_dram, sem=s_w))
        w_all = wtk[-1]
        t_nh = P.op("gpsimd", "memset", nhalf[:], -0.5)

        src_t = src.rearrange("(n p) d -> n p d", p=128)
        dst_t = dst.rearrange("(n p) d -> n p d", p=128)

        xin_free = [None, None]
        xn_free = [None] * 4
        tp_free = [None, None]
        G_free = [None, None]
        U_free = [None, None]
        sg_free = [None, None]
        Y_free = [None, None]
        xres_free = [None, None]
        xn_ready = {}
        state = {"hT_free": None, "xnT_ready": {}, "last_upgate": None}
        cnt = {"tp": 0, "gu": 0, "y": 0, "sub": 0}

        def norm_sub(t, s):
            idx = t * 4 + s
            sl = idx % 2
            col = slice(idx, idx + 1)
            ld = P.dma("sync", xin[sl][:], src_t[idx], waits=[xin_free[sl]], sem=s_xin[sl])
            a = P.op("scalar", "activation", out=junk[:], in_=xin[sl][:], func=AF.Square,
                     accum_out=ss[:, col], waits=[ld])
            b = P.op("vector", "tensor_scalar", ms[:, col], ss[:, col], 1.0 / D, EPS, ALU.mult, ALU.add,
                     waits=[a])
            c = P.op("gpsimd", "tensor_tensor", rstd[:, col], ms[:, col], nhalf[:], ALU.pow, waits=[b, t_nh])
            d = P.op("vector", "scalar_tensor_tensor", xn[idx % 4][:], xin[sl][:], rstd[:, col], gt[:],
                     ALU.mult, ALU.mult, waits=[c, xn_free[idx % 4], w_all])
            xin_free[sl] = d
            xn_ready[idx] = d

        def transpose_sub(t, s):
            idx = t * 4 + s
            sl = idx % 2
            k = cnt["tp"] % 2
            cnt["tp"] += 1
            last = None
            for kc in range(KC):
                last = P.op("tensor", "transpose", tp[k][:, kc * 128:(kc + 1) * 128],
                            xn[idx % 4][:, kc * 128:(kc + 1) * 128], ident[:],
                            waits=[xn_ready[idx], tp_free[k], w_all], sig=(kc == KC - 1))
            xn_free[idx % 4] = last
            ev = P.op("scalar", "activation", out=xnT[:, :, s * 128:(s + 1) * 128],
                      in_=tp[k][:].rearrange("p (kc t) -> p kc t", kc=KC), func=AF.Copy, waits=[last])
            tp_free[k] = ev
            state["xnT_ready"][t] = ev

        def upgate(t, j):
            k = cnt["gu"] % 2
            cnt["gu"] += 1
            rdy = state["xnT_ready"][t]
            for kc in range(KC):
                P.op("tensor", "matmul", Gp[k][:], wi[:, kc, j * 128:(j + 1) * 128], xnT[:, kc, :],
                     start=(kc == 0), stop=(kc == KC - 1), waits=[rdy, G_free[k], w_all], sig=False)
            gl = None
            for kc in range(KC):
                gl = P.op("tensor", "matmul", Up[k][:], wi[:, kc, FF + j * 128:FF + (j + 1) * 128], xnT[:, kc, :],
                          start=(kc == 0), stop=(kc == KC - 1), waits=[U_free[k]], sig=(kc == KC - 1))
            a = P.op("scalar", "activation", out=sg[k][:], in_=Gp[k][:], func=AF.Silu, waits=[gl, sg_free[k]])
            G_free[k] = a
            m = P.op("vector", "tensor_tensor", hT[:, j, :], sg[k][:], Up[k][:], ALU.mult,
                     waits=[a, state["hT_free"]])
            U_free[k] = m
            sg_free[k] = m
            state["last_h"] = m

        def down_sub(t, s):
            idx = t * 4 + s
            sl = idx % 2
            ld = P.dma("sync", xres[sl][:], src_t[idx], waits=[xres_free[sl]], sem=s_xres[sl])
            r = None
            for half in range(2):
                k = cnt["y"] % 2
                cnt["y"] += 1
                hs = slice(half * 512, (half + 1) * 512)
                last = None
                for j in range(HC):
                    last = P.op("tensor", "matmul", Yp[k][:], hT[:, j, s * 128:(s + 1) * 128], wo[:, j, hs],
                                start=(j == 0), stop=(j == HC - 1),
                                waits=[state["last_h"], Y_free[k]], sig=(j == HC - 1))
                r = P.op("vector", "scalar_tensor_tensor", xres[sl][:, hs], Yp[k][:], 0.5, xres[sl][:, hs],
                         ALU.mult, ALU.add, waits=[last, ld])
                Y_free[k] = r
                state["last_down"] = last
            stt = P.dma("gpsimd", dst_t[idx], xres[sl][:], waits=[r], sem=s_st[sl])
            xres_free[sl] = stt

        for s in range(4):
            norm_sub(0, s)
            transpose_sub(0, s)
        for t in range(NT):
            for j in range(HC):
                upgate(t, j)
                if t + 1 < NT and j in (3, 7, 11, 15):
                    norm_sub(t + 1, (j - 3) // 4)
            if t + 1 < NT:
                for s in range(4):
                    transpose_sub(t + 1, s)
            for s in range(4):
                down_sub(t, s)
            state["hT_free"] = state["last_down"]

        P.barrier(extra=[xres_free[0], xres_free[1]])
        if DEBUG:
            s_dbg = P.sem(tag + "dbg", 16)
            dd = lambda name, shape, dtype: nc.dram_tensor(tag + name, shape, dtype, kind="ExternalOutput").ap()
            tk = [P.dma("sync", dd("dbg_rstd", [128, NT * 4], F32), rstd[:], sem=s_dbg),
                  P.dma("sync", dd("dbg_ss", [128, NT * 4], F32), ss[:], sem=s_dbg),
                  P.dma("sync", dd("dbg_xn", [128, D], BF16), xn[3][:], sem=s_dbg),
                  P.dma("sync", dd("dbg_xnT", [128, KC, 512], BF16), xnT[:], sem=s_dbg),
                  P.dma("sync", dd("dbg_hT", [128, HC, 512], BF16), hT[:], sem=s_dbg),
                  P.dma("sync", dd("dbg_wi", [128, KC, 2 * FF], BF16), wi[:], sem=s_dbg),
                  P.dma("sync", dd("dbg_wo", [128, HC, D], BF16), wo[:], sem=s_dbg),
                  P.dma("sync", dd("dbg_g", [128, D], F32), gt[:], sem=s_dbg)]
            P.barrier(extra=[tk[-1]])
        P.flush()


class T:
    def __init__(self, ap):
        self.ap = ap
        self.wr = None
        self.rds = []


def _deps(P, eng, reads, writes, extra):
    own = P.esem.get(eng)
    waits = list(extra)
    for t in reads:
        waits.append(t.wr)
    for t in writes:
        for tk in t.rds + [t.wr]:
            if tk is not None and tk[0] is own:
                continue
            waits.append(tk)
    return waits


def _reg(tk, reads, writes):
    for t in reads:
        t.rds = [r for r in t.rds if r[0] is not tk[0]] + [tk]
    for t in writes:
        t.wr = tk
        t.rds = []


def OP(P, eng, name, *args, reads=(), writes=(), extra=(), **kw):
    tk = P.op(eng, name, *args, waits=_deps(P, eng, reads, writes, extra), sig=True, **kw)
    _reg(tk, reads, writes)
    return tk


def MMG(P, mms, reads=(), writes=(), extra=()):
    waits = _deps(P, "tensor", reads, writes, extra)
    tk = None
    for i, (name, args, kw) in enumerate(mms):
        tk = P.op("tensor", name, *args, waits=(waits if i == 0 else ()), sig=(i == len(mms) - 1), **kw)
    _reg(tk, reads, writes)
    return tk


def DMA(P, eng, out, in_, sem, reads=(), writes=(), extra=()):
    tk = P.dma(eng, out, in_, waits=_deps(P, eng, reads, writes, extra), sem=sem)
    _reg(tk, reads, writes)
    return tk


def proj_phase(P, x1, A, scr, NSLOT, S):
    nc = P.nc
    NTOK = NSLOT * S
    NT = NTOK // 512
    tag = "b_"
    with ExitStack() as st:
        sb = lambda name, shape, dt: st.enter_context(nc.sbuf_tensor(tag + name, shape, dt))
        ps = lambda name, shape, dt: st.enter_context(nc.psum_tensor(tag + name, shape, dt))
        win = sb("win", [128, KC, NIN], BF16)
        gt = sb("g", [128, D], F32)
        ident = sb("ident", [128, 128], BF16)
        cosT = sb("cos", [128, S // 128, 32], F32)
        sinT = sb("sin", [128, S // 128, 32], F32)
        gqk = sb("gqk", [128, 10, 64], F32)
        lbraw = sb("lbraw", [128, 2, 4, 2], F32)
        lbd = sb("lbd", [128, 8], F32)
        lb = sb("lb", [128, 8], F32)
        oml = sb("oml", [128, 8], F32)
        nhalf = sb("nhalf", [128, 16], F32)
        xin = [T(sb(f"xin{i}", [128, D], F32)) for i in range(2)]
        xn = [T(sb(f"xn{i}", [128, D], BF16)) for i in range(4)]
        xnT_r = [T(sb(f"xnT{i}", [128, KC, 512], BF16)) for i in range(2)]
        stat = [T(sb(f"stat{i}", [128, 4], F32)) for i in range(4)]
        fm = [T(sb(f"fm{i}", [128, 4, 512], F32)) for i in range(2)]
        sgm = [T(sb(f"sgm{i}", [128, 512], F32)) for i in range(2)]
        tm_v = [T(sb(f"tmv{i}", [128, 512], BF16)) for i in range(4)]
        tm_hg = [T(sb(f"tmhg{i}", [128, 512], F32)) for i in range(4)]
        tm_qk = [T(sb(f"tmqk{i}", [128, 5, 128], BF16)) for i in range(4)]
        tm_va = [T(sb(f"tmva{i}", [128, 128], BF16)) for i in range(4)]
        sqt_r = [T(sb(f"sqt{i}", [128, 10, 64], F32)) for i in range(2)]
        aqk_r = [T(sb(f"aqk{i}", [128, 10, 64], F32)) for i in range(2)]
        qst_r = [T(sb(f"qst{i}", [128, 32], F32)) for i in range(2)]
        qn_r = [T(sb(f"qn{i}", [128, 10, 64], F32)) for i in range(2)]
        rt_r = [[T(sb(f"rt{i}_{k}", [128, 10, 32], F32)) for i in range(4)] for k in range(2)]
        qr_r = [T(sb(f"qr{i}", [128, 10, 64], BF16)) for i in range(3)]
        tp = [T(ps(f"tp{i}", [128, D], BF16)) for i in range(2)]
        fps = [T(ps(f"fps{i}", [128, 512], F32)) for i in range(2)]
        tps = [T(ps(f"tps{i}", [128, 512], F32)) for i in range(3)]
        tq = T(ps("tq", [128, 5, 128], BF16))

        s_w = P.sem(tag + "w", 16)
        s_xin = [P.sem(tag + f"xin{i}", 16) for i in range(2)]
        s_fm = [P.sem(tag + f"fm{i}", 16) for i in range(2)]
        s_tm = [P.sem(tag + f"tm{i}", 16) for i in range(4)]
        s_tv = [P.sem(tag + f"tv{i}", 16) for i in range(4)]

        wv = A["w_in"].rearrange("(kc p) n -> p kc n", p=128)
        for kc in range(KC):
            P.dma("gpsimd", win[:, kc, 0:2560], wv[:, kc, 0:2560], sem=s_w)
            for kv in range(2):
                P.dma("gpsimd", win[:, kc, 2560:3072].rearrange("p (g k d) -> p k g d", k=2, d=64)[:, kv],
                      wv[:, kc, 2560 + kv * 256:2560 + (kv + 1) * 256].rearrange("p (g d) -> p g d", d=64), sem=s_w)
            P.dma("gpsimd", win[:, kc, 3072:3328], wv[:, kc, 3072:3328], sem=s_w)
        P.dma("gpsimd", gt[:], A["mix_norm"].partition_broadcast(128), sem=s_w)
        P.dma("gpsimd", ident[:], A["ident"], sem=s_w)
        P.dma("gpsimd", cosT[:], A["cosT"], sem=s_w)
        P.dma("gpsimd", sinT[:], A["sinT"], sem=s_w)
        for i in range(8):
            P.dma("gpsimd", gqk[:, i, :], A["q_norm"].partition_broadcast(128), sem=s_w)
        for i in range(8, 10):
            P.dma("gpsimd", gqk[:, i, :], A["k_norm"].partition_broadcast(128), sem=s_w)
        P.dma("gpsimd", lbraw[:, 0], A["lbf"], sem=s_w)
        w_all = P.dma("gpsimd", lbraw[:, 1], A["lbb"], sem=s_w)
        t_nh = P.op("gpsimd", "memset", nhalf[:], -0.5)
        lbv = lbraw[:].rearrange("p a h r -> p (a h) r")
        t0 = P.op("vector", "tensor_tensor", lbd[:], lbv[:, :, 0], lbv[:, :, 1], ALU.subtract, waits=[w_all])
        t1 = P.op("scalar", "activation", out=lb[:], in_=lbd[:], func=AF.Sigmoid, waits=[t0])
        t2 = P.op("scalar", "activation", out=oml[:], in_=lbd[:], func=AF.Sigmoid, scale=-1.0, waits=[t0])
        consts_ready = [w_all, t_nh, t1, t2]

        src_t = x1.rearrange("(n p) d -> n p d", p=128)

        def norm_sub(idx):
            sl = idx % 2
            stt = stat[idx % 4]
            DMA(P, "sync", xin[sl].ap[:], src_t[idx], s_xin[sl], writes=[xin[sl]])
            OP(P, "scalar", "activation", out=xn[idx % 4].ap[:], in_=xin[sl].ap[:], func=AF.Square,
               accum_out=stt.ap[:, 0:1], reads=[xin[sl]], writes=[stt, xn[idx % 4]])
            OP(P, "vector", "tensor_scalar", stt.ap[:, 1:2], stt.ap[:, 0:1], 1.0 / D, EPS, ALU.mult, ALU.add,
               reads=[stt], writes=[stt])
            OP(P, "gpsimd", "tensor_tensor", stt.ap[:, 2:3], stt.ap[:, 1:2], nhalf[:, 0:1], ALU.pow,
               reads=[stt], writes=[stt], extra=[t_nh])
            OP(P, "vector", "scalar_tensor_tensor", xn[idx % 4].ap[:], xin[sl].ap[:], stt.ap[:, 2:3], gt[:],
               ALU.mult, ALU.mult, reads=[xin[sl], stt], writes=[xn[idx % 4]], extra=[w_all])

        def transpose_sub(idx):
            s = idx % 4
            k = idx % 2
            xnT = xnT_r[(idx // 4) % 2]
            MMG(P, [("transpose", (tp[k].ap[:, kc * 128:(kc + 1) * 128], xn[idx % 4].ap[:, kc * 128:(kc + 1) * 128],
                                   ident[:]), {}) for kc in range(KC)],
                reads=[xn[idx % 4]], writes=[tp[k]], extra=[w_all])
            OP(P, "scalar", "activation", out=xnT.ap[:, :, s * 128:(s + 1) * 128],
               in_=tp[k].ap[:].rearrange("p (kc t) -> p kc t", kc=KC), func=AF.Copy, reads=[tp[k]], writes=[xnT])

        qFv = [scr[n].rearrange("h p n -> p h n") for n in ("qF", "ffF", "fbF")]
        cnt = {"f": 0, "t": 0}

        def fm_group(t, gi):
            xnT = xnT_r[t % 2]
            fs = fm[(t * 3 + gi) % 2]
            for h in range(4):
                c = gi * 4 + h
                k = cnt["f"] % 2
                cnt["f"] += 1
                MMG(P, [("matmul", (fps[k].ap[:], win[:, kc, c * 128:(c + 1) * 128], xnT.ap[:, kc, :]),
                         dict(start=(kc == 0), stop=(kc == KC - 1))) for kc in range(KC)],
                    reads=[xnT], writes=[fps[k]], extra=[w_all])
                if gi == 0:
                    OP(P, "scalar", "activation", out=fs.ap[:, h, :], in_=fps[k].ap[:], func=AF.Silu,
                       reads=[fps[k]], writes=[fs])
                else:
                    OP(P, "scalar", "activation", out=sgm[k].ap[:], in_=fps[k].ap[:], func=AF.Sigmoid,
                       reads=[fps[k]], writes=[sgm[k]])
                    ci = (gi - 1) * 4 + h
                    OP(P, "vector", "tensor_scalar", fs.ap[:, h, :], sgm[k].ap[:], oml[:, ci:ci + 1], lb[:, ci:ci + 1],
                       ALU.mult, ALU.add, reads=[sgm[k]], writes=[fs], extra=consts_ready)
            DMA(P, "gpsimd", qFv[gi][:, :, t * 512:(t + 1) * 512], fs.ap[:], s_fm[(t * 3 + gi) % 2], reads=[fs])

        def tm_proj(t, s):
            idx = t * 4 + s
            sl = idx % 2
            s4 = idx % 4
            xnT = xnT_r[t % 2]
            tok0 = idx * 128
            sqt, aqk = sqt_r[sl], aqk_r[sl]

            def proj(c0, c1):
                k = cnt["t"] % 3
                cnt["t"] += 1
                MMG(P, [("matmul", (tps[k].ap[:, 0:c1 - c0], xnT.ap[:, kc, s * 128:(s + 1) * 128], win[:, kc, c0:c1]),
                         dict(start=(kc == 0), stop=(kc == KC - 1))) for kc in range(KC)],
                    reads=[xnT], writes=[tps[k]], extra=[w_all])
                return tps[k]
            pv = proj(1536, 2048)
            OP(P, "scalar", "activation", out=tm_v[s4].ap[:], in_=pv.ap[:], func=AF.Copy, reads=[pv], writes=[tm_v[s4]])
            pg = proj(2048, 2560)
            OP(P, "scalar", "activation", out=tm_hg[s4].ap[:], in_=pg.ap[:], func=AF.Silu, reads=[pg], writes=[tm_hg[s4]])
            pq = proj(2560, 3072)
            sq_flat = sqt.ap[:].rearrange("p h d -> p (h d)")
            aq_flat = aqk.ap[:].rearrange("p h d -> p (h d)")
            OP(P, "scalar", "activation", out=aq_flat[:, 0:512], in_=pq.ap[:], func=AF.Copy, reads=[pq], writes=[aqk])
            OP(P, "scalar", "activation", out=sq_flat[:, 0:512], in_=pq.ap[:], func=AF.Square, reads=[pq], writes=[sqt])
            pk = proj(3072, 3328)
            OP(P, "scalar", "activation", out=tm_va[s4].ap[:], in_=pk.ap[:, 128:256], func=AF.Copy,
               reads=[pk], writes=[tm_va[s4]])
            OP(P, "scalar", "activation", out=aq_flat[:, 512:640], in_=pk.ap[:, 0:128], func=AF.Copy, reads=[pk], writes=[aqk])
            OP(P, "scalar", "activation", out=sq_flat[:, 512:640], in_=pk.ap[:, 0:128], func=AF.Square,
               reads=[pk], writes=[sqt])
            outs = [tm_v[s4], tm_hg[s4], tm_va[s4]]
            P.dma("sync", scr["vT"][tok0:tok0 + 128, :], tm_v[s4].ap[:], waits=[o.wr for o in outs], sem=s_tv[s4])
            P.dma("sync", scr["hgS"][tok0:tok0 + 128, :], tm_hg[s4].ap[:], sem=s_tv[s4])
            tk = P.dma("sync", scr["vA"][tok0:tok0 + 128, :], tm_va[s4].ap[:], sem=s_tv[s4])
            _reg(tk, outs, [])

        def tm_chain(t, s):
            idx = t * 4 + s
            sl = idx % 2
            blk = (idx * 128 % S) // 128
            tok0 = idx * 128
            sqt, aqk, qst, qn, rt, qr = sqt_r[sl], aqk_r[sl], qst_r[sl], qn_r[sl], rt_r[sl], qr_r[idx % 3]
            OP(P, "vector", "tensor_reduce", qst.ap[:, 0:10], sqt.ap[:], AX.X, ALU.add, reads=[sqt], writes=[qst])
            OP(P, "vector", "tensor_scalar", qst.ap[:, 10:20], qst.ap[:, 0:10], 1.0 / 64, EPS, ALU.mult, ALU.add,
               reads=[qst], writes=[qst])
            OP(P, "gpsimd", "tensor_tensor", qst.ap[:, 20:30], qst.ap[:, 10:20], nhalf[:, 0:10], ALU.pow,
               reads=[qst], writes=[qst], extra=[t_nh])
            OP(P, "vector", "tensor_tensor", qn.ap[:], aqk.ap[:], qst.ap[:, 20:30].unsqueeze(2).broadcast_to([128, 10, 64]),
               ALU.mult, reads=[aqk, qst], writes=[qn])
            OP(P, "gpsimd", "tensor_tensor", qn.ap[:], qn.ap[:], gqk[:], ALU.mult, reads=[qn], writes=[qn], extra=[w_all])
            cb = cosT[:, blk, :].unsqueeze(1).broadcast_to([128, 10, 32])
            sbb = sinT[:, blk, :].unsqueeze(1).broadcast_to([128, 10, 32])
            x1v = qn.ap[:, :, 0:32]
            x2v = qn.ap[:, :, 32:64]
            OP(P, "vector", "tensor_tensor", rt[0].ap[:], x1v, cb, ALU.mult, reads=[qn], writes=[rt[0]])
            OP(P, "gpsimd", "tensor_tensor", rt[1].ap[:], x2v, sbb, ALU.mult, reads=[qn], writes=[rt[1]])
            OP(P, "vector", "tensor_tensor", rt[2].ap[:], x2v, cb, ALU.mult, reads=[qn], writes=[rt[2]])
            OP(P, "gpsimd", "tensor_tensor", rt[3].ap[:], x1v, sbb, ALU.mult, reads=[qn], writes=[rt[3]])
            OP(P, "vector", "tensor_tensor", qr.ap[:, :, 0:32], rt[0].ap[:], rt[1].ap[:], ALU.subtract,
               reads=[rt[0], rt[1]], writes=[qr])
            OP(P, "gpsimd", "tensor_tensor", qr.ap[:, :, 32:64], rt[2].ap[:], rt[3].ap[:], ALU.add,
               reads=[rt[2], rt[3]], writes=[qr])

        def tm_store(t, s):
            idx = t * 4 + s
            sl = idx % 2
            s4 = idx % 4
            tok0 = idx * 128
            qr = qr_r[idx % 3]
            qrf = qr.ap[:].rearrange("p h d -> p (h d)")
            MMG(P, [("transpose", (tq.ap[:, i, :], qrf[:, i * 128:(i + 1) * 128], ident[:]), {}) for i in range(5)],
                reads=[qr], writes=[tq], extra=[w_all])
            OP(P, "scalar", "activation", out=tm_qk[s4].ap[:], in_=tq.ap[:], func=AF.Copy, reads=[tq], writes=[tm_qk[s4]])
            outs = [tm_qk[s4]]
            P.dma("sync", scr["qTs"][:, idx, :], tm_qk[s4].ap[:, 0:4, :].rearrange("p g t -> p (g t)"),
                  waits=[tm_qk[s4].wr], sem=s_tm[s4])
            tk = P.dma("sync", scr["kTs"][:, tok0:tok0 + 128], tm_qk[s4].ap[:, 4, :], sem=s_tm[s4])
            _reg(tk, outs, [])

        for s in range(4):
            norm_sub(s)
        for s in range(4):
            transpose_sub(s)
        pend = []

        def flush_store(keep):
            while len(pend) > keep:
                tm_store(*pend.pop(0))

        for t in range(NT):
            nxt = [(t + 1) * 4 + k for k in range(4)] if t + 1 < NT else []
            for s in range(4):
                tm_proj(t, s)
                flush_store(2)
                if s < 3:
                    fm_group(t, s)
                tm_chain(t, s)
                pend.append((t, s))
                for n_ in nxt[2 * s:2 * s + 2]:
                    norm_sub(n_)
            for n_ in nxt:
                transpose_sub(n_)
        flush_store(0)

        P.barrier(extra=[(sm, sm.n) for sm in s_fm + s_tm + s_tv])
        P.flush()


def pipeline(n, stages):
    lo = min(sk for sk, _, _ in stages)
    hi = max(sk for sk, _, _ in stages)
    for i in range(-hi, n - lo):
        for part in (0, 1, 2):
            for sk, p, fn in stages:
                if p == part and 0 <= i + sk < n:
                    fn(i + sk)


class Gla:
    RINGS = dict(qF=5, fF=5, v2=4, lf=3, b=4, g=3, d=3, E1=2, E2=2, kk=2, qt=4, kt=3, sc=5, Am=3, kTs=3,
                 eb=2, qh=4, bp=2)

    RINGS_WIDE = dict(qF=7, fF=7, v2=4, lf=4, b=6, g=2, d=4, E1=2, E2=2, kk=2, qt=5, kt=4, sc=5, Am=3, kTs=3,
                      eb=2, qh=5, bp=4)

    def __init__(self, P, st, tag, A, scr, NSLOT, bwd, items, o_banks=1, u_banks=1, wide=False):
        nc = P.nc
        self.wide = wide
        if wide:
            self.RINGS = dict(self.RINGS_WIDE)
            if bwd:
                self.RINGS.update(qF=6, fF=6, b=5, bp=3, d=3, lf=3, qt=4)
        self.P, self.scr, self.bwd, self.items = P, scr, bwd, items
        sb = lambda name, shape, dt: st.enter_context(nc.sbuf_tensor(tag + name, shape, dt))
        ps = lambda name, shape, dt: st.enter_context(nc.psum_tensor(tag + name, shape, dt))
        shapes = dict(qF=([128, 512], F32), fF=([128, 512], F32), v2=([128, 512], BF16), lf=([128, 512], F32),
                      b=([128, 512], F32), g=([128, 512], F32), d=([128, 512], F32), E1=([128, 512], F32),
                      E2=([128, 512], F32), kk=([128, 512], F32), qt=([128, 512], BF16), kt=([128, 512], BF16),
                      sc=([128, 4, 8], F32), Am=([128, 4, 64], BF16), kTs=([128, 4, 128], BF16),
                      eb=([128, 512], F32), qh=([128, 512], BF16), bp=([128, 512], F32))
        self.t = {}
        for name, (shape, dt) in shapes.items():
            if name in ("g", "bp") and not bwd:
                continue
            self.t[name] = [T(sb(f"{name}{i}", shape, dt)) for i in range(self.RINGS[name])]
        self.S = [[T(sb(f"S{i}_{k}", [128, 512], F32)) for k in range(2)] for i in range(NSLOT)]
        self.Sph = [0] * NSLOT
        self.Sb = [T(sb(f"Sb{i}", [128, 512], BF16)) for i in range(NSLOT)]
        self.A_ps = T(ps("A", [128, 4, 64], F32))
        self.kT_ps = T(ps("kT", [128, 4, 128], BF16))
        self.A_v = self.A_ps.ap[:]
        self.kT_v = self.kT_ps.ap[:]
        self.o_ps = [T(ps(f"o{i}", [128, 512], F32)) for i in range(o_banks)]
        self.U_ps = [T(ps(f"U{i}", [128, 512], F32)) for i in range(u_banks)]
        self.scanmask = sb("scanmask", [128, 512], F32)
        self.gmask = sb("gmask", [128, 2, 64], F32)
        self.ident = sb("gident", [128, 128], BF16)
        self.s_ld = {k: [P.sem(tag + f"ld{k}{i}", 16) for i in range(self.RINGS[k])] for k in ("fF", "v2")}
        self.s_c = P.sem(tag + "c", 16)
        P.dma("gpsimd", self.scanmask[:], A["scanmask"], sem=self.s_c)
        P.dma("gpsimd", self.gmask[:], A["gmask"], sem=self.s_c)
        self.c_all = P.dma("gpsimd", self.ident[:], A["ident"], sem=self.s_c)
        self.fname = "fbF" if bwd else "ffF"
        self.cnt = 0

    def tl(self, name, j):
        r = self.t[name]
        return r[j % len(r)]

    def reset_state(self, slot):
        OP(self.P, "gpsimd", "memset", self.S[slot][0].ap[:], 0.0, writes=[self.S[slot][0]])
        self.Sph[slot] = 0

    def ld_f(self, j):
        P, scr = self.P, self.scr
        tok0 = self.items[j][2]
        qF, fF = self.tl("qF", j), self.tl("fF", j)
        sem = self.s_ld["fF"][j % self.RINGS["fF"]]
        ex = _deps(P, "sync", [], [qF, fF], [])
        P.dma("sync", fF.ap[:].rearrange("p (h t) -> p h t", h=4),
              scr[self.fname].rearrange("h p n -> p h n")[:, :, tok0:tok0 + 128], waits=ex, sem=sem)
        tk = P.dma("sync", qF.ap[:].rearrange("p (h t) -> p h t", h=4),
                   scr["qF"].rearrange("h p n -> p h n")[:, :, tok0:tok0 + 128], sem=sem)
        _reg(tk, [], [qF, fF])

    def ld_v(self, j):
        tok0 = self.items[j][2]
        v2 = self.tl("v2", j)
        DMA(self.P, "sync", v2.ap[:], self.scr["vT"][tok0:tok0 + 128, :], self.s_ld["v2"][j % self.RINGS["v2"]], writes=[v2])

    @staticmethod
    def v8(t):
        return t.ap[:].rearrange("p (hc t) -> p hc t", t=64)

    def s1a(self, j):
        OP(self.P, "scalar", "activation", out=self.tl("lf", j).ap[:], in_=self.tl("fF", j).ap[:], func=AF.Ln,
           reads=[self.tl("fF", j)], writes=[self.tl("lf", j)])

    def s1b(self, j):
        OP(self.P, "vector", "tensor_tensor_scan", self.tl("b", j).ap[:], self.scanmask[:], self.tl("lf", j).ap[:], 0.0,
           ALU.mult, ALU.add, reads=[self.tl("lf", j)], writes=[self.tl("b", j)], extra=[self.c_all])

    def s2a(self, j):
        if not self.bwd:
            return
        P, v8 = self.P, self.v8
        g, lf, b, bp = self.tl("g", j), self.tl("lf", j), self.tl("b", j), self.tl("bp", j)
        OP(P, "gpsimd", "tensor_tensor", g.ap[:], lf.ap[:], b.ap[:], ALU.subtract, reads=[lf, b], writes=[g])
        OP(P, "gpsimd", "tensor_tensor", v8(bp), v8(g), v8(b)[:, :, 63:64].broadcast_to([128, 8, 64]), ALU.add,
           reads=[b, g], writes=[bp])

    def s2b(self, j):
        v8 = self.v8
        src = self.tl("g", j) if self.bwd else self.tl("b", j)
        d = self.tl("d", j)
        OP(self.P, "vector", "tensor_tensor", v8(d), v8(src), v8(src)[:, :, 32:33].broadcast_to([128, 8, 64]), ALU.subtract,
           reads=[src], writes=[d])

    def s3a(self, j):
        P, v8 = self.P, self.v8
        d, b, sc, E1, E2, kk, fF = (self.tl(k, j) for k in ("d", "b", "sc", "E1", "E2", "kk", "fF"))
        e = 0 if self.bwd else 63
        OP(P, "scalar", "activation", out=E1.ap[:], in_=d.ap[:], func=AF.Exp, reads=[d], writes=[E1])
        OP(P, "scalar", "activation", out=E2.ap[:], in_=d.ap[:], func=AF.Exp, scale=-1.0, reads=[d], writes=[E2])
        OP(P, "scalar", "activation", out=sc.ap[:, 0, :], in_=v8(b)[:, :, 63], func=AF.Exp, reads=[b], writes=[sc])
        ebs = self.tl("bp", j) if self.bwd else b
        OP(P, "scalar", "activation", out=self.tl("eb", j).ap[:], in_=ebs.ap[:], func=AF.Exp, reads=[ebs],
           writes=[self.tl("eb", j)])
        OP(P, "scalar", "activation", out=sc.ap[:, 3, :], in_=v8(d)[:, :, e], func=AF.Exp, reads=[d], writes=[sc])
        OP(P, "scalar", "activation", out=kk.ap[:], in_=fF.ap[:], func=AF.Copy, bias=1.0, scale=-1.0, reads=[fF], writes=[kk])

    def s3b(self, j):
        P = self.P
        qF, E1, E2, kk, qt, kt = (self.tl(k, j) for k in ("qF", "E1", "E2", "kk", "qt", "kt"))
        OP(P, "vector", "tensor_tensor", qt.ap[:], qF.ap[:], E1.ap[:], ALU.mult, reads=[qF, E1], writes=[qt])
        OP(P, "gpsimd", "tensor_tensor", kt.ap[:], kk.ap[:], E2.ap[:], ALU.mult, reads=[kk, E2], writes=[kt])
        eb, qh = self.tl("eb", j), self.tl("qh", j)
        OP(P, "gpsimd", "tensor_tensor", qh.ap[:], qF.ap[:], eb.ap[:], ALU.mult, reads=[qF, eb], writes=[qh])

    def s4a(self, j):
        P = self.P
        qt, kt = self.tl("qt", j), self.tl("kt", j)
        mms = []
        for h in range(4):
            for c in range(2):
                hc = h * 2 + c
                kw = dict(start=True, stop=True)
                if c:
                    kw["tile_position"] = (0, 64)
                mms.append(("matmul", (self.A_v[c * 64:(c + 1) * 64, h, :], kt.ap[:, hc * 64:(hc + 1) * 64],
                                       qt.ap[:, hc * 64:(hc + 1) * 64]), kw))
        MMG(P, mms, reads=[kt, qt], writes=[self.A_ps])
        mms = []
        for h in range(4):
            for c in range(2):
                hc = h * 2 + c
                kw = dict(tile_position=(0, 64)) if c else {}
                mms.append(("transpose", (self.kT_v[c * 64:(c + 1) * 64, h, :], kt.ap[:, hc * 64:(hc + 1) * 64],
                                          self.ident[:]), kw))
        MMG(P, mms, reads=[kt], writes=[self.kT_ps], extra=[self.c_all])

    def s4b(self, j):
        P = self.P
        Am, kTs = self.tl("Am", j), self.tl("kTs", j)
        OP(P, "vector", "tensor_tensor", Am.ap[:], self.A_v,
           self.gmask[:, 1 if self.bwd else 0, :].unsqueeze(1).broadcast_to([128, 4, 64]), ALU.mult,
           reads=[self.A_ps], writes=[Am], extra=[self.c_all])
        OP(P, "scalar", "activation", out=kTs.ap[:], in_=self.kT_v, func=AF.Copy, reads=[self.kT_ps], writes=[kTs])

    def stages(self):
        if self.wide:
            return [(8, 0, self.ld_f), (2, 0, self.ld_v), (7, 0, self.s1a), (5, 0, self.s2a), (3, 0, self.s3a), (1, 0, self.s4a),
                    (1, 2, self.s4b), (3, 2, self.s3b), (5, 2, self.s2b), (7, 2, self.s1b)]
        return [(5, 0, self.ld_f), (2, 0, self.ld_v), (4, 0, self.s1a), (3, 0, self.s2a), (2, 0, self.s3a), (1, 0, self.s4a),
                (1, 2, self.s4b), (2, 2, self.s3b), (3, 2, self.s2b), (4, 2, self.s1b)]

    def chunk(self, j, c, o_ps):
        P = self.P
        slot = self.items[j][0]
        U_ps = self.U_ps[slot % len(self.U_ps)]
        S, Sn, Sb = self.S[slot][self.Sph[slot]], self.S[slot][1 - self.Sph[slot]], self.Sb[slot]
        self.Sph[slot] = 1 - self.Sph[slot]
        sc, qh, Am, kTs, v2 = (self.tl(k, j) for k in ("sc", "qh", "Am", "kTs", "v2"))
        cs = slice(c * 64, (c + 1) * 64)
        v4 = lambda ap: ap.rearrange("p (h d) -> p h d", h=4)
        sv = sc.ap[:, 0, :].rearrange("p (h c) -> p h c", c=2)[:, :, c:c + 1].broadcast_to([128, 4, 128])
        ev = sc.ap[:, 3, :].rearrange("p (h c) -> p h c", c=2)[:, :, c:c + 1].broadcast_to([128, 4, 128])
        OP(P, "scalar", "activation", out=Sb.ap[:], in_=S.ap[:], func=AF.Copy, reads=[S], writes=[Sb])
        MMG(P, [("matmul", (U_ps.ap[:, h * 128:(h + 1) * 128], kTs.ap[cs, h, :], v2.ap[cs, h * 128:(h + 1) * 128]),
                 dict(start=True, stop=True)) for h in range(4)], reads=[kTs, v2], writes=[U_ps])
        mms = []
        for h in range(4):
            hc = h * 2 + c
            hs = slice(h * 128, (h + 1) * 128)
            kw1 = dict(start=True, stop=False)
            kw2 = dict(start=False, stop=True)
            if c:
                kw1["tile_position"] = (64, 64)
                kw2["tile_position"] = (0, 64)
            mms.append(("matmul", (o_ps.ap[cs, hs], Am.ap[cs, h, :], v2.ap[cs, hs]), kw1))
            mms.append(("matmul", (o_ps.ap[cs, hs], qh.ap[:, hc * 64:(hc + 1) * 64], Sb.ap[:, hs]), kw2))
        MMG(P, mms, reads=[Am, v2, qh, Sb], writes=[o_ps])
        for h in range(4):
            hc = h * 2 + c
            hs = slice(h * 128, (h + 1) * 128)
            OP(P, "scalar", "activation", out=Sn.ap[:, hs], in_=S.ap[:, hs], func=AF.Copy, scale=sc.ap[:, 0, hc:hc + 1],
               reads=[S, sc], writes=[Sn])
        for h in range(4):
            hc = h * 2 + c
            hs = slice(h * 128, (h + 1) * 128)
            OP(P, "vector", "scalar_tensor_tensor", Sn.ap[:, hs], U_ps.ap[:, hs], sc.ap[:, 3, hc:hc + 1], Sn.ap[:, hs],
               ALU.mult, ALU.add, reads=[U_ps, sc, Sn], writes=[Sn])


def gla_fwd_phase(P, A, scr, NSLOT, S):
    nc = P.nc
    NB = S // 128
    tag = "c_"
    with ExitStack() as st:
        items = [(slot, n, slot * S + n * 128) for n in range(NB) for slot in range(NSLOT)]
        G = Gla(P, st, tag, A, scr, NSLOT, False, items, o_banks=2, u_banks=2, wide=True)
        ost = [T(st.enter_context(nc.sbuf_tensor(tag + f"ost{i}", [128, 512], F32))) for i in range(3)]
        s_st = [P.sem(tag + f"st{i}", 16) for i in range(3)]
        for slot in range(NSLOT):
            G.reset_state(slot)

        def chunks(j):
            if j % NSLOT != NSLOT - 1:
                return
            grp = list(range(j - NSLOT + 1, j + 1))
            for c in (0, 1):
                for jj in grp:
                    G.chunk(jj, c, G.o_ps[jj % 2])
            for jj in grp:
                o = ost[jj % 3]
                OP(P, "scalar", "activation", out=o.ap[:], in_=G.o_ps[jj % 2].ap[:], func=AF.Copy,
                   reads=[G.o_ps[jj % 2]], writes=[o])
                tok0 = items[jj][2]
                DMA(P, "gpsimd", scr["oF"][tok0:tok0 + 128, :], o.ap[:], s_st[jj % 3], reads=[o])

        pipeline(len(items), G.stages() + [(0, 1, chunks)])
        P.barrier(extra=[(sm, sm.n) for sm in s_st])
        P.flush()


def mix_phase(P, x1, A, scr, NSLOT, S):
    nc = P.nc
    NB = S // 128
    tag = "m_"
    with ExitStack() as st:
        sb = lambda name, shape, dt: st.enter_context(nc.sbuf_tensor(tag + name, shape, dt))
        ps = lambda name, shape, dt: st.enter_context(nc.psum_tensor(tag + name, shape, dt))
        items = [(slot, n, slot * S + n * 128) for n in range(NB - 1, -1, -1) for slot in range(NSLOT)]
        NI = len(items)
        G = Gla(P, st, tag, A, scr, NSLOT, True, items, o_banks=2, u_banks=1, wide=WIDE_MIX)
        ident = G.ident
        ring = lambda name, n, shape, dt: [T(sb(f"{name}{i}", shape, dt)) for i in range(n)]
        wout = sb("wout", [128, KC, D], BF16)
        gnorm = sb("gnorm", [128, 128], F32)
        sink = sb("sink", [128, 8], F32)
        esink = sb("esink", [128, 8], F32)
        amask = sb("amask", [128, 2, 512], BF16)
        nhalf = sb("nhalf", [128, 4], F32)
        vaug = sb("vaug", [128, NSLOT * 4, 2, 65], BF16)
        kTr = sb("kTr", [128, NSLOT * 4, 256], BF16)
        vr = sb("vr", [128, NSLOT * 4, 256], BF16)
        kvt = [[T(None) for j in range(4)] for _ in range(NSLOT)]
        qTb = ring("qTb", 4, [128, 512], BF16)
        ofw = ring("ofw", 4, [128, 512], F32)
        hgw = ring("hgw", 4, [128, 512], F32)
        x1t = ring("x1t", 3, [128, D], F32)
        Pt = ring("Pt", 4, [128, 512], BF16)
        ot = ring("ot", 5, [128, 512], F32)
        sqo = ring("sqo", 2, [128, 512], F32)
        ostat = ring("ostat", 4, [128, 12], F32)
        on = ring("on", 3, [128, 512], F32)
        mixg = ring("mixg", 3, [128, 512], BF16)
        mixa = ring("mixa", 8, [128, 512], BF16)
        den = ring("den", 2, [128, 16], F32)
        mixT = ring("mixT", 2, [128, KC, 128], BF16)
        S_ps = [T(ps(f"Sa{i}", [128, 512], F32)) for i in range(2)]
        O_bank = ps("O", [128, 512], F32)
        O_ps = T(O_bank[:, 0:260].rearrange("p (g d) -> p g d", g=4))
        TY = O_ps
        tpm = O_bank[:].bitcast(BF16).rearrange("p (kc t) -> p kc t", kc=KC)

        s_w = P.sem(tag + "w", 16)
        s_q = [P.sem(tag + f"q{i}", 16) for i in range(4)]
        s_o = [P.sem(tag + f"o{i}", 16) for i in range(4)]
        s_h = [P.sem(tag + f"h{i}", 16) for i in range(4)]
        s_x = [P.sem(tag + f"x{i}", 16) for i in range(3)]
        s_kv = [[P.sem(tag + f"kv{sl}_{j}", 16) for j in range(4)] for sl in range(NSLOT)]
        s_st = [P.sem(tag + f"st{i}", 16) for i in range(4)]

        wv = A["w_out"].rearrange("(kc p) n -> p kc n", p=128)
        for kc in range(KC):
            P.dma("gpsimd", wout[:, kc, :], wv[:, kc, :], sem=s_w)
        P.dma("gpsimd", gnorm[:], A["hg_out_norm"].partition_broadcast(128), sem=s_w)
        P.dma("gpsimd", amask[:], A["amask"], sem=s_w)
        w_all = P.dma("gpsimd", sink[:], A["attn_sink"].partition_broadcast(128), sem=s_w)
        t_nh = P.op("gpsimd", "memset", nhalf[:], -0.5)
        t_va = P.op("gpsimd", "memset", vaug[:], 1.0)
        t_es = P.op("scalar", "activation", out=esink[:], in_=sink[:], func=AF.Exp, waits=[w_all])
        cready = [w_all, t_nh, t_va, t_es, G.c_all]

        def load_kv(slot, n):
            jj = n % 4
            t = kvt[slot][jj]
            tok0 = slot * S + n * 128
            ex = _deps(P, "sync", [], [t], [t_va])
            P.dma("sync", kTr[:, slot * 4 + jj, 0:128], scr["kTs"][:, tok0:tok0 + 128], waits=ex, sem=s_kv[slot][jj])
            tk = P.dma("sync", vr[:, slot * 4 + jj, 0:128], scr["vA"][tok0:tok0 + 128, :], sem=s_kv[slot][jj])
            tk2 = P.op("gpsimd", "tensor_copy", vaug[:, slot * 4 + jj, :, 0:64],
                       vr[:, slot * 4 + jj, 0:128].rearrange("p (k d) -> p k d", k=2), waits=[tk])
            _reg(tk2, [], [t])

        def kvl(j):
            slot, n, _ = items[j]
            if n - 1 >= 0:
                load_kv(slot, n - 1)

        def lq(j):
            slot, n, _ = items[j]
            DMA(P, "sync", qTb[j % 4].ap[:], scr["qTs"][:, slot * NB + n, :], s_q[j % 4], writes=[qTb[j % 4]])

        def lo(j):
            tok0 = items[j][2]
            DMA(P, "sync", ofw[j % 4].ap[:], scr["oF"][tok0:tok0 + 128, :], s_o[j % 4], writes=[ofw[j % 4]])

        def lh(j):
            tok0 = items[j][2]
            DMA(P, "sync", hgw[j % 4].ap[:], scr["hgS"][tok0:tok0 + 128, :], s_h[j % 4], writes=[hgw[j % 4]])

        def lx(j):
            tok0 = items[j][2]
            DMA(P, "sync", x1t[j % 3].ap[:], x1[tok0:tok0 + 128, :], s_x[j % 3], writes=[x1t[j % 3]])

        cnt = {"s": 0, "p": 0}

        def att(j):
            slot, n, _ = items[j]
            q = qTb[j % 4]
            ma = mixa[j % 8]
            dn = den[j % 2]
            kbs = [kb for kb in (n - 1, n, n + 1) if 0 <= kb < NB]
            for kv in range(2):
                ksl = slice(kv * 64, (kv + 1) * 64)
                pts = []
                for kb in kbs:
                    t = kvt[slot][kb % 4]
                    col = slot * 4 + kb % 4
                    sp = S_ps[cnt["s"] % 2]
                    cnt["s"] += 1
                    mms = [("matmul", (sp.ap[:], kTr[ksl, col, 0:128], q.ap[ksl, :]), dict(start=True, stop=(kb == n)))]
                    if kb != n:
                        mms.append(("matmul", (sp.ap[:], ident[:], amask[:, 0 if kb < n else 1, :]),
                                    dict(start=False, stop=True)))
                    MMG(P, mms, reads=[t, q], writes=[sp], extra=cready)
                    pt = Pt[cnt["p"] % 4]
                    cnt["p"] += 1
                    OP(P, "scalar", "activation", out=pt.ap[:], in_=sp.ap[:], func=AF.Exp, scale=0.125,
                       reads=[sp], writes=[pt])
                    pts.append((pt, t, col))
                mms = []
                for g in range(4):
                    for ki, (pt, t, col) in enumerate(pts):
                        mms.append(("matmul", (O_ps.ap[:, g, :], pt.ap[:, g * 128:(g + 1) * 128], vaug[:, col, kv, :]),
                                    dict(start=(ki == 0), stop=(ki == len(pts) - 1))))
                MMG(P, mms, reads=[p[0] for p in pts] + [p[1] for p in pts], writes=[O_ps])
                hs = slice(kv * 4, (kv + 1) * 4)
                rs = slice(8 + kv * 4, 12 + kv * 4)
                OP(P, "vector", "tensor_tensor", dn.ap[:, hs], O_ps.ap[:, :, 64], esink[:, hs], ALU.add,
                   reads=[O_ps], writes=[dn], extra=cready)
                OP(P, "vector", "reciprocal", dn.ap[:, rs], dn.ap[:, hs], reads=[dn], writes=[dn])
                OP(P, "vector", "tensor_tensor", ma.ap[:, kv * 256:(kv + 1) * 256].rearrange("p (g d) -> p g d", d=64),
                   O_ps.ap[:, :, 0:64], dn.ap[:, rs].unsqueeze(2).broadcast_to([128, 4, 64]),
                   ALU.mult, reads=[O_ps, dn], writes=[ma])

        def chunks(j):
            if j % NSLOT != NSLOT - 1:
                return
            grp = list(range(j - NSLOT + 1, j + 1))
            for c in (1, 0):
                for jj in grp:
                    G.chunk(jj, c, G.o_ps[jj % 2])
            for jj in grp:
                OP(P, "vector", "tensor_tensor", ot[jj % 5].ap[:], G.o_ps[jj % 2].ap[:], ofw[jj % 4].ap[:], ALU.add,
                   reads=[G.o_ps[jj % 2], ofw[jj % 4]], writes=[ot[jj % 5]])

        v4 = lambda ap: ap.rearrange("p (h d) -> p h d", h=4)

        def c1a(j):
            OP(P, "gpsimd", "tensor_tensor", sqo[j % 2].ap[:], ot[j % 5].ap[:], ot[j % 5].ap[:], ALU.mult,
               reads=[ot[j % 5]], writes=[sqo[j % 2]])

        def c1b(j):
            os_ = ostat[j % 4]
            OP(P, "vector", "tensor_reduce", os_.ap[:, 0:4], v4(sqo[j % 2].ap[:]), AX.X, ALU.add, reads=[sqo[j % 2]], writes=[os_])
            OP(P, "vector", "tensor_scalar", os_.ap[:, 4:8], os_.ap[:, 0:4], 1.0 / 128, EPS, ALU.mult, ALU.add,
               reads=[os_], writes=[os_])

        def c2a(j):
            os_ = ostat[j % 4]
            OP(P, "gpsimd", "tensor_tensor", os_.ap[:, 8:12], os_.ap[:, 4:8], nhalf[:, 0:4], ALU.pow,
               reads=[os_], writes=[os_], extra=cready)

        def c2b(j):
            os_ = ostat[j % 4]
            OP(P, "vector", "tensor_tensor", v4(on[j % 3].ap[:]), v4(ot[j % 5].ap[:]),
               os_.ap[:, 8:12].unsqueeze(2).broadcast_to([128, 4, 128]), ALU.mult, reads=[ot[j % 5], os_], writes=[on[j % 3]])

        def c3a(j):
            OP(P, "gpsimd", "tensor_tensor", v4(on[j % 3].ap[:]), v4(on[j % 3].ap[:]),
               gnorm[:].unsqueeze(1).broadcast_to([128, 4, 128]), ALU.mult, reads=[on[j % 3]], writes=[on[j % 3]], extra=cready)

        def c3b(j):
            OP(P, "vector", "tensor_tensor", mixg[j % 3].ap[:], on[j % 3].ap[:], hgw[j % 4].ap[:], ALU.mult,
               reads=[on[j % 3], hgw[j % 4]], writes=[mixg[j % 3]])

        def o1a(j):
            mms = []
            for kc in range(4):
                mms.append(("transpose", (tpm[:, kc, :], mixg[j % 3].ap[:, kc * 128:(kc + 1) * 128], ident[:]), {}))
            for kc in range(4, 8):
                mms.append(("transpose", (tpm[:, kc, :], mixa[j % 8].ap[:, (kc - 4) * 128:(kc - 3) * 128], ident[:]), {}))
            MMG(P, mms, reads=[mixg[j % 3], mixa[j % 8]], writes=[TY], extra=cready)

        def o1b(j):
            OP(P, "scalar", "activation", out=mixT[j % 2].ap[:], in_=tpm, func=AF.Copy, reads=[TY], writes=[mixT[j % 2]])

        def o2a(j):
            for half, bank in ((0, S_ps[0]), (1, S_ps[1])):
                hs = slice(half * 512, (half + 1) * 512)
                MMG(P, [("matmul", (bank.ap[:], mixT[j % 2].ap[:, kc, :], wout[:, kc, hs]),
                         dict(start=(kc == 0), stop=(kc == KC - 1))) for kc in range(KC)],
                    reads=[mixT[j % 2]], writes=[bank], extra=cready)

        def o2b(j):
            tok0 = items[j][2]
            xt = x1t[j % 3]
            for half, bank in ((0, S_ps[0]), (1, S_ps[1])):
                hs = slice(half * 512, (half + 1) * 512)
                OP(P, "vector", "tensor_tensor", xt.ap[:, hs], bank.ap[:], xt.ap[:, hs], ALU.add,
                   reads=[bank, xt], writes=[xt])
            DMA(P, "gpsimd", scr["x2"][tok0:tok0 + 128, :], xt.ap[:], s_st[j % 4], reads=[xt])
            if DEBUG:
                DMA(P, "gpsimd", scr["mixdbg"][tok0:tok0 + 128, 0:512], mixg[j % 3].ap[:], s_st[j % 4], reads=[mixg[j % 3]])
                DMA(P, "gpsimd", scr["mixdbg"][tok0:tok0 + 128, 512:1024], mixa[j % 8].ap[:], s_st[j % 4], reads=[mixa[j % 8]])

        for slot in range(NSLOT):
            G.reset_state(slot)
            load_kv(slot, NB - 1)
        sh = NSLOT - 1
        stages = [(-3 - sh, 0, c3a), (-3 - sh, 0, lx), (-2 - sh, 0, c2a), (-1 - sh, 0, c1a), (-1 - sh, 0, lh)] + \
            [s for s in G.stages() if s[1] == 0] + [(3, 0, kvl), (3, 0, lq), (2, 0, lo)] + \
            [(0, 1, chunks), (1, 1, att)] + \
            [(-3 - sh, 2, c3b), (-2 - sh, 2, c2b), (-1 - sh, 2, c1b)] + [s for s in G.stages() if s[1] == 2] + \
            [(-5 - sh, 2, o2a), (-5 - sh, 2, o2b), (-4 - sh, 2, o1a), (-4 - sh, 2, o1b)]
        pipeline(NI, stages)
        P.barrier(extra=[(sm, sm.n) for sm in s_st])
        P.flush()


WEIGHTS = [("ffn1_norm", [D]), ("ffn1_wi", [D, 2 * FF]), ("ffn1_wo", [FF, D]), ("mix_norm", [D]),
           ("w_in", [D, NIN]), ("lbf", [128, 4, 2]), ("lbb", [128, 4, 2]), ("hg_out_norm", [128]),
           ("q_norm", [64]), ("k_norm", [64]), ("attn_sink", [8]), ("w_out", [D, D]),
           ("ffn2_norm", [D]), ("ffn2_wi", [D, 2 * FF]), ("ffn2_wo", [FF, D])]


def build_program(NSLOT, S, phases=("ffn1", "proj", "glaf", "mix", "ffn2"), dbg_out=()):
    NTOK = NSLOT * S
    NBLK = NTOK // 128
    nc = bass.Bass("TRN2", target_bir_lowering=False)
    dt = lambda name, shape, dtype=F32, kind="ExternalInput": nc.dram_tensor(name, shape, dtype, kind=kind).ap()
    A = {"x": dt("x", [NTOK, D])}
    for name, shape in WEIGHTS:
        A[name] = dt(name, shape)
    A["ident"] = dt("ident", [128, 128], BF16)
    A["cosT"] = dt("cosT", [128, S // 128, 32])
    A["sinT"] = dt("sinT", [128, S // 128, 32])
    A["scanmask"] = dt("scanmask", [128, 512])
    A["gmask"] = dt("gmask", [128, 2, 64])
    A["amask"] = dt("amask", [128, 2, 512], BF16)
    y = dt("y", [NTOK, D], kind="ExternalOutput")
    scr = {}
    for name, shape, dtype in [("x1", [NTOK, D], F32), ("qF", [4, 128, NTOK], F32), ("ffF", [4, 128, NTOK], F32),
                               ("fbF", [4, 128, NTOK], F32), ("vT", [NTOK, 512], BF16), ("hgS", [NTOK, 512], F32),
                               ("qTs", [128, NBLK, 512], BF16), ("kTs", [128, NTOK], BF16), ("vA", [NTOK, 128], BF16),
                               ("oF", [NTOK, 512], F32), ("x2", [NTOK, D], F32), ("mixdbg", [NTOK, D], BF16)]:
        scr[name] = dt(name, shape, dtype, kind=("ExternalOutput" if name in dbg_out else "Internal"))

    P = Prog(nc)
    with P.gstack:
        src = A["x"]
        if "ffn1" in phases:
            ffn_phase(P, A["x"], scr["x1"], A["ffn1_norm"], A["ffn1_wi"], A["ffn1_wo"], NTOK, A["ident"], "a_")
            src = scr["x1"]
        if "proj" in phases:
            proj_phase(P, src, A, scr, NSLOT, S)
        if "glaf" in phases:
            gla_fwd_phase(P, A, scr, NSLOT, S)
        if "mix" in phases:
            mix_phase(P, src, A, scr, NSLOT, S)
            src = scr["x2"]
        if "ffn2" in phases:
            ffn_phase(P, src, y, A["ffn2_norm"], A["ffn2_wi"], A["ffn2_wo"], NTOK, A["ident"], "d_")
    return nc


def host_consts(S):
    pos = np.arange(S, dtype=np.float32)
    inv_freq = (10000.0 ** (-np.arange(0, 64, 2, dtype=np.float32) / 64)).astype(np.float32)
    ang = pos[:, None] * inv_freq[None, :]
    cos = np.cos(ang).astype(np.float32).reshape(S // 128, 128, 32).transpose(1, 0, 2)
    sin = np.sin(ang).astype(np.float32).reshape(S // 128, 128, 32).transpose(1, 0, 2)
    scanmask = np.ones((128, 512), np.float32)
    scanmask[:, ::64] = 0.0
    i = np.arange(64)
    gmask = np.stack([(i[:, None] <= i[None, :]), (i[:, None] >= i[None, :])], axis=1).astype(np.float32)
    gmask = np.concatenate([gmask, gmask], axis=0)
    j = np.arange(128)
    prev = np.where(j[:, None] >= j[None, :], 0.0, -30000.0)
    nxt = np.where(j[:, None] <= j[None, :], 0.0, -30000.0)
    amask = np.stack([np.tile(prev, (1, 4)), np.tile(nxt, (1, 4))], axis=1).astype(ml_dtypes.bfloat16)
    return dict(ident=np.eye(128).astype(ml_dtypes.bfloat16), cosT=np.ascontiguousarray(cos),
                sinT=np.ascontiguousarray(sin), scanmask=scanmask, gmask=np.ascontiguousarray(gmask), amask=amask)


_NC_CACHE = {}


def kernel(**inputs):
    S = 8192
    NSLOT = 2
    xp = np.asarray(inputs["x_prompt"], dtype=np.float32)
    xs = np.asarray(inputs["x_sample"], dtype=np.float32)
    seqs = [xp[0], xp[1]] + [xs[i] for i in range(8)]
    assign = [(c, 8 + c if c < 2 else None) for c in range(N_CORES)]
    w = {}
    for name, _shape in WEIGHTS:
        if name in ("lbf", "lbb"):
            src = inputs["hg_lb_fwd" if name == "lbf" else "hg_lb_bwd"]
            w[name] = np.ascontiguousarray(np.asarray(src, np.float32).reshape(2, 4, 128).transpose(2, 1, 0))
        else:
            w[name] = np.ascontiguousarray(np.asarray(inputs[name], np.float32)[0])
    consts = host_consts(S)
    key = (NSLOT, S)
    if key not in _NC_CACHE:
        _NC_CACHE[key] = build_program(NSLOT, S)
    nc = _NC_CACHE[key]
    zeros = np.zeros((S, D), np.float32)
    in_maps = []
    for c in range(N_CORES):
        a, b = assign[c]
        x = np.concatenate([seqs[a], seqs[b] if b is not None else zeros], axis=0)
        m = {"x": np.ascontiguousarray(x)}
        m.update(w)
        m.update(consts)
        in_maps.append(m)
    res = run_bass_kernel_spmd(nc, in_maps, core_ids=list(range(N_CORES)))
    outs = [None] * 10
    for c in range(N_CORES):
        y = np.asarray(res.results[c]["y"])
        a, b = assign[c]
        outs[a] = y[:S]
        if b is not None:
            outs[b] = y[S:]
    y_prompt = np.stack(outs[:2]).astype(np.float32)
    y_sample = np.stack(outs[2:]).astype(np.float32)
    return (y_prompt, y_sample)
```

```python
import math
from contextlib import ExitStack

import numpy as np
import ml_dtypes
import concourse.bass as bass
import concourse.mybir as mybir
from concourse.bass_utils import run_bass_kernel_spmd

F32 = mybir.dt.float32
BF16 = mybir.dt.bfloat16
AF = mybir.ActivationFunctionType
ALU = mybir.AluOpType
AX = mybir.AxisListType

D = 1024
FF = 2816
KC = D // 128
HC = FF // 128
NIN = 3328
EPS = 1e-6
N_CORES = 8
DEBUG = False
BARRIER_TEST = False
WIDE_MIX = True


class Sem:
    def __init__(self, h, step):
        self.h = h
        self.n = 0
        self.step = step


class Prog:
    ENG = ["sync", "scalar", "vector", "gpsimd", "tensor"]

    def __init__(self, nc):
        self.nc = nc
        self.gstack = ExitStack()
        self.q = {e: [] for e in self.ENG}
        self.waited = {e: {} for e in self.ENG}
        self.esem = {}
        self.all_sems = []
        for e in ["scalar", "vector", "gpsimd", "tensor"]:
            self.esem[e] = self.sem("c_" + e, 1)
        self.final = []
        self.nsem = 0

    def sem(self, name, step):
        sm = Sem(self.gstack.enter_context(self.nc.semaphore(name)), step)
        self.all_sems.append(sm)
        return sm

    def _waits(self, eng, waits):
        out = []
        w = self.waited[eng]
        for t in waits:
            if t is None:
                continue
            s, v = t
            if v <= 0:
                continue
            if w.get(id(s), 0) >= v:
                continue
            w[id(s)] = v
            out.append((s, v))
        return out

    def op(self, eng, name, *args, waits=(), sig=True, **kw):
        fn = (lambda e, name=name, args=args, kw=kw: getattr(e, name)(*args, **kw))
        ws = self._waits(eng, waits)
        s = None
        tk = None
        if sig:
            s = self.esem[eng]
            s.n += 1
            tk = (s, s.n)
        self.q[eng].append((fn, ws, s))
        return tk

    def dma(self, eng, out, in_, waits=(), sem=None):
        ws = self._waits(eng, waits)
        sem.n += 16
        self.q[eng].append((lambda e, o=out, i=in_: e.dma_start(out=o, in_=i), ws, sem))
        return (sem, sem.n)

    def barrier(self, extra=()):
        tks = [(s, s.n) for s in self.esem.values()] + list(extra)
        for e in self.ENG:
            ws = self._waits(e, tks)
            if ws:
                self.q[e].append((None, ws, None))

    def flush(self):
        nc = self.nc
        with nc.Block() as block:
            def run(eng_name):
                def body(e):
                    for fn, ws, s in self.q[eng_name]:
                        for (ws_s, v) in ws:
                            e.wait_ge(ws_s.h, v)
                        if fn is None:
                            continue
                        ins = fn(e)
                        if s is not None:
                            ins.then_inc(s.h, s.step)
                return body
            block.sync(run("sync"))
            block.scalar(run("scalar"))
            block.vector(run("vector"))
            block.gpsimd(run("gpsimd"))
            block.tensor(run("tensor"))
        self.q = {e: [] for e in self.ENG}


def bc(ap, shape_dims):
    return ap


def ffn_phase(P, src, dst, g_dram, wi_dram, wo_dram, NTOK, ident_dram, tag):
    nc = P.nc
    NT = NTOK // 512
    with ExitStack() as st:
        sb = lambda name, shape, dt: st.enter_context(nc.sbuf_tensor(tag + name, shape, dt))
        ps = lambda name, shape, dt: st.enter_context(nc.psum_tensor(tag + name, shape, dt))
        wi = sb("wi", [128, KC, 2 * FF], BF16)
        wo = sb("wo", [128, HC, D], BF16)
        gt = sb("g", [128, D], F32)
        ident = sb("ident", [128, 128], BF16)
        xin = [sb(f"xin{i}", [128, D], F32) for i in range(2)]
        xres = [sb(f"xres{i}", [128, D], F32) for i in range(2)]
        xn = [sb(f"xn{i}", [128, D], BF16) for i in range(4)]
        junk = sb("junk", [128, D], BF16)
        xnT = sb("xnT", [128, KC, 512], BF16)
        hT = sb("hT", [128, HC, 512], BF16)
        sg = [sb(f"sg{i}", [128, 512], BF16) for i in range(2)]
        ss = sb("ss", [128, NT * 4], F32)
        ms = sb("ms", [128, NT * 4], F32)
        rstd = sb("rstd", [128, NT * 4], F32)
        nhalf = sb("nhalf", [128, 1], F32)
        tp = [ps(f"tp{i}", [128, D], BF16) for i in range(2)]
        Gp = [ps(f"G{i}", [128, 512], F32) for i in range(2)]
        Up = [ps(f"U{i}", [128, 512], F32) for i in range(2)]
        Yp = [ps(f"Y{i}", [128, 512], F32) for i in range(2)]

        s_w = P.sem(tag + "w", 16)
        s_xin = [P.sem(tag + f"xin{i}", 16) for i in range(2)]
        s_xres = [P.sem(tag + f"xres{i}", 16) for i in range(2)]
        s_st = [P.sem(tag + f"st{i}", 16) for i in range(2)]

        wtk = []
        wi_v = wi_dram.rearrange("(kc p) n -> p kc n", p=128)
        for kc in range(KC):
            wtk.append(P.dma("gpsimd", wi[:, kc, :], wi_v[:, kc, :], sem=s_w))
        wtk.append(P.dma("gpsimd", gt[:], g_dram.partition_broadcast(128), sem=s_w))
        wtk.append(P.dma("gpsimd", ident[:], ident_dram, sem=s_w))
        w_all = wtk[-1]
        s_w2 = P.sem(tag + "w2", 16)
        wo_v = wo_dram.rearrange("(j p) n -> p j n", p=128)
        wo_all = None
        for j in range(0, HC, 2):
            wo_all = P.dma("gpsimd", wo[:, j:j + 2, :], wo_v[:, j:j + 2, :], sem=s_w2)
        t_nh = P.op("gpsimd", "memset", nhalf[:], -0.5)

        src_t = src.rearrange("(n p) d -> n p d", p=128)
        dst_t = dst.rearrange("(n p) d -> n p d", p=128)

        xin_free = [None, None]
        xn_free = [None] * 4
        tp_free = [None, None]
        G_free = [None, None]
        U_free = [None, None]
        sg_free = [None, None]
        Y_free = [None, None]
        xres_free = [None, None]
        xn_ready = {}
        state = {"hT_free": None, "xnT_ready": {}, "last_upgate": None}
        cnt = {"tp": 0, "gu": 0, "y": 0, "sub": 0}

        def norm_sub(t, s):
            idx = t * 4 + s
            sl = idx % 2
            col = slice(idx, idx + 1)
            ld = P.dma("sync", xin[sl][:], src_t[idx], waits=[xin_free[sl]], sem=s_xin[sl])
            a = P.op("scalar", "activation", out=junk[:], in_=xin[sl][:], func=AF.Square,
                     accum_out=ss[:, col], waits=[ld])
            b = P.op("vector", "tensor_scalar", ms[:, col], ss[:, col], 1.0 / D, EPS, ALU.mult, ALU.add,
                     waits=[a])
            c = P.op("gpsimd", "tensor_tensor", rstd[:, col], ms[:, col], nhalf[:], ALU.pow, waits=[b, t_nh])
            d = P.op("vector", "scalar_tensor_tensor", xn[idx % 4][:], xin[sl][:], rstd[:, col], gt[:],
                     ALU.mult, ALU.mult, waits=[c, xn_free[idx % 4], w_all])
            xin_free[sl] = d
            xn_ready[idx] = d

        def transpose_sub(t, s):
            idx = t * 4 + s
            sl = idx % 2
            k = cnt["tp"] % 2
            cnt["tp"] += 1
            last = None
            for kc in range(KC):
                last = P.op("tensor", "transpose", tp[k][:, kc * 128:(kc + 1) * 128],
                            xn[idx % 4][:, kc * 128:(kc + 1) * 128], ident[:],
                            waits=[xn_ready[idx], tp_free[k], w_all], sig=(kc == KC - 1))
            xn_free[idx % 4] = last
            ev = P.op("scalar", "activation", out=xnT[:, :, s * 128:(s + 1) * 128],
                      in_=tp[k][:].rearrange("p (kc t) -> p kc t", kc=KC), func=AF.Copy, waits=[last])
            tp_free[k] = ev
            state["xnT_ready"][t] = ev

        def upgate(t, j):
            k = cnt["gu"] % 2
            cnt["gu"] += 1
            rdy = state["xnT_ready"][t]
            for kc in range(KC):
                P.op("tensor", "matmul", Gp[k][:], wi[:, kc, j * 128:(j + 1) * 128], xnT[:, kc, :],
                     start=(kc == 0), stop=(kc == KC - 1), waits=[rdy, G_free[k], w_all], sig=False)
            gl = None
            for kc in range(KC):
                gl = P.op("tensor", "matmul", Up[k][:], wi[:, kc, FF + j * 128:FF + (j + 1) * 128], xnT[:, kc, :],
                          start=(kc == 0), stop=(kc == KC - 1), waits=[U_free[k]], sig=(kc == KC - 1))
            a = P.op("scalar", "activation", out=sg[k][:], in_=Gp[k][:], func=AF.Silu, waits=[gl, sg_free[k]])
            G_free[k] = a
            m = P.op("vector", "tensor_tensor", hT[:, j, :], sg[k][:], Up[k][:], ALU.mult,
                     waits=[a, state["hT_free"]])
            U_free[k] = m
            sg_free[k] = m
            state["last_h"] = m

        def down_sub(t, s):
            idx = t * 4 + s
            sl = idx % 2
            ld = P.dma("sync", xres[sl][:], src_t[idx], waits=[xres_free[sl]], sem=s_xres[sl])
            r = None
            for half in range(2):
                k = cnt["y"] % 2
                cnt["y"] += 1
                hs = slice(half * 512, (half + 1) * 512)
                last = None
                for j in range(HC):
                    last = P.op("tensor", "matmul", Yp[k][:], hT[:, j, s * 128:(s + 1) * 128], wo[:, j, hs],
                                start=(j == 0), stop=(j == HC - 1),
                                waits=[state["last_h"], Y_free[k], wo_all], sig=(j == HC - 1))
                r = P.op("vector", "scalar_tensor_tensor", xres[sl][:, hs], Yp[k][:], 0.5, xres[sl][:, hs],
                         ALU.mult, ALU.add, waits=[last, ld])
                Y_free[k] = r
                state["last_down"] = last
            stt = P.dma("gpsimd", dst_t[idx], xres[sl][:], waits=[r], sem=s_st[sl])
            xres_free[sl] = stt

        for s in range(4):
            norm_sub(0, s)
            transpose_sub(0, s)
        for t in range(NT):
            for j in range(HC):
                upgate(t, j)
                if t + 1 < NT and j in (3, 7, 11, 15):
                    norm_sub(t + 1, (j - 3) // 4)
            if t + 1 < NT:
                for s in range(4):
                    transpose_sub(t + 1, s)
            for s in range(4):
                down_sub(t, s)
            state["hT_free"] = state["last_down"]

        P.barrier(extra=[xres_free[0], xres_free[1]])
        if DEBUG:
            s_dbg = P.sem(tag + "dbg", 16)
            dd = lambda name, shape, dtype: nc.dram_tensor(tag + name, shape, dtype, kind="ExternalOutput").ap()
            tk = [P.dma("sync", dd("dbg_rstd", [128, NT * 4], F32), rstd[:], sem=s_dbg),
                  P.dma("sync", dd("dbg_ss", [128, NT * 4], F32), ss[:], sem=s_dbg),
                  P.dma("sync", dd("dbg_xn", [128, D], BF16), xn[3][:], sem=s_dbg),
                  P.dma("sync", dd("dbg_xnT", [128, KC, 512], BF16), xnT[:], sem=s_dbg),
                  P.dma("sync", dd("dbg_hT", [128, HC, 512], BF16), hT[:], sem=s_dbg),
                  P.dma("sync", dd("dbg_wi", [128, KC, 2 * FF], BF16), wi[:], sem=s_dbg),
                  P.dma("sync", dd("dbg_wo", [128, HC, D], BF16), wo[:], sem=s_dbg),
                  P.dma("sync", dd("dbg_g", [128, D], F32), gt[:], sem=s_dbg)]
            P.barrier(extra=[tk[-1]])
        P.flush()


class T:
    def __init__(self, ap):
        self.ap = ap
        self.wr = None
        self.rds = []


def _deps(P, eng, reads, writes, extra):
    own = P.esem.get(eng)
    waits = list(extra)
    for t in reads:
        waits.append(t.wr)
    for t in writes:
        for tk in t.rds + [t.wr]:
            if tk is not None and tk[0] is own:
                continue
            waits.append(tk)
    return waits


def _reg(tk, reads, writes):
    for t in reads:
        t.rds = [r for r in t.rds if r[0] is not tk[0]] + [tk]
    for t in writes:
        t.wr = tk
        t.rds = []


def OP(P, eng, name, *args, reads=(), writes=(), extra=(), **kw):
    tk = P.op(eng, name, *args, waits=_deps(P, eng, reads, writes, extra), sig=True, **kw)
    _reg(tk, reads, writes)
    return tk


def MMG(P, mms, reads=(), writes=(), extra=()):
    waits = _deps(P, "tensor", reads, writes, extra)
    tk = None
    for i, (name, args, kw) in enumerate(mms):
        tk = P.op("tensor", name, *args, waits=(waits if i == 0 else ()), sig=(i == len(mms) - 1), **kw)
    _reg(tk, reads, writes)
    return tk


def DMA(P, eng, out, in_, sem, reads=(), writes=(), extra=()):
    tk = P.dma(eng, out, in_, waits=_deps(P, eng, reads, writes, extra), sem=sem)
    _reg(tk, reads, writes)
    return tk


def proj_phase(P, x1, A, scr, NSLOT, S):
    nc = P.nc
    NTOK = NSLOT * S
    NT = NTOK // 512
    tag = "b_"
    with ExitStack() as st:
        sb = lambda name, shape, dt: st.enter_context(nc.sbuf_tensor(tag + name, shape, dt))
        ps = lambda name, shape, dt: st.enter_context(nc.psum_tensor(tag + name, shape, dt))
        win = sb("win", [128, KC, NIN], BF16)
        gt = sb("g", [128, D], F32)
        ident = sb("ident", [128, 128], BF16)
        cosT = sb("cos", [128, S // 128, 32], F32)
        sinT = sb("sin", [128, S // 128, 32], F32)
        gqk = sb("gqk", [128, 10, 64], F32)
        lbraw = sb("lbraw", [128, 2, 4, 2], F32)
        lbd = sb("lbd", [128, 8], F32)
        lb = sb("lb", [128, 8], F32)
        oml = sb("oml", [128, 8], F32)
        nhalf = sb("nhalf", [128, 16], F32)
        xin = [T(sb(f"xin{i}", [128, D], F32)) for i in range(2)]
        xn = [T(sb(f"xn{i}", [128, D], BF16)) for i in range(4)]
        xnT_r = [T(sb(f"xnT{i}", [128, KC, 512], BF16)) for i in range(2)]
        stat = [T(sb(f"stat{i}", [128, 4], F32)) for i in range(4)]
        fm = [T(sb(f"fm{i}", [128, 4, 512], F32)) for i in range(2)]
        sgm = [T(sb(f"sgm{i}", [128, 512], F32)) for i in range(2)]
        tm_v = [T(sb(f"tmv{i}", [128, 512], BF16)) for i in range(4)]
        tm_hg = [T(sb(f"tmhg{i}", [128, 512], F32)) for i in range(4)]
        tm_qk = [T(sb(f"tmqk{i}", [128, 5, 128], BF16)) for i in range(4)]
        tm_va = [T(sb(f"tmva{i}", [128, 128], BF16)) for i in range(4)]
        sqt_r = [T(sb(f"sqt{i}", [128, 10, 64], F32)) for i in range(2)]
        aqk_r = [T(sb(f"aqk{i}", [128, 10, 64], F32)) for i in range(2)]
        qst_r = [T(sb(f"qst{i}", [128, 32], F32)) for i in range(2)]
        qn_r = [T(sb(f"qn{i}", [128, 10, 64], F32)) for i in range(2)]
        rt_r = [[T(sb(f"rt{i}_{k}", [128, 10, 32], F32)) for i in range(4)] for k in range(2)]
        qr_r = [T(sb(f"qr{i}", [128, 10, 64], BF16)) for i in range(3)]
        tp = [T(ps(f"tp{i}", [128, D], BF16)) for i in range(2)]
        fps = [T(ps(f"fps{i}", [128, 512], F32)) for i in range(2)]
        tps = [T(ps(f"tps{i}", [128, 512], F32)) for i in range(3)]
        tq = T(ps("tq", [128, 5, 128], BF16))

        s_w = P.sem(tag + "w", 16)
        s_xin = [P.sem(tag + f"xin{i}", 16) for i in range(2)]
        s_fm = [P.sem(tag + f"fm{i}", 16) for i in range(2)]
        s_tm = [P.sem(tag + f"tm{i}", 16) for i in range(4)]
        s_tv = [P.sem(tag + f"tv{i}", 16) for i in range(4)]

        wv = A["w_in"].rearrange("(kc p) n -> p kc n", p=128)
        for kc in range(KC):
            P.dma("gpsimd", win[:, kc, 0:2560], wv[:, kc, 0:2560], sem=s_w)
            for kv in range(2):
                P.dma("gpsimd", win[:, kc, 2560:3072].rearrange("p (g k d) -> p k g d", k=2, d=64)[:, kv],
                      wv[:, kc, 2560 + kv * 256:2560 + (kv + 1) * 256].rearrange("p (g d) -> p g d", d=64), sem=s_w)
            P.dma("gpsimd", win[:, kc, 3072:3328], wv[:, kc, 3072:3328], sem=s_w)
        P.dma("gpsimd", gt[:], A["mix_norm"].partition_broadcast(128), sem=s_w)
        P.dma("gpsimd", ident[:], A["ident"], sem=s_w)
        P.dma("gpsimd", cosT[:], A["cosT"], sem=s_w)
        P.dma("gpsimd", sinT[:], A["sinT"], sem=s_w)
        for i in range(8):
            P.dma("gpsimd", gqk[:, i, :], A["q_norm"].partition_broadcast(128), sem=s_w)
        for i in range(8, 10):
            P.dma("gpsimd", gqk[:, i, :], A["k_norm"].partition_broadcast(128), sem=s_w)
        P.dma("gpsimd", lbraw[:, 0], A["lbf"], sem=s_w)
        w_all = P.dma("gpsimd", lbraw[:, 1], A["lbb"], sem=s_w)
        t_nh = P.op("gpsimd", "memset", nhalf[:], -0.5)
        lbv = lbraw[:].rearrange("p a h r -> p (a h) r")
        t0 = P.op("vector", "tensor_tensor", lbd[:], lbv[:, :, 0], lbv[:, :, 1], ALU.subtract, waits=[w_all])
        t1 = P.op("scalar", "activation", out=lb[:], in_=lbd[:], func=AF.Sigmoid, waits=[t0])
        t2 = P.op("scalar", "activation", out=oml[:], in_=lbd[:], func=AF.Sigmoid, scale=-1.0, waits=[t0])
        consts_ready = [w_all, t_nh, t1, t2]

        src_t = x1.rearrange("(n p) d -> n p d", p=128)

        def norm_sub(idx):
            sl = idx % 2
            stt = stat[idx % 4]
            DMA(P, "sync", xin[sl].ap[:], src_t[idx], s_xin[sl], writes=[xin[sl]])
            OP(P, "scalar", "activation", out=xn[idx % 4].ap[:], in_=xin[sl].ap[:], func=AF.Square,
               accum_out=stt.ap[:, 0:1], reads=[xin[sl]], writes=[stt, xn[idx % 4]])
            OP(P, "vector", "tensor_scalar", stt.ap[:, 1:2], stt.ap[:, 0:1], 1.0 / D, EPS, ALU.mult, ALU.add,
               reads=[stt], writes=[stt])
            OP(P, "gpsimd", "tensor_tensor", stt.ap[:, 2:3], stt.ap[:, 1:2], nhalf[:, 0:1], ALU.pow,
               reads=[stt], writes=[stt], extra=[t_nh])
            OP(P, "vector", "scalar_tensor_tensor", xn[idx % 4].ap[:], xin[sl].ap[:], stt.ap[:, 2:3], gt[:],
               ALU.mult, ALU.mult, reads=[xin[sl], stt], writes=[xn[idx % 4]], extra=[w_all])

        def transpose_sub(idx):
            s = idx % 4
            k = idx % 2
            xnT = xnT_r[(idx // 4) % 2]
            MMG(P, [("transpose", (tp[k].ap[:, kc * 128:(kc + 1) * 128], xn[idx % 4].ap[:, kc * 128:(kc + 1) * 128],
                                   ident[:]), {}) for kc in range(KC)],
                reads=[xn[idx % 4]], writes=[tp[k]], extra=[w_all])
            OP(P, "scalar", "activation", out=xnT.ap[:, :, s * 128:(s + 1) * 128],
               in_=tp[k].ap[:].rearrange("p (kc t) -> p kc t", kc=KC), func=AF.Copy, reads=[tp[k]], writes=[xnT])

        qFv = [scr[n].rearrange("h p n -> p h n") for n in ("qF", "ffF", "fbF")]
        cnt = {"f": 0, "t": 0}

        def fm_group(t, gi):
            xnT = xnT_r[t % 2]
            fs = fm[(t * 3 + gi) % 2]
            for h in range(4):
                c = gi * 4 + h
                k = cnt["f"] % 2
                cnt["f"] += 1
                MMG(P, [("matmul", (fps[k].ap[:], win[:, kc, c * 128:(c + 1) * 128], xnT.ap[:, kc, :]),
                         dict(start=(kc == 0), stop=(kc == KC - 1))) for kc in range(KC)],
                    reads=[xnT], writes=[fps[k]], extra=[w_all])
                if gi == 0:
                    OP(P, "scalar", "activation", out=fs.ap[:, h, :], in_=fps[k].ap[:], func=AF.Silu,
                       reads=[fps[k]], writes=[fs])
                else:
                    OP(P, "scalar", "activation", out=sgm[k].ap[:], in_=fps[k].ap[:], func=AF.Sigmoid,
                       reads=[fps[k]], writes=[sgm[k]])
                    ci = (gi - 1) * 4 + h
                    OP(P, "vector", "tensor_scalar", fs.ap[:, h, :], sgm[k].ap[:], oml[:, ci:ci + 1], lb[:, ci:ci + 1],
                       ALU.mult, ALU.add, reads=[sgm[k]], writes=[fs], extra=consts_ready)
            DMA(P, "gpsimd", qFv[gi][:, :, t * 512:(t + 1) * 512], fs.ap[:], s_fm[(t * 3 + gi) % 2], reads=[fs])

        def tm_proj(t, s):
            idx = t * 4 + s
            sl = idx % 2
            s4 = idx % 4
            xnT = xnT_r[t % 2]
            tok0 = idx * 128
            sqt, aqk = sqt_r[sl], aqk_r[sl]

            def proj(c0, c1):
                k = cnt["t"] % 3
                cnt["t"] += 1
                MMG(P, [("matmul", (tps[k].ap[:, 0:c1 - c0], xnT.ap[:, kc, s * 128:(s + 1) * 128], win[:, kc, c0:c1]),
                         dict(start=(kc == 0), stop=(kc == KC - 1))) for kc in range(KC)],
                    reads=[xnT], writes=[tps[k]], extra=[w_all])
                return tps[k]
            pv = proj(1536, 2048)
            OP(P, "scalar", "activation", out=tm_v[s4].ap[:], in_=pv.ap[:], func=AF.Copy, reads=[pv], writes=[tm_v[s4]])
            pg = proj(2048, 2560)
            OP(P, "scalar", "activation", out=tm_hg[s4].ap[:], in_=pg.ap[:], func=AF.Silu, reads=[pg], writes=[tm_hg[s4]])
            pq = proj(2560, 3072)
            sq_flat = sqt.ap[:].rearrange("p h d -> p (h d)")
            aq_flat = aqk.ap[:].rearrange("p h d -> p (h d)")
            OP(P, "scalar", "activation", out=aq_flat[:, 0:512], in_=pq.ap[:], func=AF.Copy, reads=[pq], writes=[aqk])
            OP(P, "scalar", "activation", out=sq_flat[:, 0:512], in_=pq.ap[:], func=AF.Square, reads=[pq], writes=[sqt])
            pk = proj(3072, 3328)
            OP(P, "scalar", "activation", out=tm_va[s4].ap[:], in_=pk.ap[:, 128:256], func=AF.Copy,
               reads=[pk], writes=[tm_va[s4]])
            OP(P, "scalar", "activation", out=aq_flat[:, 512:640], in_=pk.ap[:, 0:128], func=AF.Copy, reads=[pk], writes=[aqk])
            OP(P, "scalar", "activation", out=sq_flat[:, 512:640], in_=pk.ap[:, 0:128], func=AF.Square,
               reads=[pk], writes=[sqt])
            outs = [tm_v[s4], tm_hg[s4], tm_va[s4]]
            P.dma("sync", scr["vT"][tok0:tok0 + 128, :], tm_v[s4].ap[:], waits=[o.wr for o in outs], sem=s_tv[s4])
            P.dma("sync", scr["hgS"][tok0:tok0 + 128, :], tm_hg[s4].ap[:], sem=s_tv[s4])
            tk = P.dma("sync", scr["vA"][tok0:tok0 + 128, :], tm_va[s4].ap[:], sem=s_tv[s4])
            _reg(tk, outs, [])

        def tm_chain(t, s):
            idx = t * 4 + s
            sl = idx % 2
            blk = (idx * 128 % S) // 128
            tok0 = idx * 128
            sqt, aqk, qst, qn, rt, qr = sqt_r[sl], aqk_r[sl], qst_r[sl], qn_r[sl], rt_r[sl], qr_r[idx % 3]
            OP(P, "vector", "tensor_reduce", qst.ap[:, 0:10], sqt.ap[:], AX.X, ALU.add, reads=[sqt], writes=[qst])
            OP(P, "vector", "tensor_scalar", qst.ap[:, 10:20], qst.ap[:, 0:10], 1.0 / 64, EPS, ALU.mult, ALU.add,
               reads=[qst], writes=[qst])
            OP(P, "gpsimd", "tensor_tensor", qst.ap[:, 20:30], qst.ap[:, 10:20], nhalf[:, 0:10], ALU.pow,
               reads=[qst], writes=[qst], extra=[t_nh])
            OP(P, "vector", "tensor_tensor", qn.ap[:], aqk.ap[:], qst.ap[:, 20:30].unsqueeze(2).broadcast_to([128, 10, 64]),
               ALU.mult, reads=[aqk, qst], writes=[qn])
            OP(P, "gpsimd", "tensor_tensor", qn.ap[:], qn.ap[:], gqk[:], ALU.mult, reads=[qn], writes=[qn], extra=[w_all])
            cb = cosT[:, blk, :].unsqueeze(1).broadcast_to([128, 10, 32])
            sbb = sinT[:, blk, :].unsqueeze(1).broadcast_to([128, 10, 32])
            x1v = qn.ap[:, :, 0:32]
            x2v = qn.ap[:, :, 32:64]
            OP(P, "vector", "tensor_tensor", rt[0].ap[:], x1v, cb, ALU.mult, reads=[qn], writes=[rt[0]])
            OP(P, "gpsimd", "tensor_tensor", rt[1].ap[:], x2v, sbb, ALU.mult, reads=[qn], writes=[rt[1]])
            OP(P, "vector", "tensor_tensor", rt[2].ap[:], x2v, cb, ALU.mult, reads=[qn], writes=[rt[2]])
            OP(P, "gpsimd", "tensor_tensor", rt[3].ap[:], x1v, sbb, ALU.mult, reads=[qn], writes=[rt[3]])
            OP(P, "vector", "tensor_tensor", qr.ap[:, :, 0:32], rt[0].ap[:], rt[1].ap[:], ALU.subtract,
               reads=[rt[0], rt[1]], writes=[qr])
            OP(P, "gpsimd", "tensor_tensor", qr.ap[:, :, 32:64], rt[2].ap[:], rt[3].ap[:], ALU.add,
               reads=[rt[2], rt[3]], writes=[qr])

        def tm_store(t, s):
            idx = t * 4 + s
            sl = idx % 2
            s4 = idx % 4
            tok0 = idx * 128
            qr = qr_r[idx % 3]
            qrf = qr.ap[:].rearrange("p h d -> p (h d)")
            MMG(P, [("transpose", (tq.ap[:, i, :], qrf[:, i * 128:(i + 1) * 128], ident[:]), {}) for i in range(5)],
                reads=[qr], writes=[tq], extra=[w_all])
            OP(P, "scalar", "activation", out=tm_qk[s4].ap[:], in_=tq.ap[:], func=AF.Copy, reads=[tq], writes=[tm_qk[s4]])
            outs = [tm_qk[s4]]
            P.dma("sync", scr["qTs"][:, idx, :], tm_qk[s4].ap[:, 0:4, :].rearrange("p g t -> p (g t)"),
                  waits=[tm_qk[s4].wr], sem=s_tm[s4])
            tk = P.dma("sync", scr["kTs"][:, tok0:tok0 + 128], tm_qk[s4].ap[:, 4, :], sem=s_tm[s4])
            _reg(tk, outs, [])

        for s in range(4):
            norm_sub(s)
        for s in range(4):
            transpose_sub(s)
        pend = []

        def flush_store(keep):
            while len(pend) > keep:
                tm_store(*pend.pop(0))

        for t in range(NT):
            nxt = [(t + 1) * 4 + k for k in range(4)] if t + 1 < NT else []
            for s in range(4):
                tm_proj(t, s)
                flush_store(2)
                if s < 3:
                    fm_group(t, s)
                tm_chain(t, s)
                pend.append((t, s))
                for n_ in nxt[2 * s:2 * s + 2]:
                    norm_sub(n_)
            for n_ in nxt:
                transpose_sub(n_)
        flush_store(0)

        P.barrier(extra=[(sm, sm.n) for sm in s_fm + s_tm + s_tv])
        P.flush()


def pipeline(n, stages):
    lo = min(sk for sk, _, _ in stages)
    hi = max(sk for sk, _, _ in stages)
    for i in range(-hi, n - lo):
        for part in (0, 1, 2):
            for sk, p, fn in stages:
                if p == part and 0 <= i + sk < n:
                    fn(i + sk)


class Gla:
    RINGS = dict(qF=5, fF=5, v2=4, lf=3, b=4, g=3, d=3, E1=2, E2=2, kk=2, qt=4, kt=3, sc=5, Am=3, kTs=3,
                 eb=2, qh=4, bp=2)

    RINGS_WIDE = dict(qF=7, fF=7, v2=4, lf=4, b=6, g=2, d=4, E1=2, E2=2, kk=2, qt=5, kt=4, sc=5, Am=3, kTs=3,
                      eb=2, qh=5, bp=4)

    def __init__(self, P, st, tag, A, scr, NSLOT, bwd, items, o_banks=1, u_banks=1, wide=False):
        nc = P.nc
        self.wide = wide
        if wide:
            self.RINGS = dict(self.RINGS_WIDE)
            if bwd:
                self.RINGS.update(qF=6, fF=6, b=5, bp=3, d=3, lf=3, qt=4)
        self.P, self.scr, self.bwd, self.items = P, scr, bwd, items
        sb = lambda name, shape, dt: st.enter_context(nc.sbuf_tensor(tag + name, shape, dt))
        ps = lambda name, shape, dt: st.enter_context(nc.psum_tensor(tag + name, shape, dt))
        shapes = dict(qF=([128, 512], F32), fF=([128, 512], F32), v2=([128, 512], BF16), lf=([128, 512], F32),
                      b=([128, 512], F32), g=([128, 512], F32), d=([128, 512], F32), E1=([128, 512], F32),
                      E2=([128, 512], F32), kk=([128, 512], F32), qt=([128, 512], BF16), kt=([128, 512], BF16),
                      sc=([128, 4, 8], F32), Am=([128, 4, 64], BF16), kTs=([128, 4, 128], BF16),
                      eb=([128, 512], F32), qh=([128, 512], BF16), bp=([128, 512], F32))
        self.t = {}
        for name, (shape, dt) in shapes.items():
            if name in ("g", "bp") and not bwd:
                continue
            self.t[name] = [T(sb(f"{name}{i}", shape, dt)) for i in range(self.RINGS[name])]
        self.S = [[T(sb(f"S{i}_{k}", [128, 512], F32)) for k in range(2)] for i in range(NSLOT)]
        self.Sph = [0] * NSLOT
        self.Sb = [T(sb(f"Sb{i}", [128, 512], BF16)) for i in range(NSLOT)]
        self.A_ps = T(ps("A", [128, 4, 64], F32))
        self.kT_ps = T(ps("kT", [128, 4, 128], BF16))
        self.A_v = self.A_ps.ap[:]
        self.kT_v = self.kT_ps.ap[:]
        self.o_ps = [T(ps(f"o{i}", [128, 512], F32)) for i in range(o_banks)]
        self.U_ps = [T(ps(f"U{i}", [128, 512], F32)) for i in range(u_banks)]
        self.scanmask = sb("scanmask", [128, 512], F32)
        self.gmask = sb("gmask", [128, 2, 64], F32)
        self.ident = sb("gident", [128, 128], BF16)
        self.s_ld = {k: [P.sem(tag + f"ld{k}{i}", 16) for i in range(self.RINGS[k])] for k in ("fF", "v2")}
        self.s_c = P.sem(tag + "c", 16)
        P.dma("gpsimd", self.scanmask[:], A["scanmask"], sem=self.s_c)
        P.dma("gpsimd", self.gmask[:], A["gmask"], sem=self.s_c)
        self.c_all = P.dma("gpsimd", self.ident[:], A["ident"], sem=self.s_c)
        self.fname = "fbF" if bwd else "ffF"
        self.cnt = 0

    def tl(self, name, j):
        r = self.t[name]
        return r[j % len(r)]

    def reset_state(self, slot):
        OP(self.P, "gpsimd", "memset", self.S[slot][0].ap[:], 0.0, writes=[self.S[slot][0]])
        self.Sph[slot] = 0

    def ld_f(self, j):
        P, scr = self.P, self.scr
        tok0 = self.items[j][2]
        qF, fF = self.tl("qF", j), self.tl("fF", j)
        sem = self.s_ld["fF"][j % self.RINGS["fF"]]
        ex = _deps(P, "sync", [], [qF, fF], [])
        P.dma("sync", fF.ap[:].rearrange("p (h t) -> p h t", h=4),
              scr[self.fname].rearrange("h p n -> p h n")[:, :, tok0:tok0 + 128], waits=ex, sem=sem)
        tk = P.dma("sync", qF.ap[:].rearrange("p (h t) -> p h t", h=4),
                   scr["qF"].rearrange("h p n -> p h n")[:, :, tok0:tok0 + 128], sem=sem)
        _reg(tk, [], [qF, fF])

    def ld_v(self, j):
        tok0 = self.items[j][2]
        v2 = self.tl("v2", j)
        DMA(self.P, "sync", v2.ap[:], self.scr["vT"][tok0:tok0 + 128, :], self.s_ld["v2"][j % self.RINGS["v2"]], writes=[v2])

    @staticmethod
    def v8(t):
        return t.ap[:].rearrange("p (hc t) -> p hc t", t=64)

    def s1a(self, j):
        OP(self.P, "scalar", "activation", out=self.tl("lf", j).ap[:], in_=self.tl("fF", j).ap[:], func=AF.Ln,
           reads=[self.tl("fF", j)], writes=[self.tl("lf", j)])

    def s1b(self, j):
        OP(self.P, "vector", "tensor_tensor_scan", self.tl("b", j).ap[:], self.scanmask[:], self.tl("lf", j).ap[:], 0.0,
           ALU.mult, ALU.add, reads=[self.tl("lf", j)], writes=[self.tl("b", j)], extra=[self.c_all])

    def s2a(self, j):
        if not self.bwd:
            return
        P, v8 = self.P, self.v8
        g, lf, b, bp = self.tl("g", j), self.tl("lf", j), self.tl("b", j), self.tl("bp", j)
        OP(P, "gpsimd", "tensor_tensor", g.ap[:], lf.ap[:], b.ap[:], ALU.subtract, reads=[lf, b], writes=[g])
        OP(P, "gpsimd", "tensor_tensor", v8(bp), v8(g), v8(b)[:, :, 63:64].broadcast_to([128, 8, 64]), ALU.add,
           reads=[b, g], writes=[bp])

    def s2b(self, j):
        v8 = self.v8
        src = self.tl("g", j) if self.bwd else self.tl("b", j)
        d = self.tl("d", j)
        OP(self.P, "vector", "tensor_tensor", v8(d), v8(src), v8(src)[:, :, 32:33].broadcast_to([128, 8, 64]), ALU.subtract,
           reads=[src], writes=[d])

    def s3a(self, j):
        P, v8 = self.P, self.v8
        d, b, sc, E1, E2, kk, fF = (self.tl(k, j) for k in ("d", "b", "sc", "E1", "E2", "kk", "fF"))
        e = 0 if self.bwd else 63
        OP(P, "scalar", "activation", out=E1.ap[:], in_=d.ap[:], func=AF.Exp, reads=[d], writes=[E1])
        OP(P, "scalar", "activation", out=E2.ap[:], in_=d.ap[:], func=AF.Exp, scale=-1.0, reads=[d], writes=[E2])
        OP(P, "scalar", "activation", out=sc.ap[:, 0, :], in_=v8(b)[:, :, 63], func=AF.Exp, reads=[b], writes=[sc])
        ebs = self.tl("bp", j) if self.bwd else b
        OP(P, "scalar", "activation", out=self.tl("eb", j).ap[:], in_=ebs.ap[:], func=AF.Exp, reads=[ebs],
           writes=[self.tl("eb", j)])
        OP(P, "scalar", "activation", out=sc.ap[:, 3, :], in_=v8(d)[:, :, e], func=AF.Exp, reads=[d], writes=[sc])
        OP(P, "scalar", "activation", out=kk.ap[:], in_=fF.ap[:], func=AF.Copy, bias=1.0, scale=-1.0, reads=[fF], writes=[kk])

    def s3b(self, j):
        P = self.P
        qF, E1, E2, kk, qt, kt = (self.tl(k, j) for k in ("qF", "E1", "E2", "kk", "qt", "kt"))
        OP(P, "vector", "tensor_tensor", qt.ap[:], qF.ap[:], E1.ap[:], ALU.mult, reads=[qF, E1], writes=[qt])
        OP(P, "gpsimd", "tensor_tensor", kt.ap[:], kk.ap[:], E2.ap[:], ALU.mult, reads=[kk, E2], writes=[kt])
        eb, qh = self.tl("eb", j), self.tl("qh", j)
        OP(P, "gpsimd", "tensor_tensor", qh.ap[:], qF.ap[:], eb.ap[:], ALU.mult, reads=[qF, eb], writes=[qh])

    def s4a(self, j):
        P = self.P
        qt, kt = self.tl("qt", j), self.tl("kt", j)
        mms = []
        for h in range(4):
            for c in range(2):
                hc = h * 2 + c
                kw = dict(start=True, stop=True)
                if c:
                    kw["tile_position"] = (0, 64)
                mms.append(("matmul", (self.A_v[c * 64:(c + 1) * 64, h, :], kt.ap[:, hc * 64:(hc + 1) * 64],
                                       qt.ap[:, hc * 64:(hc + 1) * 64]), kw))
        MMG(P, mms, reads=[kt, qt], writes=[self.A_ps])
        mms = []
        for h in range(4):
            for c in range(2):
                hc = h * 2 + c
                kw = dict(tile_position=(0, 64)) if c else {}
                mms.append(("transpose", (self.kT_v[c * 64:(c + 1) * 64, h, :], kt.ap[:, hc * 64:(hc + 1) * 64],
                                          self.ident[:]), kw))
        MMG(P, mms, reads=[kt], writes=[self.kT_ps], extra=[self.c_all])

    def s4b(self, j):
        P = self.P
        Am, kTs = self.tl("Am", j), self.tl("kTs", j)
        OP(P, "vector", "tensor_tensor", Am.ap[:], self.A_v,
           self.gmask[:, 1 if self.bwd else 0, :].unsqueeze(1).broadcast_to([128, 4, 64]), ALU.mult,
           reads=[self.A_ps], writes=[Am], extra=[self.c_all])
        OP(P, "scalar", "activation", out=kTs.ap[:], in_=self.kT_v, func=AF.Copy, reads=[self.kT_ps], writes=[kTs])

    def stages(self):
        if self.wide:
            return [(8, 0, self.ld_f), (2, 0, self.ld_v), (7, 0, self.s1a), (5, 0, self.s2a), (3, 0, self.s3a), (1, 0, self.s4a),
                    (1, 2, self.s4b), (3, 2, self.s3b), (5, 2, self.s2b), (7, 2, self.s1b)]
        return [(5, 0, self.ld_f), (2, 0, self.ld_v), (4, 0, self.s1a), (3, 0, self.s2a), (2, 0, self.s3a), (1, 0, self.s4a),
                (1, 2, self.s4b), (2, 2, self.s3b), (3, 2, self.s2b), (4, 2, self.s1b)]

    def chunk(self, j, c, o_ps):
        P = self.P
        slot = self.items[j][0]
        U_ps = self.U_ps[slot % len(self.U_ps)]
        S, Sn, Sb = self.S[slot][self.Sph[slot]], self.S[slot][1 - self.Sph[slot]], self.Sb[slot]
        self.Sph[slot] = 1 - self.Sph[slot]
        sc, qh, Am, kTs, v2 = (self.tl(k, j) for k in ("sc", "qh", "Am", "kTs", "v2"))
        cs = slice(c * 64, (c + 1) * 64)
        v4 = lambda ap: ap.rearrange("p (h d) -> p h d", h=4)
        sv = sc.ap[:, 0, :].rearrange("p (h c) -> p h c", c=2)[:, :, c:c + 1].broadcast_to([128, 4, 128])
        ev = sc.ap[:, 3, :].rearrange("p (h c) -> p h c", c=2)[:, :, c:c + 1].broadcast_to([128, 4, 128])
        OP(P, "scalar", "activation", out=Sb.ap[:], in_=S.ap[:], func=AF.Copy, reads=[S], writes=[Sb])
        MMG(P, [("matmul", (U_ps.ap[:, h * 128:(h + 1) * 128], kTs.ap[cs, h, :], v2.ap[cs, h * 128:(h + 1) * 128]),
                 dict(start=True, stop=True)) for h in range(4)], reads=[kTs, v2], writes=[U_ps])
        mms = []
        for h in range(4):
            hc = h * 2 + c
            hs = slice(h * 128, (h + 1) * 128)
            kw1 = dict(start=True, stop=False)
            kw2 = dict(start=False, stop=True)
            if c:
                kw1["tile_position"] = (64, 64)
                kw2["tile_position"] = (0, 64)
            mms.append(("matmul", (o_ps.ap[cs, hs], Am.ap[cs, h, :], v2.ap[cs, hs]), kw1))
            mms.append(("matmul", (o_ps.ap[cs, hs], qh.ap[:, hc * 64:(hc + 1) * 64], Sb.ap[:, hs]), kw2))
        MMG(P, mms, reads=[Am, v2, qh, Sb], writes=[o_ps])
        for h in range(4):
            hc = h * 2 + c
            hs = slice(h * 128, (h + 1) * 128)
            OP(P, "scalar", "activation", out=Sn.ap[:, hs], in_=S.ap[:, hs], func=AF.Copy, scale=sc.ap[:, 0, hc:hc + 1],
               reads=[S, sc], writes=[Sn])
        for h in range(4):
            hc = h * 2 + c
            hs = slice(h * 128, (h + 1) * 128)
            OP(P, "vector", "scalar_tensor_tensor", Sn.ap[:, hs], U_ps.ap[:, hs], sc.ap[:, 3, hc:hc + 1], Sn.ap[:, hs],
               ALU.mult, ALU.add, reads=[U_ps, sc, Sn], writes=[Sn])


def gla_fwd_phase(P, A, scr, NSLOT, S):
    nc = P.nc
    NB = S // 128
    tag = "c_"
    with ExitStack() as st:
        items = [(slot, n, slot * S + n * 128) for n in range(NB) for slot in range(NSLOT)]
        G = Gla(P, st, tag, A, scr, NSLOT, False, items, o_banks=2, u_banks=2, wide=True)
        ost = [T(st.enter_context(nc.sbuf_tensor(tag + f"ost{i}", [128, 512], F32))) for i in range(3)]
        s_st = [P.sem(tag + f"st{i}", 16) for i in range(3)]
        for slot in range(NSLOT):
            G.reset_state(slot)

        def chunks(j):
            if j % NSLOT != NSLOT - 1:
                return
            grp = list(range(j - NSLOT + 1, j + 1))
            for c in (0, 1):
                for jj in grp:
                    G.chunk(jj, c, G.o_ps[jj % 2])
            for jj in grp:
                o = ost[jj % 3]
                OP(P, "scalar", "activation", out=o.ap[:], in_=G.o_ps[jj % 2].ap[:], func=AF.Copy,
                   reads=[G.o_ps[jj % 2]], writes=[o])
                tok0 = items[jj][2]
                DMA(P, "gpsimd", scr["oF"][tok0:tok0 + 128, :], o.ap[:], s_st[jj % 3], reads=[o])

        pipeline(len(items), G.stages() + [(0, 1, chunks)])
        P.barrier(extra=[(sm, sm.n) for sm in s_st])
        P.flush()


def mix_phase(P, x1, A, scr, NSLOT, S):
    nc = P.nc
    NB = S // 128
    tag = "m_"
    with ExitStack() as st:
        sb = lambda name, shape, dt: st.enter_context(nc.sbuf_tensor(tag + name, shape, dt))
        ps = lambda name, shape, dt: st.enter_context(nc.psum_tensor(tag + name, shape, dt))
        items = [(slot, n, slot * S + n * 128) for n in range(NB - 1, -1, -1) for slot in range(NSLOT)]
        NI = len(items)
        G = Gla(P, st, tag, A, scr, NSLOT, True, items, o_banks=2, u_banks=1, wide=WIDE_MIX)
        ident = G.ident
        ring = lambda name, n, shape, dt: [T(sb(f"{name}{i}", shape, dt)) for i in range(n)]
        wout = sb("wout", [128, KC, D], BF16)
        gnorm = sb("gnorm", [128, 128], F32)
        sink = sb("sink", [128, 8], F32)
        esink = sb("esink", [128, 8], F32)
        amask = sb("amask", [128, 2, 512], BF16)
        nhalf = sb("nhalf", [128, 4], F32)
        vaug = sb("vaug", [128, NSLOT * 4, 2, 65], BF16)
        kTr = sb("kTr", [128, NSLOT * 4, 256], BF16)
        vr = sb("vr", [128, NSLOT * 4, 256], BF16)
        kvt = [[T(None) for j in range(4)] for _ in range(NSLOT)]
        qTb = ring("qTb", 4, [128, 512], BF16)
        ofw = ring("ofw", 4, [128, 512], F32)
        hgw = ring("hgw", 4, [128, 512], F32)
        x1t = ring("x1t", 3, [128, D], F32)
        Pt = ring("Pt", 4, [128, 512], BF16)
        ot = ring("ot", 5, [128, 512], F32)
        sqo = ring("sqo", 2, [128, 512], F32)
        ostat = ring("ostat", 4, [128, 12], F32)
        on = ring("on", 3, [128, 512], F32)
        mixg = ring("mixg", 3, [128, 512], BF16)
        mixa = ring("mixa", 8, [128, 512], BF16)
        den = ring("den", 2, [128, 16], F32)
        mixT = ring("mixT", 2, [128, KC, 128], BF16)
        S_ps = [T(ps(f"Sa{i}", [128, 512], F32)) for i in range(2)]
        O_bank = ps("O", [128, 512], F32)
        O_ps = T(O_bank[:, 0:260].rearrange("p (g d) -> p g d", g=4))
        TY = O_ps
        tpm = O_bank[:].bitcast(BF16).rearrange("p (kc t) -> p kc t", kc=KC)

        s_w = P.sem(tag + "w", 16)
        s_q = [P.sem(tag + f"q{i}", 16) for i in range(4)]
        s_o = [P.sem(tag + f"o{i}", 16) for i in range(4)]
        s_h = [P.sem(tag + f"h{i}", 16) for i in range(4)]
        s_x = [P.sem(tag + f"x{i}", 16) for i in range(3)]
        s_kv = [[P.sem(tag + f"kv{sl}_{j}", 16) for j in range(4)] for sl in range(NSLOT)]
        s_st = [P.sem(tag + f"st{i}", 16) for i in range(4)]

        wv = A["w_out"].rearrange("(kc p) n -> p kc n", p=128)
        for kc in range(KC):
            P.dma("gpsimd", wout[:, kc, :], wv[:, kc, :], sem=s_w)
        P.dma("gpsimd", gnorm[:], A["hg_out_norm"].partition_broadcast(128), sem=s_w)
        P.dma("gpsimd", amask[:], A["amask"], sem=s_w)
        w_all = P.dma("gpsimd", sink[:], A["attn_sink"].partition_broadcast(128), sem=s_w)
        t_nh = P.op("gpsimd", "memset", nhalf[:], -0.5)
        t_va = P.op("gpsimd", "memset", vaug[:], 1.0)
        t_es = P.op("scalar", "activation", out=esink[:], in_=sink[:], func=AF.Exp, waits=[w_all])
        cready = [w_all, t_nh, t_va, t_es, G.c_all]

        def load_kv(slot, n):
            jj = n % 4
            t = kvt[slot][jj]
            tok0 = slot * S + n * 128
            ex = _deps(P, "sync", [], [t], [t_va])
            P.dma("sync", kTr[:, slot * 4 + jj, 0:128], scr["kTs"][:, tok0:tok0 + 128], waits=ex, sem=s_kv[slot][jj])
            tk = P.dma("sync", vr[:, slot * 4 + jj, 0:128], scr["vA"][tok0:tok0 + 128, :], sem=s_kv[slot][jj])
            tk2 = P.op("gpsimd", "tensor_copy", vaug[:, slot * 4 + jj, :, 0:64],
                       vr[:, slot * 4 + jj, 0:128].rearrange("p (k d) -> p k d", k=2), waits=[tk])
            _reg(tk2, [], [t])

        def kvl(j):
            slot, n, _ = items[j]
            if n - 1 >= 0:
                load_kv(slot, n - 1)

        def lq(j):
            slot, n, _ = items[j]
            DMA(P, "sync", qTb[j % 4].ap[:], scr["qTs"][:, slot * NB + n, :], s_q[j % 4], writes=[qTb[j % 4]])

        def lo(j):
            tok0 = items[j][2]
            DMA(P, "sync", ofw[j % 4].ap[:], scr["oF"][tok0:tok0 + 128, :], s_o[j % 4], writes=[ofw[j % 4]])

        def lh(j):
            tok0 = items[j][2]
            DMA(P, "sync", hgw[j % 4].ap[:], scr["hgS"][tok0:tok0 + 128, :], s_h[j % 4], writes=[hgw[j % 4]])

        def lx(j):
            tok0 = items[j][2]
            DMA(P, "sync", x1t[j % 3].ap[:], x1[tok0:tok0 + 128, :], s_x[j % 3], writes=[x1t[j % 3]])

        cnt = {"s": 0, "p": 0}

        def att(j):
            slot, n, _ = items[j]
            q = qTb[j % 4]
            ma = mixa[j % 8]
            dn = den[j % 2]
            kbs = [kb for kb in (n - 1, n, n + 1) if 0 <= kb < NB]

            def scores(kv, kb):
                ksl = slice(kv * 64, (kv + 1) * 64)
                t = kvt[slot][kb % 4]
                col = slot * 4 + kb % 4
                sp = S_ps[cnt["s"] % 2]
                cnt["s"] += 1
                mms = [("matmul", (sp.ap[:], kTr[ksl, col, 0:128], q.ap[ksl, :]), dict(start=True, stop=(kb == n)))]
                if kb != n:
                    mms.append(("matmul", (sp.ap[:], ident[:], amask[:, 0 if kb < n else 1, :]),
                                dict(start=False, stop=True)))
                MMG(P, mms, reads=[t, q], writes=[sp], extra=cready)
                pt = Pt[cnt["p"] % 4]
                cnt["p"] += 1
                OP(P, "scalar", "activation", out=pt.ap[:], in_=sp.ap[:], func=AF.Exp, scale=0.125,
                   reads=[sp], writes=[pt])
                return (pt, t, col)

            def pv(kv, pts):
                mms = []
                for g in range(4):
                    for ki, (pt, t, col) in enumerate(pts):
                        mms.append(("matmul", (O_ps.ap[:, g, :], pt.ap[:, g * 128:(g + 1) * 128], vaug[:, col, kv, :]),
                                    dict(start=(ki == 0), stop=(ki == len(pts) - 1))))
                MMG(P, mms, reads=[p[0] for p in pts] + [p[1] for p in pts], writes=[O_ps])
                hs = slice(kv * 4, (kv + 1) * 4)
                rs = slice(8 + kv * 4, 12 + kv * 4)
                OP(P, "vector", "tensor_tensor", dn.ap[:, hs], O_ps.ap[:, :, 64], esink[:, hs], ALU.add,
                   reads=[O_ps], writes=[dn], extra=cready)
                OP(P, "vector", "reciprocal", dn.ap[:, rs], dn.ap[:, hs], reads=[dn], writes=[dn])
                OP(P, "vector", "tensor_tensor", ma.ap[:, kv * 256:(kv + 1) * 256].rearrange("p (g d) -> p g d", d=64),
                   O_ps.ap[:, :, 0:64], dn.ap[:, rs].unsqueeze(2).broadcast_to([128, 4, 64]),
                   ALU.mult, reads=[O_ps, dn], writes=[ma])

            p0 = [scores(0, kb) for kb in kbs]
            p1 = [scores(1, kbs[0])]
            pv(0, p0)
            p1 += [scores(1, kb) for kb in kbs[1:]]
            pv(1, p1)

        def chunks(j):
            if j % NSLOT != NSLOT - 1:
                return
            grp = list(range(j - NSLOT + 1, j + 1))
            for c in (1, 0):
                for jj in grp:
                    G.chunk(jj, c, G.o_ps[jj % 2])
            for jj in grp:
                OP(P, "vector", "tensor_tensor", ot[jj % 5].ap[:], G.o_ps[jj % 2].ap[:], ofw[jj % 4].ap[:], ALU.add,
                   reads=[G.o_ps[jj % 2], ofw[jj % 4]], writes=[ot[jj % 5]])

        v4 = lambda ap: ap.rearrange("p (h d) -> p h d", h=4)

        def c1a(j):
            OP(P, "gpsimd", "tensor_tensor", sqo[j % 2].ap[:], ot[j % 5].ap[:], ot[j % 5].ap[:], ALU.mult,
               reads=[ot[j % 5]], writes=[sqo[j % 2]])

        def c1b(j):
            os_ = ostat[j % 4]
            OP(P, "vector", "tensor_reduce", os_.ap[:, 0:4], v4(sqo[j % 2].ap[:]), AX.X, ALU.add, reads=[sqo[j % 2]], writes=[os_])
            OP(P, "vector", "tensor_scalar", os_.ap[:, 4:8], os_.ap[:, 0:4], 1.0 / 128, EPS, ALU.mult, ALU.add,
               reads=[os_], writes=[os_])

        def c2a(j):
            os_ = ostat[j % 4]
            OP(P, "gpsimd", "tensor_tensor", os_.ap[:, 8:12], os_.ap[:, 4:8], nhalf[:, 0:4], ALU.pow,
               reads=[os_], writes=[os_], extra=cready)

        def c2b(j):
            os_ = ostat[j % 4]
            OP(P, "vector", "tensor_tensor", v4(on[j % 3].ap[:]), v4(ot[j % 5].ap[:]),
               os_.ap[:, 8:12].unsqueeze(2).broadcast_to([128, 4, 128]), ALU.mult, reads=[ot[j % 5], os_], writes=[on[j % 3]])

        def c3a(j):
            OP(P, "gpsimd", "tensor_tensor", v4(on[j % 3].ap[:]), v4(on[j % 3].ap[:]),
               gnorm[:].unsqueeze(1).broadcast_to([128, 4, 128]), ALU.mult, reads=[on[j % 3]], writes=[on[j % 3]], extra=cready)

        def c3b(j):
            OP(P, "vector", "tensor_tensor", mixg[j % 3].ap[:], on[j % 3].ap[:], hgw[j % 4].ap[:], ALU.mult,
               reads=[on[j % 3], hgw[j % 4]], writes=[mixg[j % 3]])

        def o1a(j):
            mms = []
            for kc in range(4):
                mms.append(("transpose", (tpm[:, kc, :], mixg[j % 3].ap[:, kc * 128:(kc + 1) * 128], ident[:]), {}))
            for kc in range(4, 8):
                mms.append(("transpose", (tpm[:, kc, :], mixa[j % 8].ap[:, (kc - 4) * 128:(kc - 3) * 128], ident[:]), {}))
            MMG(P, mms, reads=[mixg[j % 3], mixa[j % 8]], writes=[TY], extra=cready)

        def o1b(j):
            OP(P, "scalar", "activation", out=mixT[j % 2].ap[:], in_=tpm, func=AF.Copy, reads=[TY], writes=[mixT[j % 2]])

        def o2a(j):
            for half, bank in ((0, S_ps[0]), (1, S_ps[1])):
                hs = slice(half * 512, (half + 1) * 512)
                MMG(P, [("matmul", (bank.ap[:], mixT[j % 2].ap[:, kc, :], wout[:, kc, hs]),
                         dict(start=(kc == 0), stop=(kc == KC - 1))) for kc in range(KC)],
                    reads=[mixT[j % 2]], writes=[bank], extra=cready)

        def o2b(j):
            tok0 = items[j][2]
            xt = x1t[j % 3]
            for half, bank in ((0, S_ps[0]), (1, S_ps[1])):
                hs = slice(half * 512, (half + 1) * 512)
                OP(P, "vector", "tensor_tensor", xt.ap[:, hs], bank.ap[:], xt.ap[:, hs], ALU.add,
                   reads=[bank, xt], writes=[xt])
            DMA(P, "gpsimd", scr["x2"][tok0:tok0 + 128, :], xt.ap[:], s_st[j % 4], reads=[xt])
            if DEBUG:
                DMA(P, "gpsimd", scr["mixdbg"][tok0:tok0 + 128, 0:512], mixg[j % 3].ap[:], s_st[j % 4], reads=[mixg[j % 3]])
                DMA(P, "gpsimd", scr["mixdbg"][tok0:tok0 + 128, 512:1024], mixa[j % 8].ap[:], s_st[j % 4], reads=[mixa[j % 8]])

        for slot in range(NSLOT):
            G.reset_state(slot)
            load_kv(slot, NB - 1)
        sh = NSLOT - 1
        stages = [(-3 - sh, 0, c3a), (-3 - sh, 0, lx), (-2 - sh, 0, c2a), (-1 - sh, 0, c1a), (-1 - sh, 0, lh)] + \
            [s for s in G.stages() if s[1] == 0] + [(3, 0, kvl), (3, 0, lq), (2, 0, lo)] + \
            [(0, 1, chunks), (1, 1, att)] + \
            [(-3 - sh, 2, c3b), (-2 - sh, 2, c2b), (-1 - sh, 2, c1b)] + [s for s in G.stages() if s[1] == 2] + \
            [(-5 - sh, 2, o2a), (-5 - sh, 2, o2b), (-4 - sh, 2, o1a), (-4 - sh, 2, o1b)]
        pipeline(NI, stages)
        P.barrier(extra=[(sm, sm.n) for sm in s_st])
        P.flush()


WEIGHTS = [("ffn1_norm", [D]), ("ffn1_wi", [D, 2 * FF]), ("ffn1_wo", [FF, D]), ("mix_norm", [D]),
           ("w_in", [D, NIN]), ("lbf", [128, 4, 2]), ("lbb", [128, 4, 2]), ("hg_out_norm", [128]),
           ("q_norm", [64]), ("k_norm", [64]), ("attn_sink", [8]), ("w_out", [D, D]),
           ("ffn2_norm", [D]), ("ffn2_wi", [D, 2 * FF]), ("ffn2_wo", [FF, D])]


def build_program(NSLOT, S, phases=("ffn1", "proj", "glaf", "mix", "ffn2"), dbg_out=()):
    NTOK = NSLOT * S
    NBLK = NTOK // 128
    nc = bass.Bass("TRN2", target_bir_lowering=False)
    dt = lambda name, shape, dtype=F32, kind="ExternalInput": nc.dram_tensor(name, shape, dtype, kind=kind).ap()
    A = {"x": dt("x", [NTOK, D])}
    for name, shape in WEIGHTS:
        A[name] = dt(name, shape)
    A["ident"] = dt("ident", [128, 128], BF16)
    A["cosT"] = dt("cosT", [128, S // 128, 32])
    A["sinT"] = dt("sinT", [128, S // 128, 32])
    A["scanmask"] = dt("scanmask", [128, 512])
    A["gmask"] = dt("gmask", [128, 2, 64])
    A["amask"] = dt("amask", [128, 2, 512], BF16)
    y = dt("y", [NTOK, D], kind="ExternalOutput")
    scr = {}
    for name, shape, dtype in [("x1", [NTOK, D], F32), ("qF", [4, 128, NTOK], F32), ("ffF", [4, 128, NTOK], F32),
                               ("fbF", [4, 128, NTOK], F32), ("vT", [NTOK, 512], BF16), ("hgS", [NTOK, 512], F32),
                               ("qTs", [128, NBLK, 512], BF16), ("kTs", [128, NTOK], BF16), ("vA", [NTOK, 128], BF16),
                               ("oF", [NTOK, 512], F32), ("x2", [NTOK, D], F32), ("mixdbg", [NTOK, D], BF16)]:
        scr[name] = dt(name, shape, dtype, kind=("ExternalOutput" if name in dbg_out else "Internal"))

    P = Prog(nc)
    with P.gstack:
        src = A["x"]
        if "ffn1" in phases:
            ffn_phase(P, A["x"], scr["x1"], A["ffn1_norm"], A["ffn1_wi"], A["ffn1_wo"], NTOK, A["ident"], "a_")
            src = scr["x1"]
        if "proj" in phases:
            proj_phase(P, src, A, scr, NSLOT, S)
        if "glaf" in phases:
            gla_fwd_phase(P, A, scr, NSLOT, S)
        if "mix" in phases:
            mix_phase(P, src, A, scr, NSLOT, S)
            src = scr["x2"]
        if "ffn2" in phases:
            ffn_phase(P, src, y, A["ffn2_norm"], A["ffn2_wi"], A["ffn2_wo"], NTOK, A["ident"], "d_")
    return nc


def host_consts(S):
    pos = np.arange(S, dtype=np.float32)
    inv_freq = (10000.0 ** (-np.arange(0, 64, 2, dtype=np.float32) / 64)).astype(np.float32)
    ang = pos[:, None] * inv_freq[None, :]
    cos = np.cos(ang).astype(np.float32).reshape(S // 128, 128, 32).transpose(1, 0, 2)
    sin = np.sin(ang).astype(np.float32).reshape(S // 128, 128, 32).transpose(1, 0, 2)
    scanmask = np.ones((128, 512), np.float32)
    scanmask[:, ::64] = 0.0
    i = np.arange(64)
    gmask = np.stack([(i[:, None] <= i[None, :]), (i[:, None] >= i[None, :])], axis=1).astype(np.float32)
    gmask = np.concatenate([gmask, gmask], axis=0)
    j = np.arange(128)
    prev = np.where(j[:, None] >= j[None, :], 0.0, -30000.0)
    nxt = np.where(j[:, None] <= j[None, :], 0.0, -30000.0)
    amask = np.stack([np.tile(prev, (1, 4)), np.tile(nxt, (1, 4))], axis=1).astype(ml_dtypes.bfloat16)
    return dict(ident=np.eye(128).astype(ml_dtypes.bfloat16), cosT=np.ascontiguousarray(cos),
                sinT=np.ascontiguousarray(sin), scanmask=scanmask, gmask=np.ascontiguousarray(gmask), amask=amask)


_NC_CACHE = {}


def kernel(**inputs):
    S = 8192
    NSLOT = 2
    xp = np.asarray(inputs["x_prompt"], dtype=np.float32)
    xs = np.asarray(inputs["x_sample"], dtype=np.float32)
    seqs = [xp[0], xp[1]] + [xs[i] for i in range(8)]
    assign = [(c, 8 + c if c < 2 else None) for c in range(N_CORES)]
    w = {}
    for name, _shape in WEIGHTS:
        if name in ("lbf", "lbb"):
            src = inputs["hg_lb_fwd" if name == "lbf" else "hg_lb_bwd"]
            w[name] = np.ascontiguousarray(np.asarray(src, np.float32).reshape(2, 4, 128).transpose(2, 1, 0))
        else:
            w[name] = np.ascontiguousarray(np.asarray(inputs[name], np.float32)[0])
    consts = host_consts(S)
    key = (NSLOT, S)
    if key not in _NC_CACHE:
        _NC_CACHE[key] = build_program(NSLOT, S)
    nc = _NC_CACHE[key]
    zeros = np.zeros((S, D), np.float32)
    in_maps = []
    for c in range(N_CORES):
        a, b = assign[c]
        x = np.concatenate([seqs[a], seqs[b] if b is not None else zeros], axis=0)
        m = {"x": np.ascontiguousarray(x)}
        m.update(w)
        m.update(consts)
        in_maps.append(m)
    res = run_bass_kernel_spmd(nc, in_maps, core_ids=list(range(N_CORES)))
    outs = [None] * 10
    for c in range(N_CORES):
        y = np.asarray(res.results[c]["y"])
        a, b = assign[c]
        outs[a] = y[:S]
        if b is not None:
            outs[b] = y[S:]
    y_prompt = np.stack(outs[:2]).astype(np.float32)
    y_sample = np.stack(outs[2:]).astype(np.float32)
    return (y_prompt, y_sample)
```

```python
import math
from contextlib import ExitStack

import numpy as np
import ml_dtypes
import concourse.bass as bass
import concourse.mybir as mybir
from concourse.bass_utils import run_bass_kernel_spmd

F32 = mybir.dt.float32
BF16 = mybir.dt.bfloat16
AF = mybir.ActivationFunctionType
ALU = mybir.AluOpType
AX = mybir.AxisListType

D = 1024
FF = 2816
KC = D // 128
HC = FF // 128
NIN = 3328
EPS = 1e-6
N_CORES = 8
DEBUG = False
BARRIER_TEST = False
WIDE_MIX = True


class Sem:
    def __init__(self, h, step):
        self.h = h
        self.n = 0
        self.step = step


class Prog:
    ENG = ["sync", "scalar", "vector", "gpsimd", "tensor"]

    def __init__(self, nc):
        self.nc = nc
        self.gstack = ExitStack()
        self.q = {e: [] for e in self.ENG}
        self.waited = {e: {} for e in self.ENG}
        self.esem = {}
        self.all_sems = []
        for e in ["scalar", "vector", "gpsimd", "tensor"]:
            self.esem[e] = self.sem("c_" + e, 1)
        self.final = []
        self.nsem = 0

    def sem(self, name, step):
        sm = Sem(self.gstack.enter_context(self.nc.semaphore(name)), step)
        self.all_sems.append(sm)
        return sm

    def _waits(self, eng, waits):
        out = []
        w = self.waited[eng]
        for t in waits:
            if t is None:
                continue
            s, v = t
            if v <= 0:
                continue
            if w.get(id(s), 0) >= v:
                continue
            w[id(s)] = v
            out.append((s, v))
        return out

    def op(self, eng, name, *args, waits=(), sig=True, **kw):
        fn = (lambda e, name=name, args=args, kw=kw: getattr(e, name)(*args, **kw))
        ws = self._waits(eng, waits)
        s = None
        tk = None
        if sig:
            s = self.esem[eng]
            s.n += 1
            tk = (s, s.n)
        self.q[eng].append((fn, ws, s))
        return tk

    def dma(self, eng, out, in_, waits=(), sem=None):
        ws = self._waits(eng, waits)
        sem.n += 16
        self.q[eng].append((lambda e, o=out, i=in_: e.dma_start(out=o, in_=i), ws, sem))
        return (sem, sem.n)

    def barrier(self, extra=()):
        tks = [(s, s.n) for s in self.esem.values()] + list(extra)
        for e in self.ENG:
            ws = self._waits(e, tks)
            if ws:
                self.q[e].append((None, ws, None))

    def flush(self):
        nc = self.nc
        with nc.Block() as block:
            def run(eng_name):
                def body(e):
                    for fn, ws, s in self.q[eng_name]:
                        for (ws_s, v) in ws:
                            e.wait_ge(ws_s.h, v)
                        if fn is None:
                            continue
                        ins = fn(e)
                        if s is not None:
                            ins.then_inc(s.h, s.step)
                return body
            block.sync(run("sync"))
            block.scalar(run("scalar"))
            block.vector(run("vector"))
            block.gpsimd(run("gpsimd"))
            block.tensor(run("tensor"))
        self.q = {e: [] for e in self.ENG}


def bc(ap, shape_dims):
    return ap


def ffn_phase(P, src, dst, g_dram, wi_dram, wo_dram, NTOK, ident_dram, tag):
    nc = P.nc
    NT = NTOK // 512
    with ExitStack() as st:
        sb = lambda name, shape, dt: st.enter_context(nc.sbuf_tensor(tag + name, shape, dt))
        ps = lambda name, shape, dt: st.enter_context(nc.psum_tensor(tag + name, shape, dt))
        wi = sb("wi", [128, KC, 2 * FF], BF16)
        wo = sb("wo", [128, HC, D], BF16)
        gt = sb("g", [128, D], F32)
        ident = sb("ident", [128, 128], BF16)
        xin = [sb(f"xin{i}", [128, D], F32) for i in range(2)]
        xres = [sb(f"xres{i}", [128, D], F32) for i in range(2)]
        xn = [sb(f"xn{i}", [128, D], BF16) for i in range(4)]
        junk = sb("junk", [128, D], BF16)
        xnT = sb("xnT", [128, KC, 512], BF16)
        hT = sb("hT", [128, HC, 512], BF16)
        sg = [sb(f"sg{i}", [128, 512], BF16) for i in range(2)]
        ss = sb("ss", [128, NT * 4], F32)
        ms = sb("ms", [128, NT * 4], F32)
        rstd = sb("rstd", [128, NT * 4], F32)
        nhalf = sb("nhalf", [128, 1], F32)
        tp = [ps(f"tp{i}", [128, D], BF16) for i in range(2)]
        Gp = [ps(f"G{i}", [128, 512], F32) for i in range(2)]
        Up = [ps(f"U{i}", [128, 512], F32) for i in range(2)]
        Yp = [ps(f"Y{i}", [128, 512], F32) for i in range(2)]

        s_w = P.sem(tag + "w", 16)
        s_xin = [P.sem(tag + f"xin{i}", 16) for i in range(2)]
        s_xres = [P.sem(tag + f"xres{i}", 16) for i in range(2)]
        s_st = [P.sem(tag + f"st{i}", 16) for i in range(2)]

        wtk = []
        wi_v = wi_dram.rearrange("(kc p) n -> p kc n", p=128)
        for kc in range(KC):
            wtk.append(P.dma("gpsimd", wi[:, kc, :], wi_v[:, kc, :], sem=s_w))
        wtk.append(P.dma("gpsimd", gt[:], g_dram.partition_broadcast(128), sem=s_w))
        wtk.append(P.dma("gpsimd", ident[:], ident_dram, sem=s_w))
        w_all = wtk[-1]
        s_w2 = P.sem(tag + "w2", 16)
        wo_v = wo_dram.rearrange("(j p) n -> p j n", p=128)
        wo_all = None
        for j in range(0, HC, 2):
            wo_all = P.dma("gpsimd", wo[:, j:j + 2, :], wo_v[:, j:j + 2, :], sem=s_w2)
        t_nh = P.op("gpsimd", "memset", nhalf[:], -0.5)

        src_t = src.rearrange("(n p) d -> n p d", p=128)
        dst_t = dst.rearrange("(n p) d -> n p d", p=128)

        xin_free = [None, None]
        xn_free = [None] * 4
        tp_free = [None, None]
        G_free = [None, None]
        U_free = [None, None]
        sg_free = [None, None]
        Y_free = [None, None]
        xres_free = [None, None]
        xn_ready = {}
        state = {"hT_free": None, "xnT_ready": {}, "last_upgate": None}
        cnt = {"tp": 0, "gu": 0, "y": 0, "sub": 0}

        def norm_sub(t, s):
            idx = t * 4 + s
            sl = idx % 2
            col = slice(idx, idx + 1)
            ld = P.dma("sync", xin[sl][:], src_t[idx], waits=[xin_free[sl]], sem=s_xin[sl])
            a = P.op("scalar", "activation", out=junk[:], in_=xin[sl][:], func=AF.Square,
                     accum_out=ss[:, col], waits=[ld])
            b = P.op("vector", "tensor_scalar", ms[:, col], ss[:, col], 1.0 / D, EPS, ALU.mult, ALU.add,
                     waits=[a])
            c = P.op("gpsimd", "tensor_tensor", rstd[:, col], ms[:, col], nhalf[:], ALU.pow, waits=[b, t_nh])
            d = P.op("vector", "scalar_tensor_tensor", xn[idx % 4][:], xin[sl][:], rstd[:, col], gt[:],
                     ALU.mult, ALU.mult, waits=[c, xn_free[idx % 4], w_all])
            xin_free[sl] = d
            xn_ready[idx] = d

        def transpose_sub(t, s):
            idx = t * 4 + s
            sl = idx % 2
            k = cnt["tp"] % 2
            cnt["tp"] += 1
            last = None
            for kc in range(KC):
                last = P.op("tensor", "transpose", tp[k][:, kc * 128:(kc + 1) * 128],
                            xn[idx % 4][:, kc * 128:(kc + 1) * 128], ident[:],
                            waits=[xn_ready[idx], tp_free[k], w_all], sig=(kc == KC - 1))
            xn_free[idx % 4] = last
            ev = P.op("scalar", "activation", out=xnT[:, :, s * 128:(s + 1) * 128],
                      in_=tp[k][:].rearrange("p (kc t) -> p kc t", kc=KC), func=AF.Copy, waits=[last])
            tp_free[k] = ev
            state["xnT_ready"][t] = ev

        def upgate(t, j):
            k = cnt["gu"] % 2
            cnt["gu"] += 1
            rdy = state["xnT_ready"][t]
            for kc in range(KC):
                P.op("tensor", "matmul", Gp[k][:], wi[:, kc, j * 128:(j + 1) * 128], xnT[:, kc, :],
                     start=(kc == 0), stop=(kc == KC - 1), waits=[rdy, G_free[k], w_all], sig=False)
            gl = None
            for kc in range(KC):
                gl = P.op("tensor", "matmul", Up[k][:], wi[:, kc, FF + j * 128:FF + (j + 1) * 128], xnT[:, kc, :],
                          start=(kc == 0), stop=(kc == KC - 1), waits=[U_free[k]], sig=(kc == KC - 1))
            a = P.op("scalar", "activation", out=sg[k][:], in_=Gp[k][:], func=AF.Silu, waits=[gl, sg_free[k]])
            G_free[k] = a
            m = P.op("vector", "tensor_tensor", hT[:, j, :], sg[k][:], Up[k][:], ALU.mult,
                     waits=[a, state["hT_free"]])
            U_free[k] = m
            sg_free[k] = m
            state["last_h"] = m

        def down_sub(t, s):
            idx = t * 4 + s
            sl = idx % 2
            ld = P.dma("sync", xres[sl][:], src_t[idx], waits=[xres_free[sl]], sem=s_xres[sl])
            r = None
            for half in range(2):
                k = cnt["y"] % 2
                cnt["y"] += 1
                hs = slice(half * 512, (half + 1) * 512)
                last = None
                for j in range(HC):
                    last = P.op("tensor", "matmul", Yp[k][:], hT[:, j, s * 128:(s + 1) * 128], wo[:, j, hs],
                                start=(j == 0), stop=(j == HC - 1),
                                waits=[state["last_h"], Y_free[k], wo_all], sig=(j == HC - 1))
                r = P.op("vector", "scalar_tensor_tensor", xres[sl][:, hs], Yp[k][:], 0.5, xres[sl][:, hs],
                         ALU.mult, ALU.add, waits=[last, ld])
                Y_free[k] = r
                state["last_down"] = last
            stt = P.dma("gpsimd", dst_t[idx], xres[sl][:], waits=[r], sem=s_st[sl])
            xres_free[sl] = stt

        for s in range(4):
            norm_sub(0, s)
            transpose_sub(0, s)
        for t in range(NT):
            for j in range(HC):
                upgate(t, j)
                if t + 1 < NT and j in (3, 7, 11, 15):
                    norm_sub(t + 1, (j - 3) // 4)
            if t + 1 < NT:
                for s in range(4):
                    transpose_sub(t + 1, s)
            for s in range(4):
                down_sub(t, s)
            state["hT_free"] = state["last_down"]

        P.barrier(extra=[xres_free[0], xres_free[1]])
        if DEBUG:
            s_dbg = P.sem(tag + "dbg", 16)
            dd = lambda name, shape, dtype: nc.dram_tensor(tag + name, shape, dtype, kind="ExternalOutput").ap()
            tk = [P.dma("sync", dd("dbg_rstd", [128, NT * 4], F32), rstd[:], sem=s_dbg),
                  P.dma("sync", dd("dbg_ss", [128, NT * 4], F32), ss[:], sem=s_dbg),
                  P.dma("sync", dd("dbg_xn", [128, D], BF16), xn[3][:], sem=s_dbg),
                  P.dma("sync", dd("dbg_xnT", [128, KC, 512], BF16), xnT[:], sem=s_dbg),
                  P.dma("sync", dd("dbg_hT", [128, HC, 512], BF16), hT[:], sem=s_dbg),
                  P.dma("sync", dd("dbg_wi", [128, KC, 2 * FF], BF16), wi[:], sem=s_dbg),
                  P.dma("sync", dd("dbg_wo", [128, HC, D], BF16), wo[:], sem=s_dbg),
                  P.dma("sync", dd("dbg_g", [128, D], F32), gt[:], sem=s_dbg)]
            P.barrier(extra=[tk[-1]])
        P.flush()


class T:
    def __init__(self, ap):
        self.ap = ap
        self.wr = None
        self.rds = []


def _deps(P, eng, reads, writes, extra):
    own = P.esem.get(eng)
    waits = list(extra)
    for t in reads:
        waits.append(t.wr)
    for t in writes:
        for tk in t.rds + [t.wr]:
            if tk is not None and tk[0] is own:
                continue
            waits.append(tk)
    return waits


def _reg(tk, reads, writes):
    for t in reads:
        t.rds = [r for r in t.rds if r[0] is not tk[0]] + [tk]
    for t in writes:
        t.wr = tk
        t.rds = []


def OP(P, eng, name, *args, reads=(), writes=(), extra=(), **kw):
    tk = P.op(eng, name, *args, waits=_deps(P, eng, reads, writes, extra), sig=True, **kw)
    _reg(tk, reads, writes)
    return tk


def MMG(P, mms, reads=(), writes=(), extra=()):
    waits = _deps(P, "tensor", reads, writes, extra)
    tk = None
    for i, (name, args, kw) in enumerate(mms):
        tk = P.op("tensor", name, *args, waits=(waits if i == 0 else ()), sig=(i == len(mms) - 1), **kw)
    _reg(tk, reads, writes)
    return tk


def DMA(P, eng, out, in_, sem, reads=(), writes=(), extra=()):
    tk = P.dma(eng, out, in_, waits=_deps(P, eng, reads, writes, extra), sem=sem)
    _reg(tk, reads, writes)
    return tk


def proj_phase(P, x1, A, scr, NSLOT, S):
    nc = P.nc
    NTOK = NSLOT * S
    NT = NTOK // 512
    tag = "b_"
    with ExitStack() as st:
        sb = lambda name, shape, dt: st.enter_context(nc.sbuf_tensor(tag + name, shape, dt))
        ps = lambda name, shape, dt: st.enter_context(nc.psum_tensor(tag + name, shape, dt))
        win = sb("win", [128, KC, NIN], BF16)
        gt = sb("g", [128, D], F32)
        ident = sb("ident", [128, 128], BF16)
        cosT = sb("cos", [128, S // 128, 32], F32)
        sinT = sb("sin", [128, S // 128, 32], F32)
        gqk = sb("gqk", [128, 10, 64], F32)
        lbraw = sb("lbraw", [128, 2, 4, 2], F32)
        lbd = sb("lbd", [128, 8], F32)
        lb = sb("lb", [128, 8], F32)
        oml = sb("oml", [128, 8], F32)
        nhalf = sb("nhalf", [128, 16], F32)
        xin = [T(sb(f"xin{i}", [128, D], F32)) for i in range(2)]
        xn = [T(sb(f"xn{i}", [128, D], BF16)) for i in range(4)]
        xnT_r = [T(sb(f"xnT{i}", [128, KC, 512], BF16)) for i in range(2)]
        stat = [T(sb(f"stat{i}", [128, 4], F32)) for i in range(4)]
        fm = [T(sb(f"fm{i}", [128, 4, 512], F32)) for i in range(2)]
        sgm = [T(sb(f"sgm{i}", [128, 512], F32)) for i in range(2)]
        tm_v = [T(sb(f"tmv{i}", [128, 512], BF16)) for i in range(4)]
        tm_hg = [T(sb(f"tmhg{i}", [128, 512], F32)) for i in range(4)]
        tm_qk = [T(sb(f"tmqk{i}", [128, 5, 128], BF16)) for i in range(4)]
        tm_va = [T(sb(f"tmva{i}", [128, 128], BF16)) for i in range(4)]
        sqt_r = [T(sb(f"sqt{i}", [128, 10, 64], F32)) for i in range(2)]
        aqk_r = [T(sb(f"aqk{i}", [128, 10, 64], F32)) for i in range(2)]
        qst_r = [T(sb(f"qst{i}", [128, 32], F32)) for i in range(2)]
        qn_r = [T(sb(f"qn{i}", [128, 10, 64], F32)) for i in range(2)]
        rt_r = [[T(sb(f"rt{i}_{k}", [128, 10, 32], F32)) for i in range(4)] for k in range(2)]
        qr_r = [T(sb(f"qr{i}", [128, 10, 64], BF16)) for i in range(3)]
        tp = [T(ps(f"tp{i}", [128, D], BF16)) for i in range(2)]
        fps = [T(ps(f"fps{i}", [128, 512], F32)) for i in range(2)]
        tps = [T(ps(f"tps{i}", [128, 512], F32)) for i in range(3)]
        tq = T(ps("tq", [128, 5, 128], BF16))

        s_w = P.sem(tag + "w", 16)
        s_xin = [P.sem(tag + f"xin{i}", 16) for i in range(2)]
        s_fm = [P.sem(tag + f"fm{i}", 16) for i in range(2)]
        s_tm = [P.sem(tag + f"tm{i}", 16) for i in range(4)]
        s_tv = [P.sem(tag + f"tv{i}", 16) for i in range(4)]

        wv = A["w_in"].rearrange("(kc p) n -> p kc n", p=128)
        for kc in range(KC):
            P.dma("gpsimd", win[:, kc, 0:2560], wv[:, kc, 0:2560], sem=s_w)
            for kv in range(2):
                P.dma("gpsimd", win[:, kc, 2560:3072].rearrange("p (g k d) -> p k g d", k=2, d=64)[:, kv],
                      wv[:, kc, 2560 + kv * 256:2560 + (kv + 1) * 256].rearrange("p (g d) -> p g d", d=64), sem=s_w)
            P.dma("gpsimd", win[:, kc, 3072:3328], wv[:, kc, 3072:3328], sem=s_w)
        P.dma("gpsimd", gt[:], A["mix_norm"].partition_broadcast(128), sem=s_w)
        P.dma("gpsimd", ident[:], A["ident"], sem=s_w)
        P.dma("gpsimd", cosT[:], A["cosT"], sem=s_w)
        P.dma("gpsimd", sinT[:], A["sinT"], sem=s_w)
        for i in range(8):
            P.dma("gpsimd", gqk[:, i, :], A["q_norm"].partition_broadcast(128), sem=s_w)
        for i in range(8, 10):
            P.dma("gpsimd", gqk[:, i, :], A["k_norm"].partition_broadcast(128), sem=s_w)
        P.dma("gpsimd", lbraw[:, 0], A["lbf"], sem=s_w)
        w_all = P.dma("gpsimd", lbraw[:, 1], A["lbb"], sem=s_w)
        t_nh = P.op("gpsimd", "memset", nhalf[:], -0.5)
        lbv = lbraw[:].rearrange("p a h r -> p (a h) r")
        t0 = P.op("vector", "tensor_tensor", lbd[:], lbv[:, :, 0], lbv[:, :, 1], ALU.subtract, waits=[w_all])
        t1 = P.op("scalar", "activation", out=lb[:], in_=lbd[:], func=AF.Sigmoid, waits=[t0])
        t2 = P.op("scalar", "activation", out=oml[:], in_=lbd[:], func=AF.Sigmoid, scale=-1.0, waits=[t0])
        consts_ready = [w_all, t_nh, t1, t2]

        src_t = x1.rearrange("(n p) d -> n p d", p=128)

        def norm_sub(idx):
            sl = idx % 2
            stt = stat[idx % 4]
            DMA(P, "sync", xin[sl].ap[:], src_t[idx], s_xin[sl], writes=[xin[sl]])
            OP(P, "scalar", "activation", out=xn[idx % 4].ap[:], in_=xin[sl].ap[:], func=AF.Square,
               accum_out=stt.ap[:, 0:1], reads=[xin[sl]], writes=[stt, xn[idx % 4]])
            OP(P, "vector", "tensor_scalar", stt.ap[:, 1:2], stt.ap[:, 0:1], 1.0 / D, EPS, ALU.mult, ALU.add,
               reads=[stt], writes=[stt])
            OP(P, "gpsimd", "tensor_tensor", stt.ap[:, 2:3], stt.ap[:, 1:2], nhalf[:, 0:1], ALU.pow,
               reads=[stt], writes=[stt], extra=[t_nh])
            OP(P, "vector", "scalar_tensor_tensor", xn[idx % 4].ap[:], xin[sl].ap[:], stt.ap[:, 2:3], gt[:],
               ALU.mult, ALU.mult, reads=[xin[sl], stt], writes=[xn[idx % 4]], extra=[w_all])

        def transpose_sub(idx):
            s = idx % 4
            k = idx % 2
            xnT = xnT_r[(idx // 4) % 2]
            MMG(P, [("transpose", (tp[k].ap[:, kc * 128:(kc + 1) * 128], xn[idx % 4].ap[:, kc * 128:(kc + 1) * 128],
                                   ident[:]), {}) for kc in range(KC)],
                reads=[xn[idx % 4]], writes=[tp[k]], extra=[w_all])
            OP(P, "scalar", "activation", out=xnT.ap[:, :, s * 128:(s + 1) * 128],
               in_=tp[k].ap[:].rearrange("p (kc t) -> p kc t", kc=KC), func=AF.Copy, reads=[tp[k]], writes=[xnT])

        qFv = [scr[n].rearrange("h p n -> p h n") for n in ("qF", "ffF", "fbF")]
        cnt = {"f": 0, "t": 0}

        def fm_group(t, gi):
            xnT = xnT_r[t % 2]
            fs = fm[(t * 3 + gi) % 2]
            for h in range(4):
                c = gi * 4 + h
                k = cnt["f"] % 2
                cnt["f"] += 1
                MMG(P, [("matmul", (fps[k].ap[:], win[:, kc, c * 128:(c + 1) * 128], xnT.ap[:, kc, :]),
                         dict(start=(kc == 0), stop=(kc == KC - 1))) for kc in range(KC)],
                    reads=[xnT], writes=[fps[k]], extra=[w_all])
                if gi == 0:
                    OP(P, "scalar", "activation", out=fs.ap[:, h, :], in_=fps[k].ap[:], func=AF.Silu,
                       reads=[fps[k]], writes=[fs])
                else:
                    OP(P, "scalar", "activation", out=sgm[k].ap[:], in_=fps[k].ap[:], func=AF.Sigmoid,
                       reads=[fps[k]], writes=[sgm[k]])
                    ci = (gi - 1) * 4 + h
                    OP(P, "vector", "tensor_scalar", fs.ap[:, h, :], sgm[k].ap[:], oml[:, ci:ci + 1], lb[:, ci:ci + 1],
                       ALU.mult, ALU.add, reads=[sgm[k]], writes=[fs], extra=consts_ready)
            DMA(P, "gpsimd", qFv[gi][:, :, t * 512:(t + 1) * 512], fs.ap[:], s_fm[(t * 3 + gi) % 2], reads=[fs])

        def tm_proj(t, s):
            idx = t * 4 + s
            sl = idx % 2
            s4 = idx % 4
            xnT = xnT_r[t % 2]
            tok0 = idx * 128
            sqt, aqk = sqt_r[sl], aqk_r[sl]

            def proj(c0, c1):
                k = cnt["t"] % 3
                cnt["t"] += 1
                MMG(P, [("matmul", (tps[k].ap[:, 0:c1 - c0], xnT.ap[:, kc, s * 128:(s + 1) * 128], win[:, kc, c0:c1]),
                         dict(start=(kc == 0), stop=(kc == KC - 1))) for kc in range(KC)],
                    reads=[xnT], writes=[tps[k]], extra=[w_all])
                return tps[k]
            pv = proj(1536, 2048)
            OP(P, "scalar", "activation", out=tm_v[s4].ap[:], in_=pv.ap[:], func=AF.Copy, reads=[pv], writes=[tm_v[s4]])
            pg = proj(2048, 2560)
            OP(P, "scalar", "activation", out=tm_hg[s4].ap[:], in_=pg.ap[:], func=AF.Silu, reads=[pg], writes=[tm_hg[s4]])
            pq = proj(2560, 3072)
            sq_flat = sqt.ap[:].rearrange("p h d -> p (h d)")
            aq_flat = aqk.ap[:].rearrange("p h d -> p (h d)")
            OP(P, "scalar", "activation", out=aq_flat[:, 0:512], in_=pq.ap[:], func=AF.Copy, reads=[pq], writes=[aqk])
            OP(P, "scalar", "activation", out=sq_flat[:, 0:512], in_=pq.ap[:], func=AF.Square, reads=[pq], writes=[sqt])
            pk = proj(3072, 3328)
            OP(P, "scalar", "activation", out=tm_va[s4].ap[:], in_=pk.ap[:, 128:256], func=AF.Copy,
               reads=[pk], writes=[tm_va[s4]])
            OP(P, "scalar", "activation", out=aq_flat[:, 512:640], in_=pk.ap[:, 0:128], func=AF.Copy, reads=[pk], writes=[aqk])
            OP(P, "scalar", "activation", out=sq_flat[:, 512:640], in_=pk.ap[:, 0:128], func=AF.Square,
               reads=[pk], writes=[sqt])
            outs = [tm_v[s4], tm_hg[s4], tm_va[s4]]
            P.dma("sync", scr["vT"][tok0:tok0 + 128, :], tm_v[s4].ap[:], waits=[o.wr for o in outs], sem=s_tv[s4])
            P.dma("sync", scr["hgS"][tok0:tok0 + 128, :], tm_hg[s4].ap[:], sem=s_tv[s4])
            tk = P.dma("sync", scr["vA"][tok0:tok0 + 128, :], tm_va[s4].ap[:], sem=s_tv[s4])
            _reg(tk, outs, [])

        def tm_chain(t, s):
            idx = t * 4 + s
            sl = idx % 2
            blk = (idx * 128 % S) // 128
            tok0 = idx * 128
            sqt, aqk, qst, qn, rt, qr = sqt_r[sl], aqk_r[sl], qst_r[sl], qn_r[sl], rt_r[sl], qr_r[idx % 3]
            OP(P, "vector", "tensor_reduce", qst.ap[:, 0:10], sqt.ap[:], AX.X, ALU.add, reads=[sqt], writes=[qst])
            OP(P, "vector", "tensor_scalar", qst.ap[:, 10:20], qst.ap[:, 0:10], 1.0 / 64, EPS, ALU.mult, ALU.add,
               reads=[qst], writes=[qst])
            OP(P, "gpsimd", "tensor_tensor", qst.ap[:, 20:30], qst.ap[:, 10:20], nhalf[:, 0:10], ALU.pow,
               reads=[qst], writes=[qst], extra=[t_nh])
            OP(P, "vector", "tensor_tensor", qn.ap[:], aqk.ap[:], qst.ap[:, 20:30].unsqueeze(2).broadcast_to([128, 10, 64]),
               ALU.mult, reads=[aqk, qst], writes=[qn])
            OP(P, "gpsimd", "tensor_tensor", qn.ap[:], qn.ap[:], gqk[:], ALU.mult, reads=[qn], writes=[qn], extra=[w_all])
            cb = cosT[:, blk, :].unsqueeze(1).broadcast_to([128, 10, 32])
            sbb = sinT[:, blk, :].unsqueeze(1).broadcast_to([128, 10, 32])
            x1v = qn.ap[:, :, 0:32]
            x2v = qn.ap[:, :, 32:64]
            OP(P, "vector", "tensor_tensor", rt[0].ap[:], x1v, cb, ALU.mult, reads=[qn], writes=[rt[0]])
            OP(P, "gpsimd", "tensor_tensor", rt[1].ap[:], x2v, sbb, ALU.mult, reads=[qn], writes=[rt[1]])
            OP(P, "vector", "tensor_tensor", rt[2].ap[:], x2v, cb, ALU.mult, reads=[qn], writes=[rt[2]])
            OP(P, "gpsimd", "tensor_tensor", rt[3].ap[:], x1v, sbb, ALU.mult, reads=[qn], writes=[rt[3]])
            OP(P, "vector", "tensor_tensor", qr.ap[:, :, 0:32], rt[0].ap[:], rt[1].ap[:], ALU.subtract,
               reads=[rt[0], rt[1]], writes=[qr])
            OP(P, "gpsimd", "tensor_tensor", qr.ap[:, :, 32:64], rt[2].ap[:], rt[3].ap[:], ALU.add,
               reads=[rt[2], rt[3]], writes=[qr])

        def tm_store(t, s):
            idx = t * 4 + s
            sl = idx % 2
            s4 = idx % 4
            tok0 = idx * 128
            qr = qr_r[idx % 3]
            qrf = qr.ap[:].rearrange("p h d -> p (h d)")
            MMG(P, [("transpose", (tq.ap[:, i, :], qrf[:, i * 128:(i + 1) * 128], ident[:]), {}) for i in range(5)],
                reads=[qr], writes=[tq], extra=[w_all])
            OP(P, "scalar", "activation", out=tm_qk[s4].ap[:], in_=tq.ap[:], func=AF.Copy, reads=[tq], writes=[tm_qk[s4]])
            outs = [tm_qk[s4]]
            P.dma("sync", scr["qTs"][:, idx, :], tm_qk[s4].ap[:, 0:4, :].rearrange("p g t -> p (g t)"),
                  waits=[tm_qk[s4].wr], sem=s_tm[s4])
            tk = P.dma("sync", scr["kTs"][:, tok0:tok0 + 128], tm_qk[s4].ap[:, 4, :], sem=s_tm[s4])
            _reg(tk, outs, [])

        for s in range(4):
            norm_sub(s)
        for s in range(4):
            transpose_sub(s)
        pend = []

        def flush_store(keep):
            while len(pend) > keep:
                tm_store(*pend.pop(0))

        for t in range(NT):
            nxt = [(t + 1) * 4 + k for k in range(4)] if t + 1 < NT else []
            for s in range(4):
                tm_proj(t, s)
                flush_store(2)
                if s < 3:
                    fm_group(t, s)
                tm_chain(t, s)
                pend.append((t, s))
                for n_ in nxt[2 * s:2 * s + 2]:
                    norm_sub(n_)
            for n_ in nxt:
                transpose_sub(n_)
        flush_store(0)

        P.barrier(extra=[(sm, sm.n) for sm in s_fm + s_tm + s_tv])
        P.flush()


def pipeline(n, stages):
    lo = min(sk for sk, _, _ in stages)
    hi = max(sk for sk, _, _ in stages)
    for i in range(-hi, n - lo):
        for part in (0, 1, 2):
            for sk, p, fn in stages:
                if p == part and 0 <= i + sk < n:
                    fn(i + sk)


class Gla:
    RINGS = dict(qF=5, fF=5, v2=4, lf=3, b=4, g=3, d=3, E1=2, E2=2, kk=2, qt=4, kt=3, sc=5, Am=3, kTs=3,
                 eb=2, qh=4, bp=2)

    RINGS_WIDE = dict(qF=7, fF=7, v2=4, lf=4, b=6, g=2, d=4, E1=2, E2=2, kk=2, qt=5, kt=4, sc=5, Am=3, kTs=3,
                      eb=2, qh=5, bp=4)

    def __init__(self, P, st, tag, A, scr, NSLOT, bwd, items, o_banks=1, u_banks=1, wide=False):
        nc = P.nc
        self.wide = wide
        if wide:
            self.RINGS = dict(self.RINGS_WIDE)
            if bwd:
                self.RINGS.update(qF=6, fF=6, b=5, bp=3, d=3, lf=3, qt=4)
        self.P, self.scr, self.bwd, self.items = P, scr, bwd, items
        sb = lambda name, shape, dt: st.enter_context(nc.sbuf_tensor(tag + name, shape, dt))
        ps = lambda name, shape, dt: st.enter_context(nc.psum_tensor(tag + name, shape, dt))
        shapes = dict(qF=([128, 512], F32), fF=([128, 512], F32), v2=([128, 512], BF16), lf=([128, 512], F32),
                      b=([128, 512], F32), g=([128, 512], F32), d=([128, 512], F32), E1=([128, 512], F32),
                      E2=([128, 512], F32), kk=([128, 512], F32), qt=([128, 512], BF16), kt=([128, 512], BF16),
                      sc=([128, 4, 8], F32), Am=([128, 4, 64], BF16), kTs=([128, 4, 128], BF16),
                      eb=([128, 512], F32), qh=([128, 512], BF16), bp=([128, 512], F32))
        self.t = {}
        for name, (shape, dt) in shapes.items():
            if name in ("g", "bp") and not bwd:
                continue
            self.t[name] = [T(sb(f"{name}{i}", shape, dt)) for i in range(self.RINGS[name])]
        self.S = [[T(sb(f"S{i}_{k}", [128, 512], F32)) for k in range(2)] for i in range(NSLOT)]
        self.Sph = [0] * NSLOT
        self.Sb = [T(sb(f"Sb{i}", [128, 512], BF16)) for i in range(NSLOT)]
        self.A_ps = T(ps("A", [128, 4, 64], F32))
        self.kT_ps = T(ps("kT", [128, 4, 128], BF16))
        self.A_v = self.A_ps.ap[:]
        self.kT_v = self.kT_ps.ap[:]
        self.o_ps = [T(ps(f"o{i}", [128, 512], F32)) for i in range(o_banks)]
        self.U_ps = [T(ps(f"U{i}", [128, 512], F32)) for i in range(u_banks)]
        self.scanmask = sb("scanmask", [128, 512], F32)
        self.gmask = sb("gmask", [128, 2, 64], F32)
        self.ident = sb("gident", [128, 128], BF16)
        self.s_ld = {k: [P.sem(tag + f"ld{k}{i}", 16) for i in range(self.RINGS[k])] for k in ("fF", "v2")}
        self.s_c = P.sem(tag + "c", 16)
        P.dma("gpsimd", self.scanmask[:], A["scanmask"], sem=self.s_c)
        P.dma("gpsimd", self.gmask[:], A["gmask"], sem=self.s_c)
        self.c_all = P.dma("gpsimd", self.ident[:], A["ident"], sem=self.s_c)
        self.fname = "fbF" if bwd else "ffF"
        self.cnt = 0

    def tl(self, name, j):
        r = self.t[name]
        return r[j % len(r)]

    def reset_state(self, slot):
        OP(self.P, "gpsimd", "memset", self.S[slot][0].ap[:], 0.0, writes=[self.S[slot][0]])
        self.Sph[slot] = 0

    def ld_f(self, j):
        P, scr = self.P, self.scr
        tok0 = self.items[j][2]
        qF, fF = self.tl("qF", j), self.tl("fF", j)
        sem = self.s_ld["fF"][j % self.RINGS["fF"]]
        ex = _deps(P, "sync", [], [qF, fF], [])
        P.dma("sync", fF.ap[:].rearrange("p (h t) -> p h t", h=4),
              scr[self.fname].rearrange("h p n -> p h n")[:, :, tok0:tok0 + 128], waits=ex, sem=sem)
        tk = P.dma("sync", qF.ap[:].rearrange("p (h t) -> p h t", h=4),
                   scr["qF"].rearrange("h p n -> p h n")[:, :, tok0:tok0 + 128], sem=sem)
        _reg(tk, [], [qF, fF])

    def ld_v(self, j):
        tok0 = self.items[j][2]
        v2 = self.tl("v2", j)
        DMA(self.P, "sync", v2.ap[:], self.scr["vT"][tok0:tok0 + 128, :], self.s_ld["v2"][j % self.RINGS["v2"]], writes=[v2])

    @staticmethod
    def v8(t):
        return t.ap[:].rearrange("p (hc t) -> p hc t", t=64)

    def s1a(self, j):
        OP(self.P, "scalar", "activation", out=self.tl("lf", j).ap[:], in_=self.tl("fF", j).ap[:], func=AF.Ln,
           reads=[self.tl("fF", j)], writes=[self.tl("lf", j)])

    def s1b(self, j):
        OP(self.P, "vector", "tensor_tensor_scan", self.tl("b", j).ap[:], self.scanmask[:], self.tl("lf", j).ap[:], 0.0,
           ALU.mult, ALU.add, reads=[self.tl("lf", j)], writes=[self.tl("b", j)], extra=[self.c_all])

    def s2a(self, j):
        if not self.bwd:
            return
        P, v8 = self.P, self.v8
        g, lf, b, bp = self.tl("g", j), self.tl("lf", j), self.tl("b", j), self.tl("bp", j)
        OP(P, "gpsimd", "tensor_tensor", g.ap[:], lf.ap[:], b.ap[:], ALU.subtract, reads=[lf, b], writes=[g])
        OP(P, "gpsimd", "tensor_tensor", v8(bp), v8(g), v8(b)[:, :, 63:64].broadcast_to([128, 8, 64]), ALU.add,
           reads=[b, g], writes=[bp])

    def s2b(self, j):
        v8 = self.v8
        src = self.tl("g", j) if self.bwd else self.tl("b", j)
        d = self.tl("d", j)
        OP(self.P, "vector", "tensor_tensor", v8(d), v8(src), v8(src)[:, :, 32:33].broadcast_to([128, 8, 64]), ALU.subtract,
           reads=[src], writes=[d])

    def s3a(self, j):
        P, v8 = self.P, self.v8
        d, b, sc, E1, E2, kk, fF = (self.tl(k, j) for k in ("d", "b", "sc", "E1", "E2", "kk", "fF"))
        e = 0 if self.bwd else 63
        OP(P, "scalar", "activation", out=E1.ap[:], in_=d.ap[:], func=AF.Exp, reads=[d], writes=[E1])
        OP(P, "scalar", "activation", out=E2.ap[:], in_=d.ap[:], func=AF.Exp, scale=-1.0, reads=[d], writes=[E2])
        OP(P, "scalar", "activation", out=sc.ap[:, 0, :], in_=v8(b)[:, :, 63], func=AF.Exp, reads=[b], writes=[sc])
        ebs = self.tl("bp", j) if self.bwd else b
        OP(P, "scalar", "activation", out=self.tl("eb", j).ap[:], in_=ebs.ap[:], func=AF.Exp, reads=[ebs],
           writes=[self.tl("eb", j)])
        OP(P, "scalar", "activation", out=sc.ap[:, 3, :], in_=v8(d)[:, :, e], func=AF.Exp, reads=[d], writes=[sc])
        OP(P, "scalar", "activation", out=kk.ap[:], in_=fF.ap[:], func=AF.Copy, bias=1.0, scale=-1.0, reads=[fF], writes=[kk])

    def s3b(self, j):
        P = self.P
        qF, E1, E2, kk, qt, kt = (self.tl(k, j) for k in ("qF", "E1", "E2", "kk", "qt", "kt"))
        OP(P, "vector", "tensor_tensor", qt.ap[:], qF.ap[:], E1.ap[:], ALU.mult, reads=[qF, E1], writes=[qt])
        OP(P, "gpsimd", "tensor_tensor", kt.ap[:], kk.ap[:], E2.ap[:], ALU.mult, reads=[kk, E2], writes=[kt])
        eb, qh = self.tl("eb", j), self.tl("qh", j)
        OP(P, "gpsimd", "tensor_tensor", qh.ap[:], qF.ap[:], eb.ap[:], ALU.mult, reads=[qF, eb], writes=[qh])

    def s4a(self, j):
        P = self.P
        qt, kt = self.tl("qt", j), self.tl("kt", j)
        mms = []
        for h in range(4):
            for c in range(2):
                hc = h * 2 + c
                kw = dict(start=True, stop=True)
                if c:
                    kw["tile_position"] = (0, 64)
                mms.append(("matmul", (self.A_v[c * 64:(c + 1) * 64, h, :], kt.ap[:, hc * 64:(hc + 1) * 64],
                                       qt.ap[:, hc * 64:(hc + 1) * 64]), kw))
        MMG(P, mms, reads=[kt, qt], writes=[self.A_ps])
        mms = []
        for h in range(4):
            for c in range(2):
                hc = h * 2 + c
                kw = dict(tile_position=(0, 64)) if c else {}
                mms.append(("transpose", (self.kT_v[c * 64:(c + 1) * 64, h, :], kt.ap[:, hc * 64:(hc + 1) * 64],
                                          self.ident[:]), kw))
        MMG(P, mms, reads=[kt], writes=[self.kT_ps], extra=[self.c_all])

    def s4b(self, j):
        P = self.P
        Am, kTs = self.tl("Am", j), self.tl("kTs", j)
        OP(P, "vector", "tensor_tensor", Am.ap[:], self.A_v,
           self.gmask[:, 1 if self.bwd else 0, :].unsqueeze(1).broadcast_to([128, 4, 64]), ALU.mult,
           reads=[self.A_ps], writes=[Am], extra=[self.c_all])
        OP(P, "scalar", "activation", out=kTs.ap[:], in_=self.kT_v, func=AF.Copy, reads=[self.kT_ps], writes=[kTs])

    def stages(self):
        if self.wide:
            return [(8, 0, self.ld_f), (2, 0, self.ld_v), (7, 0, self.s1a), (5, 0, self.s2a), (3, 0, self.s3a), (1, 0, self.s4a),
                    (1, 2, self.s4b), (3, 2, self.s3b), (5, 2, self.s2b), (7, 2, self.s1b)]
        return [(5, 0, self.ld_f), (2, 0, self.ld_v), (4, 0, self.s1a), (3, 0, self.s2a), (2, 0, self.s3a), (1, 0, self.s4a),
                (1, 2, self.s4b), (2, 2, self.s3b), (3, 2, self.s2b), (4, 2, self.s1b)]

    def chunk(self, j, c, o_ps):
        P = self.P
        slot = self.items[j][0]
        U_ps = self.U_ps[slot % len(self.U_ps)]
        S, Sn, Sb = self.S[slot][self.Sph[slot]], self.S[slot][1 - self.Sph[slot]], self.Sb[slot]
        self.Sph[slot] = 1 - self.Sph[slot]
        sc, qh, Am, kTs, v2 = (self.tl(k, j) for k in ("sc", "qh", "Am", "kTs", "v2"))
        cs = slice(c * 64, (c + 1) * 64)
        v4 = lambda ap: ap.rearrange("p (h d) -> p h d", h=4)
        sv = sc.ap[:, 0, :].rearrange("p (h c) -> p h c", c=2)[:, :, c:c + 1].broadcast_to([128, 4, 128])
        ev = sc.ap[:, 3, :].rearrange("p (h c) -> p h c", c=2)[:, :, c:c + 1].broadcast_to([128, 4, 128])
        OP(P, "scalar", "activation", out=Sb.ap[:], in_=S.ap[:], func=AF.Copy, reads=[S], writes=[Sb])
        MMG(P, [("matmul", (U_ps.ap[:, h * 128:(h + 1) * 128], kTs.ap[cs, h, :], v2.ap[cs, h * 128:(h + 1) * 128]),
                 dict(start=True, stop=True)) for h in range(4)], reads=[kTs, v2], writes=[U_ps])
        mms = []
        for h in range(4):
            hc = h * 2 + c
            hs = slice(h * 128, (h + 1) * 128)
            kw1 = dict(start=True, stop=False)
            kw2 = dict(start=False, stop=True)
            if c:
                kw1["tile_position"] = (64, 64)
                kw2["tile_position"] = (0, 64)
            mms.append(("matmul", (o_ps.ap[cs, hs], Am.ap[cs, h, :], v2.ap[cs, hs]), kw1))
            mms.append(("matmul", (o_ps.ap[cs, hs], qh.ap[:, hc * 64:(hc + 1) * 64], Sb.ap[:, hs]), kw2))
        MMG(P, mms, reads=[Am, v2, qh, Sb], writes=[o_ps])
        for h in range(4):
            hc = h * 2 + c
            hs = slice(h * 128, (h + 1) * 128)
            OP(P, "scalar", "activation", out=Sn.ap[:, hs], in_=S.ap[:, hs], func=AF.Copy, scale=sc.ap[:, 0, hc:hc + 1],
               reads=[S, sc], writes=[Sn])
        for h in range(4):
            hc = h * 2 + c
            hs = slice(h * 128, (h + 1) * 128)
            OP(P, "vector", "scalar_tensor_tensor", Sn.ap[:, hs], U_ps.ap[:, hs], sc.ap[:, 3, hc:hc + 1], Sn.ap[:, hs],
               ALU.mult, ALU.add, reads=[U_ps, sc, Sn], writes=[Sn])


def gla_fwd_phase(P, A, scr, NSLOT, S):
    nc = P.nc
    NB = S // 128
    tag = "c_"
    with ExitStack() as st:
        items = [(slot, n, slot * S + n * 128) for n in range(NB) for slot in range(NSLOT)]
        G = Gla(P, st, tag, A, scr, NSLOT, False, items, o_banks=2, u_banks=2, wide=True)
        ost = [T(st.enter_context(nc.sbuf_tensor(tag + f"ost{i}", [128, 512], F32))) for i in range(3)]
        s_st = [P.sem(tag + f"st{i}", 16) for i in range(3)]
        for slot in range(NSLOT):
            G.reset_state(slot)

        def chunks(j):
            if j % NSLOT != NSLOT - 1:
                return
            grp = list(range(j - NSLOT + 1, j + 1))
            for c in (0, 1):
                for jj in grp:
                    G.chunk(jj, c, G.o_ps[jj % 2])
            for jj in grp:
                o = ost[jj % 3]
                OP(P, "scalar", "activation", out=o.ap[:], in_=G.o_ps[jj % 2].ap[:], func=AF.Copy,
                   reads=[G.o_ps[jj % 2]], writes=[o])
                tok0 = items[jj][2]
                DMA(P, "gpsimd", scr["oF"][tok0:tok0 + 128, :], o.ap[:], s_st[jj % 3], reads=[o])

        pipeline(len(items), G.stages() + [(0, 1, chunks)])
        P.barrier(extra=[(sm, sm.n) for sm in s_st])
        P.flush()


def mix_phase(P, x1, A, scr, NSLOT, S):
    nc = P.nc
    NB = S // 128
    tag = "m_"
    with ExitStack() as st:
        sb = lambda name, shape, dt: st.enter_context(nc.sbuf_tensor(tag + name, shape, dt))
        ps = lambda name, shape, dt: st.enter_context(nc.psum_tensor(tag + name, shape, dt))
        items = [(slot, n, slot * S + n * 128) for n in range(NB - 1, -1, -1) for slot in range(NSLOT)]
        NI = len(items)
        G = Gla(P, st, tag, A, scr, NSLOT, True, items, o_banks=2, u_banks=1, wide=WIDE_MIX)
        ident = G.ident
        ring = lambda name, n, shape, dt: [T(sb(f"{name}{i}", shape, dt)) for i in range(n)]
        wout = sb("wout", [128, KC, D], BF16)
        gnorm = sb("gnorm", [128, 128], F32)
        sink = sb("sink", [128, 8], F32)
        esink = sb("esink", [128, 8], F32)
        amask = sb("amask", [128, 2, 512], BF16)
        nhalf = sb("nhalf", [128, 4], F32)
        vaug = sb("vaug", [128, NSLOT * 4, 2, 65], BF16)
        kTr = sb("kTr", [128, NSLOT * 4, 256], BF16)
        vr = sb("vr", [128, NSLOT * 4, 256], BF16)
        kvt = [[T(None) for j in range(4)] for _ in range(NSLOT)]
        qTb = ring("qTb", 4, [128, 512], BF16)
        ofw = ring("ofw", 4, [128, 512], F32)
        hgw = ring("hgw", 4, [128, 512], F32)
        x1t = ring("x1t", 3, [128, D], F32)
        Pt = ring("Pt", 4, [128, 512], BF16)
        ot = ring("ot", 5, [128, 512], F32)
        sqo = ring("sqo", 2, [128, 512], F32)
        ostat = ring("ostat", 4, [128, 12], F32)
        on = ring("on", 3, [128, 512], F32)
        mixg = ring("mixg", 3, [128, 512], BF16)
        mixa = ring("mixa", 8, [128, 512], BF16)
        den = ring("den", 2, [128, 16], F32)
        mixT = ring("mixT", 2, [128, KC, 128], BF16)
        S_ps = [T(ps(f"Sa{i}", [128, 512], F32)) for i in range(2)]
        O_bank = ps("O", [128, 512], F32)
        O_ps = T(O_bank[:, 0:260].rearrange("p (g d) -> p g d", g=4))
        TY = O_ps
        tpm = O_bank[:].bitcast(BF16).rearrange("p (kc t) -> p kc t", kc=KC)

        s_w = P.sem(tag + "w", 16)
        s_q = [P.sem(tag + f"q{i}", 16) for i in range(4)]
        s_o = [P.sem(tag + f"o{i}", 16) for i in range(4)]
        s_h = [P.sem(tag + f"h{i}", 16) for i in range(4)]
        s_x = [P.sem(tag + f"x{i}", 16) for i in range(3)]
        s_kv = [[P.sem(tag + f"kv{sl}_{j}", 16) for j in range(4)] for sl in range(NSLOT)]
        s_st = [P.sem(tag + f"st{i}", 16) for i in range(4)]

        wv = A["w_out"].rearrange("(kc p) n -> p kc n", p=128)
        for kc in range(KC):
            P.dma("gpsimd", wout[:, kc, :], wv[:, kc, :], sem=s_w)
        P.dma("gpsimd", gnorm[:], A["hg_out_norm"].partition_broadcast(128), sem=s_w)
        P.dma("gpsimd", amask[:], A["amask"], sem=s_w)
        w_all = P.dma("gpsimd", sink[:], A["attn_sink"].partition_broadcast(128), sem=s_w)
        t_nh = P.op("gpsimd", "memset", nhalf[:], -0.5)
        t_va = P.op("gpsimd", "memset", vaug[:], 1.0)
        t_es = P.op("scalar", "activation", out=esink[:], in_=sink[:], func=AF.Exp, waits=[w_all])
        cready = [w_all, t_nh, t_va, t_es, G.c_all]

        def load_kv(slot, n):
            jj = n % 4
            t = kvt[slot][jj]
            tok0 = slot * S + n * 128
            ex = _deps(P, "sync", [], [t], [t_va])
            P.dma("sync", kTr[:, slot * 4 + jj, 0:128], scr["kTs"][:, tok0:tok0 + 128], waits=ex, sem=s_kv[slot][jj])
            tk = P.dma("sync", vr[:, slot * 4 + jj, 0:128], scr["vA"][tok0:tok0 + 128, :], sem=s_kv[slot][jj])
            tk2 = P.op("gpsimd", "tensor_copy", vaug[:, slot * 4 + jj, :, 0:64],
                       vr[:, slot * 4 + jj, 0:128].rearrange("p (k d) -> p k d", k=2), waits=[tk])
            _reg(tk2, [], [t])

        def kvl(j):
            slot, n, _ = items[j]
            if n - 1 >= 0:
                load_kv(slot, n - 1)

        def lq(j):
            slot, n, _ = items[j]
            DMA(P, "sync", qTb[j % 4].ap[:], scr["qTs"][:, slot * NB + n, :], s_q[j % 4], writes=[qTb[j % 4]])

        def lo(j):
            tok0 = items[j][2]
            DMA(P, "sync", ofw[j % 4].ap[:], scr["oF"][tok0:tok0 + 128, :], s_o[j % 4], writes=[ofw[j % 4]])

        def lh(j):
            tok0 = items[j][2]
            DMA(P, "sync", hgw[j % 4].ap[:], scr["hgS"][tok0:tok0 + 128, :], s_h[j % 4], writes=[hgw[j % 4]])

        def lx(j):
            tok0 = items[j][2]
            DMA(P, "sync", x1t[j % 3].ap[:], x1[tok0:tok0 + 128, :], s_x[j % 3], writes=[x1t[j % 3]])

        cnt = {"s": 0, "p": 0}

        def att(j):
            slot, n, _ = items[j]
            q = qTb[j % 4]
            ma = mixa[j % 8]
            dn = den[j % 2]
            kbs = [kb for kb in (n - 1, n, n + 1) if 0 <= kb < NB]

            def scores(kv, kb):
                ksl = slice(kv * 64, (kv + 1) * 64)
                t = kvt[slot][kb % 4]
                col = slot * 4 + kb % 4
                sp = S_ps[cnt["s"] % 2]
                cnt["s"] += 1
                mms = [("matmul", (sp.ap[:], kTr[ksl, col, 0:128], q.ap[ksl, :]), dict(start=True, stop=(kb == n)))]
                if kb != n:
                    mms.append(("matmul", (sp.ap[:], ident[:], amask[:, 0 if kb < n else 1, :]),
                                dict(start=False, stop=True)))
                MMG(P, mms, reads=[t, q], writes=[sp], extra=cready)
                pt = Pt[cnt["p"] % 4]
                cnt["p"] += 1
                OP(P, "scalar", "activation", out=pt.ap[:], in_=sp.ap[:], func=AF.Exp, scale=0.125,
                   reads=[sp], writes=[pt])
                return (pt, t, col)

            def pv(kv, pts):
                mms = []
                for g in range(4):
                    for ki, (pt, t, col) in enumerate(pts):
                        mms.append(("matmul", (O_ps.ap[:, g, :], pt.ap[:, g * 128:(g + 1) * 128], vaug[:, col, kv, :]),
                                    dict(start=(ki == 0), stop=(ki == len(pts) - 1))))
                MMG(P, mms, reads=[p[0] for p in pts] + [p[1] for p in pts], writes=[O_ps])
                hs = slice(kv * 4, (kv + 1) * 4)
                rs = slice(8 + kv * 4, 12 + kv * 4)
                OP(P, "vector", "tensor_tensor", dn.ap[:, hs], O_ps.ap[:, :, 64], esink[:, hs], ALU.add,
                   reads=[O_ps], writes=[dn], extra=cready)
                OP(P, "vector", "reciprocal", dn.ap[:, rs], dn.ap[:, hs], reads=[dn], writes=[dn])
                OP(P, "vector", "tensor_tensor", ma.ap[:, kv * 256:(kv + 1) * 256].rearrange("p (g d) -> p g d", d=64),
                   O_ps.ap[:, :, 0:64], dn.ap[:, rs].unsqueeze(2).broadcast_to([128, 4, 64]),
                   ALU.mult, reads=[O_ps, dn], writes=[ma])

            p0 = [scores(0, kb) for kb in kbs]
            p1 = [scores(1, kbs[0])]
            pv(0, p0)
            p1 += [scores(1, kb) for kb in kbs[1:]]
            pv(1, p1)

        def chunks(j):
            if j % NSLOT != NSLOT - 1:
                return
            grp = list(range(j - NSLOT + 1, j + 1))
            for c in (1, 0):
                for jj in grp:
                    G.chunk(jj, c, G.o_ps[jj % 2])
            for jj in grp:
                OP(P, "vector", "tensor_tensor", ot[jj % 5].ap[:], G.o_ps[jj % 2].ap[:], ofw[jj % 4].ap[:], ALU.add,
                   reads=[G.o_ps[jj % 2], ofw[jj % 4]], writes=[ot[jj % 5]])

        v4 = lambda ap: ap.rearrange("p (h d) -> p h d", h=4)

        def c1a(j):
            OP(P, "gpsimd", "tensor_tensor", sqo[j % 2].ap[:], ot[j % 5].ap[:], ot[j % 5].ap[:], ALU.mult,
               reads=[ot[j % 5]], writes=[sqo[j % 2]])

        def c1b(j):
            os_ = ostat[j % 4]
            OP(P, "vector", "tensor_reduce", os_.ap[:, 0:4], v4(sqo[j % 2].ap[:]), AX.X, ALU.add, reads=[sqo[j % 2]], writes=[os_])
            OP(P, "vector", "tensor_scalar", os_.ap[:, 4:8], os_.ap[:, 0:4], 1.0 / 128, EPS, ALU.mult, ALU.add,
               reads=[os_], writes=[os_])

        def c2a(j):
            os_ = ostat[j % 4]
            OP(P, "gpsimd", "tensor_tensor", os_.ap[:, 8:12], os_.ap[:, 4:8], nhalf[:, 0:4], ALU.pow,
               reads=[os_], writes=[os_], extra=cready)

        def c2b(j):
            os_ = ostat[j % 4]
            OP(P, "vector", "tensor_tensor", v4(on[j % 3].ap[:]), v4(ot[j % 5].ap[:]),
               os_.ap[:, 8:12].unsqueeze(2).broadcast_to([128, 4, 128]), ALU.mult, reads=[ot[j % 5], os_], writes=[on[j % 3]])

        def c3a(j):
            OP(P, "gpsimd", "tensor_tensor", v4(on[j % 3].ap[:]), v4(on[j % 3].ap[:]),
               gnorm[:].unsqueeze(1).broadcast_to([128, 4, 128]), ALU.mult, reads=[on[j % 3]], writes=[on[j % 3]], extra=cready)

        def c3b(j):
            OP(P, "vector", "tensor_tensor", mixg[j % 3].ap[:], on[j % 3].ap[:], hgw[j % 4].ap[:], ALU.mult,
               reads=[on[j % 3], hgw[j % 4]], writes=[mixg[j % 3]])

        def o1a(j):
            mms = []
            for kc in range(4):
                mms.append(("transpose", (tpm[:, kc, :], mixg[j % 3].ap[:, kc * 128:(kc + 1) * 128], ident[:]), {}))
            for kc in range(4, 8):
                mms.append(("transpose", (tpm[:, kc, :], mixa[j % 8].ap[:, (kc - 4) * 128:(kc - 3) * 128], ident[:]), {}))
            MMG(P, mms, reads=[mixg[j % 3], mixa[j % 8]], writes=[TY], extra=cready)

        def o1b(j):
            OP(P, "scalar", "activation", out=mixT[j % 2].ap[:], in_=tpm, func=AF.Copy, reads=[TY], writes=[mixT[j % 2]])

        def o2a(j):
            for half, bank in ((0, S_ps[0]), (1, S_ps[1])):
                hs = slice(half * 512, (half + 1) * 512)
                MMG(P, [("matmul", (bank.ap[:], mixT[j % 2].ap[:, kc, :], wout[:, kc, hs]),
                         dict(start=(kc == 0), stop=(kc == KC - 1))) for kc in range(KC)],
                    reads=[mixT[j % 2]], writes=[bank], extra=cready)

        def o2b(j):
            tok0 = items[j][2]
            xt = x1t[j % 3]
            for half, bank in ((0, S_ps[0]), (1, S_ps[1])):
                hs = slice(half * 512, (half + 1) * 512)
                OP(P, "vector", "tensor_tensor", xt.ap[:, hs], bank.ap[:], xt.ap[:, hs], ALU.add,
                   reads=[bank, xt], writes=[xt])
            DMA(P, "gpsimd", scr["x2"][tok0:tok0 + 128, :], xt.ap[:], s_st[j % 4], reads=[xt])
            if DEBUG:
                DMA(P, "gpsimd", scr["mixdbg"][tok0:tok0 + 128, 0:512], mixg[j % 3].ap[:], s_st[j % 4], reads=[mixg[j % 3]])
                DMA(P, "gpsimd", scr["mixdbg"][tok0:tok0 + 128, 512:1024], mixa[j % 8].ap[:], s_st[j % 4], reads=[mixa[j % 8]])

        for slot in range(NSLOT):
            G.reset_state(slot)
            load_kv(slot, NB - 1)
        sh = NSLOT - 1
        stages = [(-3 - sh, 0, c3a), (-3 - sh, 0, lx), (-2 - sh, 0, c2a), (-1 - sh, 0, c1a), (-1 - sh, 0, lh)] + \
            [s for s in G.stages() if s[1] == 0] + [(3, 0, kvl), (3, 0, lq), (2, 0, lo)] + \
            [(1, 1, att), (0, 1, chunks)] + \
            [(-3 - sh, 2, c3b), (-2 - sh, 2, c2b), (-1 - sh, 2, c1b)] + [s for s in G.stages() if s[1] == 2] + \
            [(-5 - sh, 2, o2a), (-5 - sh, 2, o2b), (-4 - sh, 2, o1a), (-4 - sh, 2, o1b)]
        pipeline(NI, stages)
        P.barrier(extra=[(sm, sm.n) for sm in s_st])
        P.flush()


WEIGHTS = [("ffn1_norm", [D]), ("ffn1_wi", [D, 2 * FF]), ("ffn1_wo", [FF, D]), ("mix_norm", [D]),
           ("w_in", [D, NIN]), ("lbf", [128, 4, 2]), ("lbb", [128, 4, 2]), ("hg_out_norm", [128]),
           ("q_norm", [64]), ("k_norm", [64]), ("attn_sink", [8]), ("w_out", [D, D]),
           ("ffn2_norm", [D]), ("ffn2_wi", [D, 2 * FF]), ("ffn2_wo", [FF, D])]


def build_program(NSLOT, S, phases=("ffn1", "proj", "glaf", "mix", "ffn2"), dbg_out=()):
    NTOK = NSLOT * S
    NBLK = NTOK // 128
    nc = bass.Bass("TRN2", target_bir_lowering=False)
    dt = lambda name, shape, dtype=F32, kind="ExternalInput": nc.dram_tensor(name, shape, dtype, kind=kind).ap()
    A = {"x": dt("x", [NTOK, D])}
    for name, shape in WEIGHTS:
        A[name] = dt(name, shape)
    A["ident"] = dt("ident", [128, 128], BF16)
    A["cosT"] = dt("cosT", [128, S // 128, 32])
    A["sinT"] = dt("sinT", [128, S // 128, 32])
    A["scanmask"] = dt("scanmask", [128, 512])
    A["gmask"] = dt("gmask", [128, 2, 64])
    A["amask"] = dt("amask", [128, 2, 512], BF16)
    y = dt("y", [NTOK, D], kind="ExternalOutput")
    scr = {}
    for name, shape, dtype in [("x1", [NTOK, D], F32), ("qF", [4, 128, NTOK], F32), ("ffF", [4, 128, NTOK], F32),
                               ("fbF", [4, 128, NTOK], F32), ("vT", [NTOK, 512], BF16), ("hgS", [NTOK, 512], F32),
                               ("qTs", [128, NBLK, 512], BF16), ("kTs", [128, NTOK], BF16), ("vA", [NTOK, 128], BF16),
                               ("oF", [NTOK, 512], F32), ("x2", [NTOK, D], F32), ("mixdbg", [NTOK, D], BF16)]:
        scr[name] = dt(name, shape, dtype, kind=("ExternalOutput" if name in dbg_out else "Internal"))

    P = Prog(nc)
    with P.gstack:
        src = A["x"]
        if "ffn1" in phases:
            ffn_phase(P, A["x"], scr["x1"], A["ffn1_norm"], A["ffn1_wi"], A["ffn1_wo"], NTOK, A["ident"], "a_")
            src = scr["x1"]
        if "proj" in phases:
            proj_phase(P, src, A, scr, NSLOT, S)
        if "glaf" in phases:
            gla_fwd_phase(P, A, scr, NSLOT, S)
        if "mix" in phases:
            mix_phase(P, src, A, scr, NSLOT, S)
            src = scr["x2"]
        if "ffn2" in phases:
            ffn_phase(P, src, y, A["ffn2_norm"], A["ffn2_wi"], A["ffn2_wo"], NTOK, A["ident"], "d_")
    return nc


def host_consts(S):
    pos = np.arange(S, dtype=np.float32)
    inv_freq = (10000.0 ** (-np.arange(0, 64, 2, dtype=np.float32) / 64)).astype(np.float32)
    ang = pos[:, None] * inv_freq[None, :]
    cos = np.cos(ang).astype(np.float32).reshape(S // 128, 128, 32).transpose(1, 0, 2)
    sin = np.sin(ang).astype(np.float32).reshape(S // 128, 128, 32).transpose(1, 0, 2)
    scanmask = np.ones((128, 512), np.float32)
    scanmask[:, ::64] = 0.0
    i = np.arange(64)
    gmask = np.stack([(i[:, None] <= i[None, :]), (i[:, None] >= i[None, :])], axis=1).astype(np.float32)
    gmask = np.concatenate([gmask, gmask], axis=0)
    j = np.arange(128)
    prev = np.where(j[:, None] >= j[None, :], 0.0, -30000.0)
    nxt = np.where(j[:, None] <= j[None, :], 0.0, -30000.0)
    amask = np.stack([np.tile(prev, (1, 4)), np.tile(nxt, (1, 4))], axis=1).astype(ml_dtypes.bfloat16)
    return dict(ident=np.eye(128).astype(ml_dtypes.bfloat16), cosT=np.ascontiguousarray(cos),
                sinT=np.ascontiguousarray(sin), scanmask=scanmask, gmask=np.ascontiguousarray(gmask), amask=amask)


_NC_CACHE = {}


def kernel(**inputs):
    S = 8192
    NSLOT = 2
    xp = np.asarray(inputs["x_prompt"], dtype=np.float32)
    xs = np.asarray(inputs["x_sample"], dtype=np.float32)
    seqs = [xp[0], xp[1]] + [xs[i] for i in range(8)]
    assign = [(c, 8 + c if c < 2 else None) for c in range(N_CORES)]
    w = {}
    for name, _shape in WEIGHTS:
        if name in ("lbf", "lbb"):
            src = inputs["hg_lb_fwd" if name == "lbf" else "hg_lb_bwd"]
            w[name] = np.ascontiguousarray(np.asarray(src, np.float32).reshape(2, 4, 128).transpose(2, 1, 0))
        else:
            w[name] = np.ascontiguousarray(np.asarray(inputs[name], np.float32)[0])
    consts = host_consts(S)
    key = (NSLOT, S)
    if key not in _NC_CACHE:
        _NC_CACHE[key] = build_program(NSLOT, S)
    nc = _NC_CACHE[key]
    zeros = np.zeros((S, D), np.float32)
    in_maps = []
    for c in range(N_CORES):
        a, b = assign[c]
        x = np.concatenate([seqs[a], seqs[b] if b is not None else zeros], axis=0)
        m = {"x": np.ascontiguousarray(x)}
        m.update(w)
        m.update(consts)
        in_maps.append(m)
    res = run_bass_kernel_spmd(nc, in_maps, core_ids=list(range(N_CORES)))
    outs = [None] * 10
    for c in range(N_CORES):
        y = np.asarray(res.results[c]["y"])
        a, b = assign[c]
        outs[a] = y[:S]
        if b is not None:
            outs[b] = y[S:]
    y_prompt = np.stack(outs[:2]).astype(np.float32)
    y_sample = np.stack(outs[2:]).astype(np.float32)
    return (y_prompt, y_sample)
```
